# Optimizing a Trainium2 kernel written in Bass

```python
import math
import jax, jax.numpy as jnp
from jax import lax
import numpy as np

D_MODEL = 1024
BATCH = 4
SEQ = 4096
DEPTH = 1

HEAD_DIM = 64
HEADS_PER_GROUP = 8
DILATION_GROUPS = ((128, 1), (512, 4), (2048, 16))
N_GROUPS = len(DILATION_GROUPS)
ATTN_QKV_WIDTH = N_GROUPS * HEADS_PER_GROUP * HEAD_DIM
ATTN_OUT_WIDTH = HEADS_PER_GROUP * HEAD_DIM
BAND_BLOCK = 128
ROPE_THETA = 10000.0
LRU_WIDTH = 1024
LRU_BLOCKS = 16
LRU_BLOCK_WIDTH = LRU_WIDTH // LRU_BLOCKS
CONV_WIDTH = 4
LRU_C = 8.0
N_BRANCHES = 2
SPLIT_SIZES = (ATTN_QKV_WIDTH, ATTN_QKV_WIDTH, ATTN_QKV_WIDTH, ATTN_OUT_WIDTH,
               LRU_WIDTH, LRU_WIDTH, N_BRANCHES * D_MODEL)
IN_WIDTH = sum(SPLIT_SIZES)
SPLIT_POINTS = [int(v) for v in np.cumsum(SPLIT_SIZES)[:-1]]
DEEPNORM_ALPHA = (2.0 * DEPTH) ** 0.25
DEEPNORM_BETA = (8.0 * DEPTH) ** -0.25
LN_EPS = 1e-5
NEG_INF = -1e30

kernel_name = "hybrid_dilated_attn_rglru_deepnorm"


def layer_norm(x, gain, bias):
    xf = x.astype(jnp.float32)
    mu = jnp.mean(xf, axis=-1, keepdims=True)
    var = jnp.mean(jnp.square(xf - mu), axis=-1, keepdims=True)
    y = (xf - mu) * lax.rsqrt(var + LN_EPS) * gain.astype(jnp.float32) + bias.astype(jnp.float32)
    return y.astype(x.dtype)


def rope(x):
    S = x.shape[1]
    half = HEAD_DIM // 2
    inv_freq = ROPE_THETA ** (-jnp.arange(half, dtype=jnp.float32) / half)
    ang = jnp.arange(S, dtype=jnp.float32)[:, None] * inv_freq[None, :]
    cos = jnp.cos(ang)[None, :, None, :]
    sin = jnp.sin(ang)[None, :, None, :]
    xf = x.astype(jnp.float32)
    x1, x2 = xf[..., :half], xf[..., half:]
    return jnp.concatenate([x1 * cos - x2 * sin, x2 * cos + x1 * sin], axis=-1)


def dilated_band_attention(q, k, v, window, dilation):
    B, S, H, Dh = q.shape
    span = window // dilation
    chunk = dilation * BAND_BLOCK
    Sp = -(-S // chunk) * chunk
    M = Sp // dilation
    nb = M // BAND_BLOCK

    def to_blocks(t):
        t = jnp.pad(t, ((0, 0), (0, Sp - S), (0, 0), (0, 0)))
        t = t.reshape(B, M, dilation, H, Dh).transpose(0, 2, 1, 3, 4)
        return t.reshape(B, dilation, nb, BAND_BLOCK, H, Dh)

    qb, kb, vb = to_blocks(q), to_blocks(k), to_blocks(v)

    def with_prev(t):
        prev = jnp.concatenate([jnp.zeros_like(t[:, :, :1]), t[:, :, :-1]], axis=2)
        return jnp.concatenate([prev, t], axis=3)

    kw, vw = with_prev(kb), with_prev(vb)
    s = jnp.einsum('brnqhd,brnkhd->brnhqk', qb, kw) * (Dh ** -0.5)
    qi = jnp.arange(BAND_BLOCK)[:, None]
    ki = jnp.arange(2 * BAND_BLOCK)[None, :]
    dist = qi + BAND_BLOCK - ki
    band = (dist >= 0) & (dist <= span)
    not_first = jnp.arange(nb)[:, None, None] > 0
    valid = band[None] & (not_first | (ki >= BAND_BLOCK)[None])
    s = jnp.where(valid[None, None, :, None], s, NEG_INF)
    m = jnp.max(s, axis=-1, keepdims=True)
    p = jnp.exp(s - m)
    l = jnp.sum(p, axis=-1, keepdims=True)
    o = jnp.einsum('brnhqk,brnkhd->brnqhd', p, vw)
    l_q = jnp.transpose(l[..., 0], (0, 1, 2, 4, 3))
    lse = jnp.transpose((m + jnp.log(l))[..., 0], (0, 1, 2, 4, 3))
    o = o / l_q[..., None]

    def from_blocks(t):
        tail = t.shape[5:]
        t = t.reshape((B, dilation, M, H) + tail)
        t = jnp.moveaxis(t, 1, 2).reshape((B, Sp, H) + tail)
        return t[:, :S]

    return from_blocks(o), from_blocks(lse)


def causal_depthwise_conv(x, w, b):
    S = x.shape[1]
    xp = jnp.pad(x, ((0, 0), (CONV_WIDTH - 1, 0), (0, 0)))
    y = b
    for j in range(CONV_WIDTH):
        y = y + xp[:, j:j + S] * w[j]
    return y


def rg_lru(x, w_r, b_r, w_i, b_i, lam):
    B, S, C = x.shape
    xb = x.reshape(B, S, LRU_BLOCKS, LRU_BLOCK_WIDTH)
    r = jax.nn.sigmoid((jnp.einsum('bsgi,gij->bsgj', xb, w_r).reshape(B, S, C) + b_r).astype(jnp.float32))
    i = jax.nn.sigmoid((jnp.einsum('bsgi,gij->bsgj', xb, w_i).reshape(B, S, C) + b_i).astype(jnp.float32))
    log_a = -LRU_C * r * jax.nn.softplus(-lam.astype(jnp.float32))
    a = jnp.exp(log_a)
    mult = jnp.sqrt(jnp.maximum(-jnp.expm1(2.0 * log_a), 0.0))
    u = mult * (i * x.astype(jnp.float32))

    def combine(e1, e2):
        a1, b1 = e1
        a2, b2 = e2
        return a1 * a2, a2 * b1 + b2

    _, h = lax.associative_scan(combine, (a, u), axis=1)
    return h


def setup_inputs(seed: int = 0) -> dict:
    key = jax.random.key(seed)
    ks = jax.random.split(key, 16)
    L, D = DEPTH, D_MODEL
    x = jax.random.normal(ks[0], (BATCH, SEQ, D), jnp.float32)
    w_in = jax.random.normal(ks[1], (L, D, IN_WIDTH), jnp.float32) * D ** -0.5
    v_lo, v_hi = 2 * ATTN_QKV_WIDTH, 3 * ATTN_QKV_WIDTH
    col_scale = jnp.ones((IN_WIDTH,), jnp.float32).at[v_lo:v_hi].set(DEEPNORM_BETA)
    w_in = w_in * col_scale
    b_in = 0.02 * jax.random.normal(ks[2], (L, IN_WIDTH), jnp.float32)
    conv_w = jax.random.normal(ks[3], (L, CONV_WIDTH, LRU_WIDTH), jnp.float32) * CONV_WIDTH ** -0.5
    conv_b = 0.02 * jax.random.normal(ks[4], (L, LRU_WIDTH), jnp.float32)
    lru_wr = jax.random.normal(ks[5], (L, LRU_BLOCKS, LRU_BLOCK_WIDTH, LRU_BLOCK_WIDTH), jnp.float32) * LRU_BLOCK_WIDTH ** -0.5
    lru_br = 0.02 * jax.random.normal(ks[6], (L, LRU_WIDTH), jnp.float32)
    lru_wi = jax.random.normal(ks[7], (L, LRU_BLOCKS, LRU_BLOCK_WIDTH, LRU_BLOCK_WIDTH), jnp.float32) * LRU_BLOCK_WIDTH ** -0.5
    lru_bi = 0.02 * jax.random.normal(ks[8], (L, LRU_WIDTH), jnp.float32)
    a_c = jax.random.uniform(ks[9], (L, LRU_WIDTH), jnp.float32, 0.9, 0.999)
    a0 = a_c ** (1.0 / LRU_C)
    lru_lambda = jnp.log(a0) - jnp.log1p(-a0)
    w_attn_proj = jax.random.normal(ks[10], (L, ATTN_OUT_WIDTH, D), jnp.float32) * ATTN_OUT_WIDTH ** -0.5 * DEEPNORM_BETA
    w_lru_proj = jax.random.normal(ks[11], (L, LRU_WIDTH, D), jnp.float32) * LRU_WIDTH ** -0.5 * DEEPNORM_BETA
    w_out = jax.random.normal(ks[12], (L, D, D), jnp.float32) * D ** -0.5 * DEEPNORM_BETA
    b_out = 0.02 * jax.random.normal(ks[13], (L, D), jnp.float32)
    ln_gain = 1.0 + 0.02 * jax.random.normal(ks[14], (L, D), jnp.float32)
    ln_bias = 0.02 * jax.random.normal(ks[15], (L, D), jnp.float32)
    return {"x": x, "w_in": w_in, "b_in": b_in, "conv_w": conv_w, "conv_b": conv_b,
            "lru_wr": lru_wr, "lru_br": lru_br, "lru_wi": lru_wi, "lru_bi": lru_bi,
            "lru_lambda": lru_lambda, "w_attn_proj": w_attn_proj, "w_lru_proj": w_lru_proj,
            "w_out": w_out, "b_out": b_out, "ln_gain": ln_gain, "ln_bias": ln_bias}


def reference(x, w_in, b_in, conv_w, conv_b, lru_wr, lru_br, lru_wi, lru_bi, lru_lambda,
              w_attn_proj, w_lru_proj, w_out, b_out, ln_gain, ln_bias):
    B, S, D = x.shape
    dt = x.dtype
    for l in range(DEPTH):
        proj = x @ w_in[l] + b_in[l]
        q, k, v, g_attn, u_lru, g_lru, gate_logits = jnp.split(proj, SPLIT_POINTS, axis=-1)

        q = rope(q.reshape(B, S, N_GROUPS * HEADS_PER_GROUP, HEAD_DIM)).reshape(B, S, N_GROUPS, HEADS_PER_GROUP, HEAD_DIM)
        k = rope(k.reshape(B, S, N_GROUPS * HEADS_PER_GROUP, HEAD_DIM)).reshape(B, S, N_GROUPS, HEADS_PER_GROUP, HEAD_DIM)
        v = v.astype(jnp.float32).reshape(B, S, N_GROUPS, HEADS_PER_GROUP, HEAD_DIM)
        outs, lses = [], []
        for g, (window, dilation) in enumerate(DILATION_GROUPS):
            o_g, lse_g = dilated_band_attention(q[:, :, g], k[:, :, g], v[:, :, g], window, dilation)
            outs.append(o_g)
            lses.append(lse_g)
        wts = jax.nn.softmax(jnp.stack(lses, axis=0), axis=0)
        attn = jnp.sum(wts[..., None] * jnp.stack(outs, axis=0), axis=0)
        attn = attn.reshape(B, S, ATTN_OUT_WIDTH).astype(dt) * jax.nn.silu(g_attn)
        y_a = attn @ w_attn_proj[l]

        u = causal_depthwise_conv(u_lru, conv_w[l], conv_b[l])
        h = rg_lru(u, lru_wr[l], lru_br[l], lru_wi[l], lru_bi[l], lru_lambda[l]).astype(dt)
        y_b = (h * jax.nn.silu(g_lru)) @ w_lru_proj[l]

        gates = jax.nn.sigmoid(gate_logits).reshape(B, S, N_BRANCHES, D)
        merged = gates[:, :, 0] * y_a + gates[:, :, 1] * y_b
        out = merged @ w_out[l] + b_out[l]
        x = layer_norm(DEEPNORM_ALPHA * x + out, ln_gain[l], ln_bias[l])
    return x
```

```python
import contextlib
import numpy as np
import concourse.bass as bass
import concourse.mybir as mybir
from concourse.bass_utils import run_bass_kernel_spmd

F32 = mybir.dt.float32
BF16 = mybir.dt.bfloat16
AF = mybir.ActivationFunctionType
ALU = mybir.AluOpType

NT = 2048
ALPHA = 2.0 ** 0.25
LN_EPS = 1e-5
DILS = (1, 4, 16)

C_BIN, C_CW, C_CB, C_BR, C_BI, C_LAM, C_FLAG, C_ONE, C_MHALF, C_EPS, NCST = 0, 72, 104, 112, 120, 128, 136, 137, 138, 140, 144
D_HBIN, D_HBR, D_HBI, D_HSC, D_SC1, D_FBU, D_TMP, NCST2 = 0, 72, 80, 88, 96, 104, 112, 128
B_ID, B_PM, B_ONES, B_M, B_MH = 0, 128, 256, 320, 576


class Tracker:
    def __init__(self, nc, stack):
        self.nc = nc
        self.stack = stack
        self.engs = {'pe': nc.tensor, 'act': nc.scalar, 'dve': nc.vector, 'pool': nc.gpsimd, 'sp': nc.sync}
        self.streams = {e: [] for e in self.engs}
        self.sems = {}
        self.semval = {}
        self.waited = {e: {} for e in self.engs}
        self.last_w = {}
        self.last_r = {}

    def sem(self, name):
        if name not in self.sems:
            self.sems[name] = self.stack.enter_context(self.nc.semaphore(name))
            self.semval[name] = 0
        return name

    def _need(self, eng, deps):
        for (s, v) in deps:
            if self.waited[eng].get(s, 0) < v:
                self.waited[eng][s] = v
                self.streams[eng].append(('wait', s, v))

    def op(self, eng, fn, reads=(), writes=(), sem=None, inc=1):
        deps = []
        for r in reads:
            if r in self.last_w:
                deps.append(self.last_w[r])
        for w in writes:
            if w in self.last_w:
                deps.append(self.last_w[w])
            for s, v in self.last_r.get(w, {}).items():
                deps.append((s, v))
        own = 'S_' + eng
        if eng == 'pe':
            deps = [d for d in deps if d[0] != own]
        self._need(eng, deps)
        s = self.sem(sem if sem is not None else own)
        self.semval[s] += inc
        v = self.semval[s]
        self.streams[eng].append(('op', fn, s, inc))
        for w in writes:
            self.last_w[w] = (s, v)
            self.last_r[w] = {}
        for r in reads:
            d = self.last_r.setdefault(r, {})
            d[s] = max(d.get(s, 0), v)

    def dma(self, eng, fn, reads, writes, slot):
        self.op(eng, fn, reads, writes, sem='D_' + slot, inc=16)

    def wait_all(self, eng):
        self._need(eng, [(s, v) for s, v in self.semval.items() if v > 0])

    def barrier(self):
        for e in self.engs:
            self.wait_all(e)

    def emit(self):
        nc = self.nc
        with nc.Block() as block:
            def run(engname, eng):
                for item in self.streams[engname]:
                    if item[0] == 'wait':
                        eng.wait_ge(self.sems[item[1]], item[2])
                    else:
                        _, fn, s, inc = item
                        ins = fn(eng)
                        ins.then_inc(self.sems[s], inc)

            @block.tensor
            def _(e):
                run('pe', e)

            @block.scalar
            def _(e):
                run('act', e)

            @block.vector
            def _(e):
                run('dve', e)

            @block.gpsimd
            def _(e):
                run('pool', e)

            @block.sync
            def _(e):
                run('sp', e)


def gview(ap, d, j):
    if d == 1:
        return ap[:, 512 * j:512 * j + 512]
    if d == 4:
        return ap.rearrange("p (m r) -> p r m", r=4)[:, j, :]
    return ap.rearrange("p (m r) -> p r m", r=16)[:, 4 * j:4 * j + 4, :]


def hview(ap, d):
    if d == 1:
        return ap[:, 1920:2048]
    return ap.rearrange("p (m r) -> p r m", r=4)[:, :, 384:512]


def dstview(buf, d, j):
    if d == 1:
        return buf[:, 512 * j:512 * j + 512]
    if d == 4:
        return buf.rearrange("p (r m) -> p r m", r=4)[:, :, 128 * j:128 * j + 128]
    return buf.rearrange("p (r m) -> p r m", r=16)[:, :, 32 * j:32 * j + 32]


def srcview(flat, d):
    if d == 1:
        return flat
    return flat.rearrange("p (m r) -> p r m", r=d)


def like(flat, ref):
    sh = ref.shape
    if len(sh) == 2:
        return flat
    return flat.rearrange("p (a b) -> p a b", a=sh[1])


def build():
    nc = bass.Bass("TRN2", target_bir_lowering=False)
    dt = lambda name, shape, kind="ExternalInput": nc.dram_tensor(name, shape, F32, kind=kind).ap()
    sbn = lambda n: "s_" + n
    xT_d = dt("xT", [8, 128, 4096])
    xres_d = dt("xres", [16, 128, 1024])
    win_d = dt("win", [72, 128, 1024])
    cst_d = dt("cst", [128, NCST])
    cb_d = dt("cb", [128, 1024])
    tab_d = dt("tab", [4, 128, 2048])
    bd_d = dt("bd", [2, 128, 1024])
    wap_d = dt("wap", [4, 128, 1024])
    wlp_d = dt("wlp", [8, 128, 1024])
    wo_d = dt("wo", [8, 128, 1024])
    bc_d = dt("bc", [3, 128, 1024])
    y_d = dt("y", [16, 128, 1024], kind="ExternalOutput")

    with contextlib.ExitStack() as st:
        T = Tracker(nc, st)
        sb = lambda name, shape, dtp: st.enter_context(nc.sbuf_tensor("s_" + name, shape, dtp))
        ps = lambda name, shape, dtp: st.enter_context(nc.psum_tensor(name, shape, dtp))

        XT = sb("XT", [128, 8 * 4096], BF16)
        XTv = XT[:].rearrange("p (k t) -> p k t", k=8)
        AG = sb("AG", [128, 4 * NT], BF16)
        AGv = AG[:].rearrange("p (k t) -> p k t", k=4)
        cst = sb("cst", [128, NCST], F32)
        cst2 = sb("cst2", [128, NCST2], F32)
        CB = sb("CB", [128, 704], BF16)
        wst = [sb("wst%d" % i, [128, 1024], F32) for i in range(2)]
        wbf = [sb("wbf%d" % i, [128, 1024], BF16) for i in range(3)]
        ARENA_F = 28288
        arena = sb("arena", [128, ARENA_F], F32)

        pA = [ps("pA%d" % i, [128, 512], F32) for i in range(2)]
        pB = ps("pB", [128, 512], F32)
        pS = [ps("pS%d" % i, [128, 512], F32) for i in range(2)]
        pX = ps("pX", [128, 512], F32)
        pY = ps("pY", [128, 512], F32)
        pT = ps("pT", [128, 1024], BF16)
        pT32 = pT[:].bitcast(F32)
        pBb = pB[:].bitcast(BF16)
        vtcount = [0]
        SBANKS = [(pS[0], 'bS0'), (pS[1], 'bS1'), (pA[0], 'pA0'), (pA[1], 'pA1')]
        XYSETS = [((pX, 'pX'), (pY, 'pY')), ((pB, 'pB'), (pT32, 'pT'))]

        class Carver:
            def __init__(self):
                self.off = 0

            def f32(self, n):
                a = arena[:, self.off:self.off + n]
                self.off += n
                assert self.off <= ARENA_F, self.off
                return a

            def bf16(self, n):
                assert n % 2 == 0
                return self.f32(n // 2).bitcast(BF16)

        col = lambda c: cst[:, c:c + 1]
        col2 = lambda c: cst2[:, c:c + 1]
        ident = CB[:, B_ID:B_ID + 128]
        permm = CB[:, B_PM:B_PM + 128]
        ones64 = CB[:, B_ONES:B_ONES + 64]
        maskM = CB[:, B_M:B_M + 256]
        maskH = CB[:, B_MH:B_MH + 128]

        wcount = [0]
        CASTE = 'act'

        def do_cast(eng, dst_ap, src_ap):
            if eng == 'act':
                return lambda e: e.activation(out=dst_ap, in_=src_ap, func=AF.Identity)
            return lambda e: e.tensor_copy(out=dst_ap, in_=src_ap)

        def stage_cast(dram_ap, dst_ap, dst_res, eng=None, stg=None):
            eng = eng or CASTE
            i = wcount[0]
            wcount[0] += 1
            if stg is None:
                s_ = i % 2
                sap, sres = wst[s_][:], 'wst%d' % s_
            else:
                sap, sres = stg[i % len(stg)]
            T.dma('sp', lambda e: e.dma_start(out=sap, in_=dram_ap), [], [sres], sres)
            T.op(eng, do_cast(eng, dst_ap, sap), [sres], [dst_res])

        WORDER = []
        for pair_ in range(4):
            WORDER.append(36 + pair_)
            for g_ in range(3):
                WORDER += [12 + 4 * g_ + pair_, 4 * g_ + pair_, 24 + 4 * g_ + pair_]
        WORDER.append(40)
        for ch_ in range(1, 8):
            WORDER += [40 + ch_, 48 + ch_ - 1]
        WORDER.append(55)
        for dc_ in range(8):
            WORDER += [56 + dc_, 64 + dc_]
        wissued = [0]
        wbcount = [0]
        WLA = [2]

        def w_prefetch(upto):
            while wissued[0] < min(upto, len(WORDER)):
                kk = wissued[0]
                wissued[0] += 1
                stage_cast(win_d[WORDER[kk]], wbf[kk % 3][:], 'wbf%d' % (kk % 3))

        def load_w(c):
            k = wbcount[0]
            wbcount[0] += 1
            assert WORDER[k] == c, (k, c, WORDER[k])
            while wissued[0] <= min(k + WLA[0], len(WORDER) - 1):
                kk = wissued[0]
                wissued[0] += 1
                stage_cast(win_d[WORDER[kk]], wbf[kk % 3][:], 'wbf%d' % (kk % 3))
            return wbf[k % 3], 'wbf%d' % (k % 3)

        PEND = []

        def defer(f):
            PEND.append(f)

        def flush():
            run_ = PEND[:]
            del PEND[:]
            for f_ in run_:
                f_()

        pacount = [0]

        def next_pA():
            i = pacount[0] % 2
            pacount[0] += 1
            return pA[i], 'pA%d' % i

        XTALL = ['XT%d_%d' % (kc_, j_) for kc_ in range(8) for j_ in range(4)]

        def inproj(w, wres, pst, psres, movfn, n, extra=(), xcols=None):
            if xcols is None:
                xres_ = XTALL
            else:
                xres_ = ['XT%d_%d' % (kc_, j_) for kc_ in range(8) for j_ in range(xcols[0] // 1024, (xcols[1] - 1) // 1024 + 1)]
            def fn(e):
                ins = None
                for kc in range(8):
                    ins = e.matmul(pst[:, 0:n], lhsT=w[:, kc * 128:(kc + 1) * 128], rhs=movfn(kc),
                                   start=(kc == 0), stop=(kc == 7))
                return ins
            T.op('pe', fn, [wres] + xres_ + list(extra), [psres])
            flush()

        T.dma('sp', lambda e: e.dma_start(out=cst[:], in_=cst_d[:, :]), [], ['cst'], 'cst')
        T.dma('sp', lambda e: e.dma_start(out=wst[0][:, 0:704], in_=cb_d[:, 0:704]), [], ['wst0'], 'wst0')
        T.op('act', lambda e: e.activation(out=CB[:], in_=wst[0][:, 0:704], func=AF.Identity), ['wst0'], ['CB'])
        AGf = AG[:].bitcast(F32)
        stg0 = [(AGf[:, 1024 * i:1024 * i + 1024], 'stgA%d' % i) for i in range(4)] + [(wst[i][:], 'wst%d' % i) for i in range(2)]
        engs3 = ['act', 'dve', 'pool']
        n_ = 0
        for j in (2, 3, 0, 1):
            if j == 0:
                w_prefetch(3)
            for kc in range(8):
                stage_cast(xT_d[kc, :, 1024 * j:1024 * j + 1024], XTv[:, kc, 1024 * j:1024 * j + 1024], 'XT%d_%d' % (kc, j),
                           eng=engs3[n_ % 3], stg=stg0)
                n_ += 1
        T.op('dve', lambda e: e.tensor_scalar(out=cst2[:, D_HBIN:D_HBIN + 72], in0=cst[:, C_BIN:C_BIN + 72], scalar1=0.5,
                                              scalar2=None, op0=ALU.mult), ['cst'], ['cst2a'])
        T.op('dve', lambda e: e.tensor_scalar(out=cst2[:, D_HBR:D_HBR + 16], in0=cst[:, C_BR:C_BR + 16], scalar1=0.5,
                                              scalar2=None, op0=ALU.mult), ['cst'], ['cst2b'])
        T.op('dve', lambda e: e.tensor_scalar(out=cst2[:, D_FBU:D_FBU + 8], in0=cst[:, C_BIN + 40:C_BIN + 48],
                                              scalar1=col(C_FLAG), scalar2=None, op0=ALU.mult), ['cst'], ['cst2c'])
        T.op('act', lambda e: e.activation(out=cst2[:, D_TMP:D_TMP + 8], in_=cst[:, C_LAM:C_LAM + 8], func=AF.Exp, scale=-1.0),
             ['cst'], ['cst2t'])
        T.op('act', lambda e: e.activation(out=cst2[:, D_TMP + 8:D_TMP + 16], in_=cst2[:, D_TMP:D_TMP + 8], func=AF.Ln,
                                           bias=col(C_ONE), scale=1.0), ['cst2t', 'cst'], ['cst2u'])
        T.op('dve', lambda e: e.tensor_scalar(out=cst2[:, D_HSC:D_HSC + 8], in0=cst2[:, D_TMP + 8:D_TMP + 16], scalar1=-4.0,
                                              scalar2=None, op0=ALU.mult), ['cst2u'], ['cst2d'])
        T.op('dve', lambda e: e.tensor_scalar(out=cst2[:, D_SC1:D_SC1 + 8], in0=cst2[:, D_TMP + 8:D_TMP + 16], scalar1=-8.0,
                                              scalar2=None, op0=ALU.mult), ['cst2u'], ['cst2e'])
        CST = ['cst', 'cst2a', 'cst2b', 'cst2c', 'cst2d', 'cst2e', 'CB']

        cv = Carver()
        tabs = [cv.f32(2048) for _ in range(4)]
        accX = cv.f32(2048)
        accY = cv.f32(2048)
        QTP = [cv.bf16(2048) for _ in range(2)]
        KT = cv.bf16(4096)
        VK = cv.bf16(4096)
        kn = [cv.bf16(512) for _ in range(2)]
        t1 = [cv.f32(512) for _ in range(2)]
        t2 = cv.f32(512)
        NPS = 6
        PT = [[cv.bf16(256) for _ in range(NPS)] for _ in range(2)]
        sgz = cv.f32(512)
        sgt = cv.f32(512)
        SG = cv.bf16(2048)
        fin = KT.bitcast(F32)
        VTN = cv.bf16(4096)

        for i in range(4):
            T.dma('sp', (lambda i: lambda e: e.dma_start(out=tabs[i], in_=tab_d[i]))(i), [], ['tab%d' % i], 'tab%d' % i)
        TAB = ['tab0', 'tab1', 'tab2', 'tab3']
        T.op('dve', lambda e: e.memset(QTP[0][64:128, :], 0.0), [], ['QTz0'])
        T.op('dve', lambda e: e.memset(QTP[1][0:64, :], 0.0), [], ['QTz1'])

        rcount = [0]

        def rope_tile(pst, psres, n, bcol, ctab, stab, dsts, dstres, d):
            i = rcount[0] % 2
            rcount[0] += 1
            knb, t1b = kn[i], t1[i]
            T.op('act', lambda e: e.activation(out=knb[:, 0:n], in_=pst[:, 0:n], func=AF.Identity, bias=col(bcol), scale=1.0),
                 [psres] + CST, ['kn%d' % i])
            T.op('pool', lambda e: e.tensor_tensor(out=t1b[:, 0:n], in0=knb[:, 0:n], in1=ctab, op=ALU.mult),
                 ['kn%d' % i] + TAB, ['t1%d' % i])

            def post():
                T.op('pe', lambda e: e.matmul(pB[:, 0:n], lhsT=permm, rhs=knb[:, 0:n], start=True, stop=True),
                     ['kn%d' % i, 'CB'], ['pB'])
                T.op('dve', lambda e: e.tensor_tensor(out=t2[:, 0:n], in0=pB[:, 0:n], in1=stab, op=ALU.mult),
                     ['pB'] + TAB, ['t2'])
                for (d_, lo, hi) in dsts:
                    T.op('dve', lambda e, d_=d_, lo=lo, hi=hi: e.tensor_tensor(
                        out=d_, in0=srcview(t1b[lo:hi, 0:n], d), in1=srcview(t2[lo:hi, 0:n], d), op=ALU.add),
                        ['t1%d' % i, 't2'], [dstres])
            defer(post)

        xmain = lambda kc: XTv[:, kc, 2048:4096]
        xhalo = lambda kc: XTv[:, kc, 0:2048]

        for pair in range(4):
            c = 36 + pair
            w, wres = load_w(c)
            for j in range(4):
                pst, psres = next_pA()
                inproj(w, wres, pst, psres, lambda kc, j=j: xmain(kc)[:, 512 * j:512 * j + 512], 512, xcols=(2048 + 512 * j, 2560 + 512 * j))
                T.op('act', lambda e, pst=pst, c=c: e.activation(out=sgz[:], in_=pst[:, 0:512], func=AF.Identity,
                                                                 bias=col(C_BIN + c), scale=1.0), [psres] + CST, ['sgz'])
                T.op('act', lambda e, pst=pst, c=c: e.activation(out=sgt[:], in_=pst[:, 0:512], func=AF.Tanh,
                                                                 bias=col2(D_HBIN + c), scale=0.5), [psres] + CST, ['sgt'])
                T.op('dve', lambda e, j=j: e.scalar_tensor_tensor(out=SG[:, 512 * j:512 * j + 512], in0=sgt[:], scalar=1.0,
                                                                  in1=sgz[:], op0=ALU.add, op1=ALU.mult),
                     ['sgz', 'sgt'], ['SG'])

            for g, d in enumerate(DILS):
                nb = 16 // d
                nhalo = {1: 128, 4: 512, 16: 2048}[d]
                c = 12 + 4 * g + pair
                w, wres = load_w(c)
                KTm = KT[:, 2048:4096]
                for j in range(4):
                    pst, psres = next_pA()
                    inproj(w, wres, pst, psres, lambda kc, j=j: xmain(kc)[:, 512 * j:512 * j + 512], 512, xcols=(2048 + 512 * j, 2560 + 512 * j))
                    rope_tile(pst, psres, 512, C_BIN + c, tabs[0][:, 512 * j:512 * j + 512], tabs[1][:, 512 * j:512 * j + 512],
                              [(dstview(KTm, d, j), 0, 128)], 'KT', d)
                if d == 16:
                    for j in range(4):
                        pst, psres = next_pA()
                        inproj(w, wres, pst, psres, lambda kc, j=j: xhalo(kc)[:, 512 * j:512 * j + 512], 512, xcols=(512 * j, 512 * j + 512))
                        rope_tile(pst, psres, 512, C_BIN + c, tabs[2][:, 512 * j:512 * j + 512], tabs[3][:, 512 * j:512 * j + 512],
                                  [(dstview(KT[:, 0:2048], d, j), 0, 128)], 'KT', d)
                else:
                    h0 = 2048 - nhalo
                    pst, psres = next_pA()
                    inproj(w, wres, pst, psres, lambda kc, h0=h0: xhalo(kc)[:, h0:2048], nhalo)
                    hd = KT[:, 0:nhalo] if d == 1 else KT[:, 0:512].rearrange("p (r m) -> p r m", r=4)
                    rope_tile(pst, psres, nhalo, C_BIN + c, tabs[2][:, h0:2048], tabs[3][:, h0:2048], [(hd, 0, 128)], 'KT', d)
                c = 4 * g + pair
                w, wres = load_w(c)
                for j in range(4):
                    pst, psres = next_pA()
                    inproj(w, wres, pst, psres, lambda kc, j=j: xmain(kc)[:, 512 * j:512 * j + 512], 512, xcols=(2048 + 512 * j, 2560 + 512 * j))
                    rope_tile(pst, psres, 512, C_BIN + c, tabs[0][:, 512 * j:512 * j + 512], tabs[1][:, 512 * j:512 * j + 512],
                              [(dstview(QTP[0][0:64, :], d, j), 0, 64), (dstview(QTP[1][64:128, :], d, j), 64, 128)], 'QT', d)
                c = 24 + 4 * g + pair
                w, wres = load_w(c)

                def vproj(movfn, n, col0, c=c, w=w, wres=wres):
                    pst, psres = next_pA()
                    inproj(w, wres, pst, psres, movfn, n)
                    T.op('act', lambda e: e.activation(out=VTN[:, col0:col0 + n], in_=pst[:, 0:n], func=AF.Identity,
                                                       bias=col(C_BIN + c), scale=1.0), [psres] + CST, ['VTN'])

                def vtrans(srcs, tile0):
                    nq = len(srcs)
                    vi = vtcount[0] % 2
                    vtcount[0] += 1
                    bank, bres = (pT, 'pT') if vi == 0 else (pBb, 'pB')

                    def tr(e):
                        ins = None
                        for q, sv in enumerate(srcs):
                            ins = e.transpose(out=bank[:, 128 * q:128 * q + 128], in_=sv, identity=ident)
                        return ins
                    T.op('pe', tr, ['VTN', 'CB'], [bres])
                    if vi == 0:
                        T.op('dve', lambda e: e.tensor_copy(out=VK[:, 128 * tile0:128 * tile0 + 128 * nq], in_=bank[:, 0:128 * nq]),
                             [bres], ['VK%d' % vi])
                    else:
                        T.op('act', lambda e: e.activation(out=VK[:, 128 * tile0:128 * tile0 + 128 * nq], in_=bank[:, 0:128 * nq],
                                                           func=AF.Identity), [bres], ['VK%d' % vi])

                for j in range(4):
                    vproj(lambda kc, j=j: xmain(kc)[:, 512 * j:512 * j + 512], 512, 2048 + 512 * j)
                if d == 16:
                    for j in range(4):
                        vproj(lambda kc, j=j: xhalo(kc)[:, 512 * j:512 * j + 512], 512, 512 * j)
                else:
                    h0 = 2048 - nhalo
                    vproj(lambda kc, h0=h0: xhalo(kc)[:, h0:2048], nhalo, h0)
                flush()
                VM = VTN[:, 2048:4096]
                VH = VTN[:, 0:2048]
                gsub = lambda ap, r: (ap if d == 1 else ap.rearrange("p (m r) -> p r m", r=d)[:, r, :])
                for q4 in range(4):
                    srcs = []
                    for gb in range(4 * q4, 4 * q4 + 4):
                        s_, b_ = gb // nb, gb % nb
                        srcs.append(gsub(VM, s_)[:, 128 * b_:128 * b_ + 128])
                    vtrans(srcs, 16 + 4 * q4)
                if d == 16:
                    for q4 in range(4):
                        vtrans([gsub(VH, r_)[:, 0:128] for r_ in range(4 * q4, 4 * q4 + 4)], 4 * q4)
                elif d == 4:
                    vtrans([gsub(VH, r_)[:, 384:512] for r_ in range(4)], 0)
                else:
                    vtrans([VH[:, 1920:2048]], 0)

                flush()
                tiles = []
                for s in range(d):
                    for kt in range(-1, nb):
                        tiles.append((s, kt))
                tidx = {t: i for i, t in enumerate(tiles)}
                emitted = [0]

                def emit_S(i):
                    s, kt = tiles[i]
                    for hh in range(2):
                        if kt == -1:
                            ncols, q0, kcol, msk = 128, 128 * (s * nb), 128 * s, maskH
                        else:
                            ncols = 128 if kt == nb - 1 else 256
                            q0, kcol, msk = 128 * (s * nb + kt), 2048 + 128 * (s * nb + kt), maskM[:, 0:ncols]
                        si = (2 * i + hh) % 4
                        pst = SBANKS[si][0][:, 0:ncols]
                        psres = SBANKS[si][1]
                        ptb = PT[hh][i % NPS]
                        ptres = 'PT%d_%d' % (hh, i % NPS)
                        def sfn(e, pst=pst, hh=hh, kcol=kcol, q0=q0, ncols=ncols, msk=msk):
                            e.matmul(pst, lhsT=KT[:, kcol:kcol + 128], rhs=QTP[hh][:, q0:q0 + ncols], start=True, stop=False)
                            return e.matmul(pst, lhsT=ident, rhs=msk, start=False, stop=True)
                        T.op('pe', sfn, ['KT', 'QT', 'QTz0', 'QTz1', 'CB'], [psres])
                        T.op('act', lambda e, pst=pst, ptb=ptb, ncols=ncols: e.activation(
                            out=ptb[:, 0:ncols], in_=pst, func=AF.Exp, scale=0.125), [psres], [ptres])

                def emit_XY(s, b):
                    gb = s * nb + b
                    (bx, bxres), (by, byres) = XYSETS[(gb // 4) % 2]
                    c0 = 128 * (gb % 4)
                    ip = tidx[(s, b - 1)]
                    ic = tidx[(s, b)]
                    prev_cols = (0, 128) if b == 0 else (128, 256)
                    vprev = (s if b == 0 else 16 + gb - 1)
                    vcur = 16 + gb
                    reads = ['VK0', 'VK1', 'CB'] + ['PT%d_%d' % (hh, ip % NPS) for hh in range(2)] + \
                            ['PT%d_%d' % (hh, ic % NPS) for hh in range(2)]

                    def fn(e):
                        ins = None
                        for hh in range(2):
                            pp = PT[hh][ip % NPS][:, prev_cols[0]:prev_cols[1]]
                            pc = PT[hh][ic % NPS][:, 0:128]
                            for (dstp, lp, lc) in ((bx, VK[:, 128 * vprev + 64 * hh:128 * vprev + 64 * hh + 64],
                                                    VK[:, 128 * vcur + 64 * hh:128 * vcur + 64 * hh + 64]),
                                                   (by, ones64, ones64)):
                                o = dstp[64 * hh:64 * hh + 64, c0:c0 + 128]
                                e.matmul(o, lhsT=lp, rhs=pp, start=True, stop=False)
                                ins = e.matmul(o, lhsT=lc, rhs=pc, start=False, stop=True)
                        return ins
                    T.op('pe', fn, reads, [bxres, byres])
                    if gb % 4 == 3:
                        q4 = gb // 4
                        for (bank, res, acc, accres) in ((bx, bxres, accX, 'accX'), (by, byres, accY, 'accY')):
                            dst = gview(acc, d, q4)
                            src = like(bank[:, 0:512], dst)
                            if g == 0:
                                T.op('dve', lambda e, dst=dst, src=src: e.tensor_copy(out=dst, in_=src), [res], [accres])
                            else:
                                T.op('dve', lambda e, dst=dst, src=src: e.tensor_tensor(out=dst, in0=dst, in1=src, op=ALU.add),
                                     [res, accres], [accres])

                LOOK = 2
                for s in range(d):
                    for b in range(nb):
                        need = min(tidx[(s, b)] + LOOK, len(tiles) - 1)
                        while emitted[0] <= need:
                            emit_S(emitted[0])
                            emitted[0] += 1
                        emit_XY(s, b)

            T.op('act', lambda e: e.activation(out=fin, in_=accY, func=AF.Ln), ['accY'], ['KT'])
            T.op('act', lambda e: e.activation(out=fin, in_=fin, func=AF.Exp, scale=-1.0), ['KT'], ['KT'])
            T.op('dve', lambda e: e.tensor_tensor(out=fin, in0=fin, in1=accX, op=ALU.mult), ['KT', 'accX'], ['KT'])
            T.op('dve', lambda e, pair=pair: e.scalar_tensor_tensor(out=AGv[:, pair, :], in0=fin, scalar=0.5, in1=SG,
                                                                    op0=ALU.mult, op1=ALU.mult), ['KT', 'SG'] + ['stgA%d' % i_ for i_ in range(4)], ['AG'])

        T.barrier()

        cv = Carver()
        HG = cv.bf16(8 * NT)
        HGv = HG.rearrange("p (k t) -> p k t", k=8)
        UO = 8
        Us = [cv.bf16(2048 + 8) for _ in range(2)]
        As = [cv.f32(2048) for _ in range(2)]
        Ms = [cv.f32(2048) for _ in range(2)]
        ICs = [cv.f32(2048) for _ in range(2)]
        Ct = [cv.bf16(512) for _ in range(3)]
        thr = cv.f32(512)
        this_ = [cv.f32(512) for _ in range(2)]
        thi = this_[0]
        zb = [cv.bf16(512) for _ in range(2)]
        thb = [cv.bf16(512) for _ in range(2)]
        ticount = [0]
        Ht = [cv.f32(512) for _ in range(2)]
        BDR = cv.bf16(1024)
        BDI = cv.bf16(1024)
        DG = cv.bf16(4 * 128)
        hinit = cv.f32(2)
        stage_cast(bd_d[0], BDR, 'BDR')
        stage_cast(bd_d[1], BDI, 'BDI')
        WLA[0] = 1
        PB4 = [(pA[0], 'pA0'), (pA[1], 'pA1'), (pX, 'pX'), (pY, 'pY')]
        p4count = [0]

        def next_p4():
            i = p4count[0] % 4
            p4count[0] += 1
            return PB4[i]

        hcount = [0]
        ccount = [0]
        wB_ = {}

        def s12_steps(ch, hf):
            st_ = hf
            Uu, ures = Us[st_], 'U%d' % st_
            Au, ares = As[st_], 'A%d' % st_
            Mu, mres = Ms[st_], 'M%d' % st_
            ICu, icres = ICs[st_], 'IC%d' % st_
            c = 40 + ch
            if hf == 0:
                wB_['w'] = load_w(c)
                for j in range(4):
                    T.op('dve', lambda e, j=j: e.tensor_scalar(out=DG[:, 128 * j:128 * j + 128], in0=ident,
                                                               scalar1=col(C_CW + ch * 4 + j), scalar2=None, op0=ALU.mult),
                         CST, ['DG'])
                T.op('dve', lambda e: e.memset(Uu[:, 0:UO], 0.0), [], [ures])
            else:
                T.op('dve', lambda e: e.tensor_copy(out=Uu[:, UO - 3:UO], in_=Us[0][:, UO + 2045:UO + 2048]), ['U0'], [ures])
            w, wres = wB_['w']
            off = 2048 * hf
            convs = {}
            gts = {}

            def proj(k):
                if not (0 <= k < 4):
                    return
                (pb1, pres1) = next_p4()
                inproj(w, wres, pb1, pres1, lambda kc: XTv[:, kc, off + 512 * k:off + 512 * k + 512], 512,
                       xcols=(off + 512 * k, off + 512 * k + 512))
                if hf == 0:
                    T.op('dve', lambda e: e.tensor_scalar(out=Uu[:, UO + 512 * k:UO + 512 * k + 512], in0=pb1[:, 0:512],
                                                          scalar1=col(C_FLAG), scalar2=col2(D_FBU + ch), op0=ALU.mult,
                                                          op1=ALU.add), [pres1] + CST, [ures])
                else:
                    T.op('dve', lambda e: e.tensor_scalar(out=Uu[:, UO + 512 * k:UO + 512 * k + 512], in0=pb1[:, 0:512],
                                                          scalar1=col(C_BIN + c), scalar2=None, op0=ALU.add),
                         [pres1] + CST, [ures])

            def conv(j):
                if not (0 <= j < 4):
                    return
                (pb, pres) = next_p4()

                def convfn(e):
                    ins = None
                    for tp in range(4):
                        ins = e.matmul(pb[:, 0:512], lhsT=DG[:, 128 * tp:128 * tp + 128],
                                       rhs=Uu[:, UO + 512 * j - 3 + tp:UO + 512 * j - 3 + tp + 512], start=(tp == 0), stop=(tp == 3))
                    return ins
                T.op('pe', convfn, ['DG', ures], [pres])
                ci = ccount[0] % 3
                ccount[0] += 1
                cb_, cres = Ct[ci], 'Ct%d' % ci
                convs[j] = (cb_, cres)
                T.op('dve', lambda e: e.tensor_scalar(out=cb_[:], in0=pb[:, 0:512], scalar1=col(C_CB + ch), scalar2=None,
                                                      op0=ALU.add), [pres] + CST, [cres])

            def gates(j2):
                if not (0 <= j2 < 4):
                    return
                gb_, gres = convs[j2]
                sl = slice(512 * j2, 512 * j2 + 512)
                ti = ticount[0] % 2
                ticount[0] += 1
                tib, tires = this_[ti], 'thi%d' % ti
                gts[j2] = (tib, tires, gb_, gres)
                T.op('pe', lambda e: e.matmul(pB[:, 0:512], lhsT=BDR[:, 128 * ch:128 * ch + 128], rhs=gb_[:], start=True, stop=True),
                     [gres, 'BDR'], ['pB'])
                T.op('pe', lambda e: e.matmul(pS[0][:, 0:512], lhsT=BDI[:, 128 * ch:128 * ch + 128], rhs=gb_[:], start=True,
                                              stop=True), [gres, 'BDI'], ['bS0'])
                T.op('act', lambda e: e.activation(out=thr[:], in_=pB[:, 0:512], func=AF.Tanh, bias=col2(D_HBR + ch), scale=0.5),
                     ['pB'] + CST, ['thr'])
                T.op('act', lambda e: e.activation(out=tib[:], in_=pS[0][:, 0:512], func=AF.Tanh, bias=col2(D_HBI + ch), scale=0.5),
                     ['bS0'] + CST, [tires])
                T.op('act', lambda e: e.activation(out=Au[:, sl], in_=thr[:], func=AF.Exp, bias=col2(D_HSC + ch),
                                                   scale=col2(D_HSC + ch)), ['thr'] + CST, [ares])
                T.op('act', lambda e: e.activation(out=Mu[:, sl], in_=thr[:], func=AF.Exp, bias=col2(D_SC1 + ch),
                                                   scale=col2(D_SC1 + ch)), ['thr'] + CST, [mres])

            def ic(j2):
                if not (0 <= j2 < 4):
                    return
                tib, tires, gb_, gres = gts[j2]
                sl = slice(512 * j2, 512 * j2 + 512)
                T.op('dve', lambda e: e.scalar_tensor_tensor(out=ICu[:, sl], in0=tib[:], scalar=1.0, in1=gb_[:], op0=ALU.add,
                                                             op1=ALU.mult), [tires, gres], [icres])
            return proj, conv, gates, ic

        def s45_steps(ch, hf):
            st_ = hf
            Au, ares = As[st_], 'A%d' % st_
            Mu, mres = Ms[st_], 'M%d' % st_
            ICu, icres = ICs[st_], 'IC%d' % st_
            T.op('act', lambda e: e.activation(out=Mu[:], in_=Mu[:], func=AF.Sqrt, bias=col(C_ONE), scale=-1.0),
                 [mres] + CST, [mres])
            if hf == 1:
                c2 = 48 + ch
                w2, wres2 = load_w(c2)
            hts = {}

            def main(j):
                if not (0 <= j < 4):
                    return
                sl = slice(512 * j, 512 * j + 512)
                T.op('dve', lambda e: e.scalar_tensor_tensor(out=Mu[:, sl], in0=Mu[:, sl], scalar=0.5, in1=ICu[:, sl],
                                                             op0=ALU.mult, op1=ALU.mult), [mres, icres], [mres])
                hi = hcount[0] % 2
                hcount[0] += 1
                hb_, hres = Ht[hi], 'Ht%d' % hi
                pb_, pres_ = Ht[1 - hi], 'Ht%d' % (1 - hi)
                hts[j] = (hb_, hres)
                if hf == 0 and j == 0:
                    T.op('dve', lambda e: e.tensor_tensor_scan(out=hb_[:], data0=Au[:, sl], data1=Mu[:, sl], initial=0.0,
                                                               op0=ALU.mult, op1=ALU.add), [ares, mres], [hres])
                elif hf == 1 and j == 0:
                    T.op('dve', lambda e: e.tensor_scalar(out=hinit[:, 0:1], in0=pb_[:, 511:512], scalar1=col(C_FLAG), scalar2=None,
                                                          op0=ALU.mult), [pres_] + CST, ['hinit'])
                    T.op('dve', lambda e: e.tensor_tensor_scan(out=hb_[:], data0=Au[:, sl], data1=Mu[:, sl], initial=hinit[:, 0:1],
                                                               op0=ALU.mult, op1=ALU.add), [ares, mres, 'hinit'], [hres])
                else:
                    T.op('dve', lambda e: e.tensor_tensor_scan(out=hb_[:], data0=Au[:, sl], data1=Mu[:, sl], initial=pb_[:, 511:512],
                                                               op0=ALU.mult, op1=ALU.add), [ares, mres, pres_], [hres])
                if hf == 1:
                    (pb, pres) = next_p4()
                    inproj(w2, wres2, pb, pres, lambda kc: XTv[:, kc, 2048 + 512 * j:2048 + 512 * j + 512], 512,
                           xcols=(2048 + 512 * j, 2560 + 512 * j))
                    zi = j % 2
                    T.op('act', lambda e: e.activation(out=zb[zi][:], in_=pb[:, 0:512], func=AF.Identity, bias=col(C_BIN + c2),
                                                       scale=1.0), [pres] + CST, ['zb%d' % zi])
                    T.op('act', lambda e: e.activation(out=thb[zi][:], in_=pb[:, 0:512], func=AF.Tanh, bias=col2(D_HBIN + c2),
                                                       scale=0.5), [pres] + CST, ['thb%d' % zi])

            def tail(j):
                if not (0 <= j < 4) or hf == 0:
                    return
                sl = slice(512 * j, 512 * j + 512)
                zi = j % 2
                hb_, hres = hts[j]
                T.op('dve', lambda e: e.scalar_tensor_tensor(out=zb[zi][:], in0=thb[zi][:], scalar=1.0, in1=zb[zi][:], op0=ALU.add,
                                                             op1=ALU.mult), ['zb%d' % zi, 'thb%d' % zi], ['zb%d' % zi])
                T.op('dve', lambda e: e.scalar_tensor_tensor(out=HGv[:, ch, sl], in0=zb[zi][:], scalar=0.5, in1=hb_[:],
                                                             op0=ALU.mult, op1=ALU.mult), ['zb%d' % zi, hres], ['HG'])
            return main, tail

        units = [(ch, hf) for ch in range(8) for hf in range(2)]
        NFILL = 2
        NU = len(units)
        E12 = {}
        E45 = {}
        nop = lambda k: None
        for g_ in range(4 * NU + 12):
            def filler(e):
                ins = None
                for _q in range(NFILL):
                    ins = e.matmul(pS[1][:, 0:512], lhsT=ident, rhs=XTv[:, 0, 0:512], start=True, stop=True)
                return ins
            T.op('pe', filler, [], ['bS1'])
            for v in range(NU):
                if v in E45:
                    E45[v][1](g_ - (4 * v + 8))
            gt = g_ - 3
            if 0 <= gt < 4 * NU:
                E12[gt // 4][2](gt % 4)
            for v in range(NU):
                if g_ == 4 * v + 6:
                    E45[v] = s45_steps(*units[v])
                if v in E45:
                    E45[v][0](g_ - (4 * v + 7))
            if 0 <= gt < 4 * NU:
                E12[gt // 4][3](gt % 4)
            ct = g_ - 1
            if 0 <= ct < 4 * NU:
                E12[ct // 4][1](ct % 4)
            if g_ < 4 * NU:
                if g_ % 4 == 0:
                    E12[g_ // 4] = s12_steps(*units[g_ // 4])
                E12[g_ // 4][0](g_ % 4)

        T.barrier()

        cv = Carver()
        HG = cv.bf16(8 * NT)
        HGv = HG.rearrange("p (k t) -> p k t", k=8)
        MG = cv.bf16(8 * NT)
        MGv = MG.rearrange("p (k t) -> p k t", k=8)
        c1_mark = cv.off
        WAP = cv.bf16(4 * 1024)
        WLP = cv.bf16(8 * 1024)
        thA = cv.f32(512)
        thB = cv.f32(512)
        m1 = cv.f32(512)
        m2 = cv.f32(512)
        stgC = [(cv.f32(1024), 'stgC%d' % i) for i in range(3)] + [(wst[i][:], 'wst%d' % i) for i in range(2)]
        engsC = ['act', 'dve', 'pool']
        for kc in range(4):
            stage_cast(wap_d[kc], WAP[:, 1024 * kc:1024 * kc + 1024], 'WAP%d' % kc, eng=engsC[kc % 2], stg=stgC)
        for kc in range(8):
            stage_cast(wlp_d[kc], WLP[:, 1024 * kc:1024 * kc + 1024], 'WLP%d' % kc, eng=engsC[kc % 3], stg=stgC)

        WLA[0] = 1
        for dc in range(8):
            cA, cB_ = 56 + dc, 64 + dc
            wA, wAres = load_w(cA)
            wB, wBres = load_w(cB_)
            for j in range(4):
                sl = slice(512 * j, 512 * j + 512)
                pstA, psresA = next_pA()
                inproj(wA, wAres, pstA, psresA, lambda kc, j=j: xmain(kc)[:, 512 * j:512 * j + 512], 512)
                T.op('act', lambda e, pstA=pstA, cA=cA: e.activation(out=thA[:], in_=pstA[:, 0:512], func=AF.Tanh,
                                                                     bias=col2(D_HBIN + cA), scale=0.5), [psresA] + CST, ['thA'])
                pstB, psresB = next_pA()
                inproj(wB, wBres, pstB, psresB, lambda kc, j=j: xmain(kc)[:, 512 * j:512 * j + 512], 512)
                T.op('act', lambda e, pstB=pstB, cB_=cB_: e.activation(out=thB[:], in_=pstB[:, 0:512], func=AF.Tanh,
                                                                       bias=col2(D_HBIN + cB_), scale=0.5), [psresB] + CST, ['thB'])

                def yafn(e, dc=dc, sl=sl):
                    ins = None
                    for kc in range(4):
                        ins = e.matmul(pX[:, 0:512], lhsT=WAP[:, 1024 * kc + 128 * dc:1024 * kc + 128 * dc + 128],
                                       rhs=AGv[:, kc, sl], start=(kc == 0), stop=(kc == 3))
                    return ins
                T.op('pe', yafn, ['WAP%d' % i_ for i_ in range(4)] + ['AG'], ['pX'])

                def ybfn(e, dc=dc, sl=sl):
                    ins = None
                    for kc in range(8):
                        ins = e.matmul(pY[:, 0:512], lhsT=WLP[:, 1024 * kc + 128 * dc:1024 * kc + 128 * dc + 128],
                                       rhs=HGv[:, kc, sl], start=(kc == 0), stop=(kc == 7))
                    return ins
                T.op('pe', ybfn, ['WLP%d' % i_ for i_ in range(8)] + ['HG'], ['pY'])
                T.op('dve', lambda e: e.scalar_tensor_tensor(out=m1[:], in0=thA[:], scalar=1.0, in1=pX[:, 0:512], op0=ALU.add,
                                                             op1=ALU.mult), ['thA', 'pX'], ['m1'])
                T.op('dve', lambda e: e.scalar_tensor_tensor(out=m2[:], in0=thB[:], scalar=1.0, in1=pY[:, 0:512], op0=ALU.add,
                                                             op1=ALU.mult), ['thB', 'pY'], ['m2'])
                T.op('dve', lambda e, dc=dc, sl=sl: e.tensor_tensor(out=MGv[:, dc, sl], in0=m1[:], in1=m2[:], op=ALU.add),
                     ['m1', 'm2'], ['MG'])

        T.barrier()

        cv.off = c1_mark
        WO = cv.bf16(8 * 1024)
        stats = [cv.f32(16) for _ in range(2)]
        mv = [cv.f32(8) for _ in range(2)]
        for kc in range(8):
            stage_cast(wo_d[kc], WO[:, 1024 * kc:1024 * kc + 1024], 'WO')
        xh = lambda kc: XTv[:, kc, 0:2048].bitcast(F32)
        yo = [xh(3), xh(4)]
        bcb = [xh(5), xh(6), xh(7)]
        hgf = HG.bitcast(F32)
        ybufs = [hgf[:, 0:1024], hgf[:, 1024:2048]]
        for i in range(3):
            T.dma('sp', (lambda i: lambda e: e.dma_start(out=bcb[i], in_=bc_d[i]))(i), [], ['bc%d' % i], 'bc%d' % i)

        xr = [xh(0), xh(1), xh(2)]

        def dma_x(tt):
            xi = tt % 3
            T.dma('sp', lambda e: e.dma_start(out=xr[xi], in_=xres_d[tt]), [], ['xr%d' % xi], 'xr%d' % xi)

        def prep_x(tt):
            xi = tt % 3
            T.op('act', lambda e: e.activation(out=xr[xi], in_=xr[xi], func=AF.Identity, scale=ALPHA), ['xr%d' % xi], ['xr%d' % xi])

        def stage_b1(tt):
            i2 = tt % 2
            mv_, mvres = mv[i2], 'mv%d' % i2
            ybuf, yres = ybufs[i2], 'ybuf%d' % i2
            T.op('dve', lambda e: e.scalar_tensor_tensor(out=mv_[:, 4:5], in0=mv_[:, 0:1], scalar=-1.0, in1=mv_[:, 3:4],
                                                         op0=ALU.mult, op1=ALU.mult), [mvres, mvres + 'c'], [mvres + 'd'])
            T.op('act', lambda e: e.activation(out=yo[i2], in_=ybuf, func=AF.Identity, bias=mv_[:, 4:5], scale=mv_[:, 3:4]),
                 [yres, mvres + 'c', mvres + 'd'], ['yo%d' % i2])

        def stage_b2(tt):
            i2 = tt % 2
            T.op('dve', lambda e: e.tensor_tensor(out=yo[i2], in0=yo[i2], in1=bcb[1], op=ALU.mult), ['yo%d' % i2, 'bc1'],
                 ['yo%d' % i2])
            T.op('pool', lambda e: e.tensor_tensor(out=yo[i2], in0=yo[i2], in1=bcb[2], op=ALU.add), ['yo%d' % i2, 'bc2'],
                 ['yo%d' % i2])
            T.dma('sp', lambda e: e.dma_start(out=y_d[tt], in_=yo[i2]), ['yo%d' % i2], [], 'y%d' % i2)

        brow = cv.f32(1024)
        onesr = cv.f32(128)
        T.dma('sp', lambda e: e.dma_start(out=brow[0:1, :], in_=bc_d[0, 0:1, :]), [], ['brow'], 'brow')
        T.op('dve', lambda e: e.tensor_scalar(out=brow[0:1, :], in0=brow[0:1, :], scalar1=2.0, scalar2=None, op0=ALU.mult),
             ['brow'], ['brow'])
        T.op('dve', lambda e: e.memset(onesr[0:1, :], 1.0), [], ['onesr'])

        def stage_a(tt):
            i2, xi = tt % 2, tt % 3
            ybuf, yres = ybufs[i2], 'ybuf%d' % i2
            st_, stres = stats[i2], 'stats%d' % i2
            mv_, mvres = mv[i2], 'mv%d' % i2
            for hf in range(2):
                pst, psres = next_pA()

                def ofn(e, pst=pst, hf=hf):
                    ins = None
                    for kc in range(8):
                        e.matmul(pst[:, 0:512], lhsT=MGv[:, kc, 128 * tt:128 * tt + 128],
                                 rhs=WO[:, 1024 * kc + 512 * hf:1024 * kc + 512 * hf + 512], start=(kc == 0), stop=False)
                    return e.matmul(pst[:, 0:512], lhsT=onesr[0:1, 0:128], rhs=brow[0:1, 512 * hf:512 * hf + 512], start=False, stop=True)
                T.op('pe', ofn, ['MG', 'WO', 'brow', 'onesr'], [psres])
                T.op('dve', lambda e, pst=pst, hf=hf: e.scalar_tensor_tensor(
                    out=ybuf[:, 512 * hf:512 * hf + 512], in0=pst[:, 0:512], scalar=0.5, in1=xr[xi][:, 512 * hf:512 * hf + 512],
                    op0=ALU.mult, op1=ALU.add), [psres, 'xr%d' % xi], [yres])
                T.op('dve', lambda e, hf=hf: e.bn_stats(out=st_[:, 6 * hf:6 * hf + 6], in_=ybuf[:, 512 * hf:512 * hf + 512]),
                     [yres], [stres])
            T.op('dve', lambda e: e.bn_aggr(out=mv_[:, 0:2], in_=st_[:, 0:12]), [stres], [mvres])
            T.op('act', lambda e: e.activation(out=mv_[:, 2:3], in_=mv_[:, 1:2], func=AF.Ln, bias=col(C_EPS), scale=1.0),
                 [mvres] + CST, [mvres + 'b'])
            T.op('act', lambda e: e.activation(out=mv_[:, 3:4], in_=mv_[:, 2:3], func=AF.Exp, scale=-0.5),
                 [mvres + 'b'], [mvres + 'c'])

        dma_x(0)
        dma_x(1)
        dma_x(2)
        prep_x(0)
        prep_x(1)
        for tt in range(17):
            if tt >= 1:
                stage_b1(tt - 1)
            if tt < 16:
                stage_a(tt)
            if tt + 3 < 16:
                dma_x(tt + 3)
            if tt >= 1:
                stage_b2(tt - 1)
            if tt + 2 < 16:
                prep_x(tt + 2)

        T.wait_all('sp')
        T.emit()
    return nc


def _rope_tables(pos0):
    half = 32
    inv_freq = (10000.0 ** (-np.arange(half, dtype=np.float32) / half)).astype(np.float32)
    pos = np.arange(pos0, pos0 + 2048, dtype=np.float32)
    ang = pos[None, :] * inv_freq[:, None]
    cos = np.cos(ang).astype(np.float32)
    sin = np.sin(ang).astype(np.float32)
    p = np.arange(128)
    i = p % 32
    sign = np.where((p % 64) < 32, -1.0, 1.0).astype(np.float32)
    return cos[i], sin[i] * sign[:, None]


def _consts_bf(flag):
    cb = np.zeros((128, 1024), np.float32)
    cb[:, B_ID:B_ID + 128] = np.eye(128, dtype=np.float32)
    pm = np.zeros((128, 128), np.float32)
    for m in range(128):
        partner = m + 32 if (m % 64) < 32 else m - 32
        pm[partner, m] = 1.0
    cb[:, B_PM:B_PM + 128] = pm
    cb[:, B_ONES:B_ONES + 64] = 1.0
    k = np.arange(128)[:, None]
    q = np.arange(128)[None, :]
    NEG = -1000.0
    cur = np.where(k <= q, 0.0, NEG).astype(np.float32)
    prev = np.where(k >= q, 0.0, NEG).astype(np.float32)
    cb[:, B_M:B_M + 128] = cur
    cb[:, B_M + 128:B_M + 256] = prev
    cb[:, B_MH:B_MH + 128] = prev if flag > 0.5 else NEG
    return cb


_NC_CACHE = {}


def kernel(x, w_in, b_in, conv_w, conv_b, lru_wr, lru_br, lru_wi, lru_bi, lru_lambda,
           w_attn_proj, w_lru_proj, w_out, b_out, ln_gain, ln_bias):
    f = lambda a: np.ascontiguousarray(np.asarray(a, dtype=np.float32))
    x = f(x)
    W = f(w_in)[0]
    win_l = f(W.reshape(8, 128, 72, 128).transpose(2, 1, 0, 3).reshape(72, 128, 1024))
    colmaj = lambda v, n: f(np.asarray(v, np.float32).reshape(n, 128).T)
    bd = np.zeros((2, 8, 128, 128), np.float32)
    for t, wsrc in enumerate((f(lru_wr)[0], f(lru_wi)[0])):
        for ch in range(8):
            bd[t, ch, 0:64, 0:64] = wsrc[2 * ch]
            bd[t, ch, 64:128, 64:128] = wsrc[2 * ch + 1]
    bd_l = f(bd.transpose(0, 2, 1, 3).reshape(2, 128, 1024))
    wap_l = f(f(w_attn_proj)[0].reshape(4, 128, 1024))
    wlp_l = f(f(w_lru_proj)[0].reshape(8, 128, 1024))
    wo_l = f(f(w_out)[0].reshape(8, 128, 1024))
    bc = f(np.stack([np.broadcast_to(f(b_out)[0], (128, 1024)), np.broadcast_to(f(ln_gain)[0], (128, 1024)),
                     np.broadcast_to(f(ln_bias)[0], (128, 1024))]))
    cw = f(conv_w)[0]
    in_maps = []
    for c in range(8):
        b, half = c // 2, c % 2
        flag = float(half)
        xm = x[b, half * 2048:(half + 1) * 2048]
        xh = x[b, 0:2048] if half == 1 else np.zeros((2048, 1024), np.float32)
        xT = np.concatenate([xh.T, xm.T], axis=1)
        cst = np.zeros((128, NCST), np.float32)
        cst[:, C_BIN:C_BIN + 72] = colmaj(f(b_in)[0], 72)
        for ch in range(8):
            for j in range(4):
                cst[:, C_CW + ch * 4 + j] = cw[j, ch * 128:(ch + 1) * 128]
        cst[:, C_CB:C_CB + 8] = colmaj(f(conv_b)[0], 8)
        cst[:, C_BR:C_BR + 8] = colmaj(f(lru_br)[0], 8)
        cst[:, C_BI:C_BI + 8] = colmaj(f(lru_bi)[0], 8)
        cst[:, C_LAM:C_LAM + 8] = colmaj(f(lru_lambda)[0], 8)
        cst[:, C_FLAG] = flag
        cst[:, C_ONE] = 1.0
        cst[:, C_MHALF] = -0.5
        cst[:, C_EPS] = LN_EPS
        cm, sm = _rope_tables(half * 2048)
        chh, shh = _rope_tables(0)
        in_maps.append({
            "xT": f(xT.reshape(8, 128, 4096)),
            "xres": f(xm.reshape(16, 128, 1024)),
            "win": win_l,
            "cst": cst,
            "cb": _consts_bf(flag),
            "tab": f(np.stack([cm, sm, chh, shh])),
            "bd": bd_l,
            "wap": wap_l,
            "wlp": wlp_l,
            "wo": wo_l,
            "bc": bc,
        })
    if "nc" not in _NC_CACHE:
        _NC_CACHE["nc"] = build()
    res = run_bass_kernel_spmd(_NC_CACHE["nc"], in_maps, core_ids=list(range(8)))
    out = np.zeros((4, 4096, 1024), np.float32)
    for c in range(8):
        b, half = c // 2, c % 2
        out[b, half * 2048:(half + 1) * 2048] = np.asarray(res.results[c]["y"]).reshape(2048, 1024)
    return out
```

```python
import contextlib
import numpy as np
import concourse.bass as bass
import concourse.mybir as mybir
from concourse.bass_utils import run_bass_kernel_spmd

F32 = mybir.dt.float32
BF16 = mybir.dt.bfloat16
AF = mybir.ActivationFunctionType
ALU = mybir.AluOpType

NT = 2048
ALPHA = 2.0 ** 0.25
LN_EPS = 1e-5
DILS = (1, 4, 16)

C_BIN, C_CW, C_CB, C_BR, C_BI, C_LAM, C_FLAG, C_ONE, C_MHALF, C_EPS, NCST = 0, 72, 104, 112, 120, 128, 136, 137, 138, 140, 144
D_HBIN, D_HBR, D_HBI, D_HSC, D_SC1, D_FBU, D_TMP, NCST2 = 0, 72, 80, 88, 96, 104, 112, 128
B_ID, B_PM, B_ONES, B_M, B_MH = 0, 128, 256, 320, 576


class Tracker:
    def __init__(self, nc, stack):
        self.nc = nc
        self.stack = stack
        self.engs = {'pe': nc.tensor, 'act': nc.scalar, 'dve': nc.vector, 'pool': nc.gpsimd, 'sp': nc.sync}
        self.streams = {e: [] for e in self.engs}
        self.sems = {}
        self.semval = {}
        self.waited = {e: {} for e in self.engs}
        self.last_w = {}
        self.last_r = {}

    def sem(self, name):
        if name not in self.sems:
            self.sems[name] = self.stack.enter_context(self.nc.semaphore(name))
            self.semval[name] = 0
        return name

    def _need(self, eng, deps):
        for (s, v) in deps:
            if self.waited[eng].get(s, 0) < v:
                self.waited[eng][s] = v
                self.streams[eng].append(('wait', s, v))

    def op(self, eng, fn, reads=(), writes=(), sem=None, inc=1):
        deps = []
        for r in reads:
            if r in self.last_w:
                deps.append(self.last_w[r])
        for w in writes:
            if w in self.last_w:
                deps.append(self.last_w[w])
            for s, v in self.last_r.get(w, {}).items():
                deps.append((s, v))
        own = 'S_' + eng
        if eng == 'pe':
            deps = [d for d in deps if d[0] != own]
        self._need(eng, deps)
        s = self.sem(sem if sem is not None else own)
        self.semval[s] += inc
        v = self.semval[s]
        self.streams[eng].append(('op', fn, s, inc))
        for w in writes:
            self.last_w[w] = (s, v)
            self.last_r[w] = {}
        for r in reads:
            d = self.last_r.setdefault(r, {})
            d[s] = max(d.get(s, 0), v)

    def dma(self, eng, fn, reads, writes, slot):
        self.op(eng, fn, reads, writes, sem='D_' + slot, inc=16)

    def wait_all(self, eng):
        self._need(eng, [(s, v) for s, v in self.semval.items() if v > 0])

    def barrier(self):
        for e in self.engs:
            self.wait_all(e)

    def emit(self):
        nc = self.nc
        with nc.Block() as block:
            def run(engname, eng):
                for item in self.streams[engname]:
                    if item[0] == 'wait':
                        eng.wait_ge(self.sems[item[1]], item[2])
                    else:
                        _, fn, s, inc = item
                        ins = fn(eng)
                        ins.then_inc(self.sems[s], inc)

            @block.tensor
            def _(e):
                run('pe', e)

            @block.scalar
            def _(e):
                run('act', e)

            @block.vector
            def _(e):
                run('dve', e)

            @block.gpsimd
            def _(e):
                run('pool', e)

            @block.sync
            def _(e):
                run('sp', e)


def gview(ap, d, j):
    if d == 1:
        return ap[:, 512 * j:512 * j + 512]
    if d == 4:
        return ap.rearrange("p (m r) -> p r m", r=4)[:, j, :]
    return ap.rearrange("p (m r) -> p r m", r=16)[:, 4 * j:4 * j + 4, :]


def hview(ap, d):
    if d == 1:
        return ap[:, 1920:2048]
    return ap.rearrange("p (m r) -> p r m", r=4)[:, :, 384:512]


def dstview(buf, d, j):
    if d == 1:
        return buf[:, 512 * j:512 * j + 512]
    if d == 4:
        return buf.rearrange("p (r m) -> p r m", r=4)[:, :, 128 * j:128 * j + 128]
    return buf.rearrange("p (r m) -> p r m", r=16)[:, :, 32 * j:32 * j + 32]


def srcview(flat, d):
    if d == 1:
        return flat
    return flat.rearrange("p (m r) -> p r m", r=d)


def like(flat, ref):
    sh = ref.shape
    if len(sh) == 2:
        return flat
    return flat.rearrange("p (a b) -> p a b", a=sh[1])


def build():
    nc = bass.Bass("TRN2", target_bir_lowering=False)
    dt = lambda name, shape, kind="ExternalInput": nc.dram_tensor(name, shape, F32, kind=kind).ap()
    sbn = lambda n: "s_" + n
    xT_d = dt("xT", [8, 128, 4096])
    xres_d = dt("xres", [16, 128, 1024])
    win_d = dt("win", [72, 128, 1024])
    cst_d = dt("cst", [128, NCST])
    cb_d = dt("cb", [128, 1024])
    tab_d = dt("tab", [4, 128, 2048])
    bd_d = dt("bd", [2, 128, 1024])
    wap_d = dt("wap", [4, 128, 1024])
    wlp_d = dt("wlp", [8, 128, 1024])
    wo_d = dt("wo", [8, 128, 1024])
    bc_d = dt("bc", [3, 128, 1024])
    y_d = dt("y", [16, 128, 1024], kind="ExternalOutput")

    with contextlib.ExitStack() as st:
        T = Tracker(nc, st)
        sb = lambda name, shape, dtp: st.enter_context(nc.sbuf_tensor("s_" + name, shape, dtp))
        ps = lambda name, shape, dtp: st.enter_context(nc.psum_tensor(name, shape, dtp))

        XT = sb("XT", [128, 8 * 4096], BF16)
        XTv = XT[:].rearrange("p (k t) -> p k t", k=8)
        AG = sb("AG", [128, 4 * NT], BF16)
        AGv = AG[:].rearrange("p (k t) -> p k t", k=4)
        cst = sb("cst", [128, NCST], F32)
        cst2 = sb("cst2", [128, NCST2], F32)
        CB = sb("CB", [128, 704], BF16)
        wst = [sb("wst%d" % i, [128, 1024], F32) for i in range(2)]
        wbf = [sb("wbf%d" % i, [128, 1024], BF16) for i in range(3)]
        ARENA_F = 28288
        arena = sb("arena", [128, ARENA_F], F32)

        pA = [ps("pA%d" % i, [128, 512], F32) for i in range(2)]
        pB = ps("pB", [128, 512], F32)
        pS = [ps("pS%d" % i, [128, 512], F32) for i in range(2)]
        pX = ps("pX", [128, 512], F32)
        pY = ps("pY", [128, 512], F32)
        pT = ps("pT", [128, 1024], BF16)
        pT32 = pT[:].bitcast(F32)
        pBb = pB[:].bitcast(BF16)
        vtcount = [0]
        SBANKS = [(pS[0], 'bS0'), (pS[1], 'bS1'), (pA[0], 'pA0'), (pA[1], 'pA1')]
        XYSETS = [((pX, 'pX'), (pY, 'pY')), ((pB, 'pB'), (pT32, 'pT'))]

        class Carver:
            def __init__(self):
                self.off = 0

            def f32(self, n):
                a = arena[:, self.off:self.off + n]
                self.off += n
                assert self.off <= ARENA_F, self.off
                return a

            def bf16(self, n):
                assert n % 2 == 0
                return self.f32(n // 2).bitcast(BF16)

        col = lambda c: cst[:, c:c + 1]
        col2 = lambda c: cst2[:, c:c + 1]
        ident = CB[:, B_ID:B_ID + 128]
        permm = CB[:, B_PM:B_PM + 128]
        ones64 = CB[:, B_ONES:B_ONES + 64]
        maskM = CB[:, B_M:B_M + 256]
        maskH = CB[:, B_MH:B_MH + 128]

        wcount = [0]
        CASTE = 'act'

        def do_cast(eng, dst_ap, src_ap):
            if eng == 'act':
                return lambda e: e.activation(out=dst_ap, in_=src_ap, func=AF.Identity)
            return lambda e: e.tensor_copy(out=dst_ap, in_=src_ap)

        def stage_cast(dram_ap, dst_ap, dst_res, eng=None, stg=None):
            eng = eng or CASTE
            i = wcount[0]
            wcount[0] += 1
            if stg is None:
                s_ = i % 2
                sap, sres = wst[s_][:], 'wst%d' % s_
            else:
                sap, sres = stg[i % len(stg)]
            T.dma('sp', lambda e: e.dma_start(out=sap, in_=dram_ap), [], [sres], sres)
            T.op(eng, do_cast(eng, dst_ap, sap), [sres], [dst_res])

        WORDER = []
        for pair_ in range(4):
            WORDER.append(36 + pair_)
            for g_ in range(3):
                WORDER += [12 + 4 * g_ + pair_, 4 * g_ + pair_, 24 + 4 * g_ + pair_]
        WORDER.append(40)
        for ch_ in range(1, 8):
            WORDER += [40 + ch_, 48 + ch_ - 1]
        WORDER.append(55)
        for dc_ in range(8):
            WORDER += [56 + dc_, 64 + dc_]
        wissued = [0]
        wbcount = [0]
        WLA = [2]

        def w_prefetch(upto):
            while wissued[0] < min(upto, len(WORDER)):
                kk = wissued[0]
                wissued[0] += 1
                stage_cast(win_d[WORDER[kk]], wbf[kk % 3][:], 'wbf%d' % (kk % 3))

        def load_w(c):
            k = wbcount[0]
            wbcount[0] += 1
            assert WORDER[k] == c, (k, c, WORDER[k])
            while wissued[0] <= min(k + WLA[0], len(WORDER) - 1):
                kk = wissued[0]
                wissued[0] += 1
                stage_cast(win_d[WORDER[kk]], wbf[kk % 3][:], 'wbf%d' % (kk % 3))
            return wbf[k % 3], 'wbf%d' % (k % 3)

        PEND = []

        def defer(f):
            PEND.append(f)

        def flush():
            run_ = PEND[:]
            del PEND[:]
            for f_ in run_:
                f_()

        pacount = [0]

        def next_pA():
            i = pacount[0] % 2
            pacount[0] += 1
            return pA[i], 'pA%d' % i

        XTALL = ['XT%d_%d' % (kc_, j_) for kc_ in range(8) for j_ in range(4)]

        def inproj(w, wres, pst, psres, movfn, n, extra=(), xcols=None):
            if xcols is None:
                xres_ = XTALL
            else:
                xres_ = ['XT%d_%d' % (kc_, j_) for kc_ in range(8) for j_ in range(xcols[0] // 1024, (xcols[1] - 1) // 1024 + 1)]
            def fn(e):
                ins = None
                for kc in range(8):
                    ins = e.matmul(pst[:, 0:n], lhsT=w[:, kc * 128:(kc + 1) * 128], rhs=movfn(kc),
                                   start=(kc == 0), stop=(kc == 7))
                return ins
            T.op('pe', fn, [wres] + xres_ + list(extra), [psres])
            flush()

        T.dma('sp', lambda e: e.dma_start(out=cst[:], in_=cst_d[:, :]), [], ['cst'], 'cst')
        T.dma('sp', lambda e: e.dma_start(out=wst[0][:, 0:704], in_=cb_d[:, 0:704]), [], ['wst0'], 'wst0')
        T.op('act', lambda e: e.activation(out=CB[:], in_=wst[0][:, 0:704], func=AF.Identity), ['wst0'], ['CB'])
        AGf = AG[:].bitcast(F32)
        stg0 = [(AGf[:, 1024 * i:1024 * i + 1024], 'stgA%d' % i) for i in range(4)] + [(wst[i][:], 'wst%d' % i) for i in range(2)]
        engs3 = ['act', 'dve', 'pool']
        n_ = 0
        for j in (2, 3, 0, 1):
            if j == 0:
                w_prefetch(3)
            for kc in range(8):
                stage_cast(xT_d[kc, :, 1024 * j:1024 * j + 1024], XTv[:, kc, 1024 * j:1024 * j + 1024], 'XT%d_%d' % (kc, j),
                           eng=engs3[n_ % 3], stg=stg0)
                n_ += 1
        T.op('dve', lambda e: e.tensor_scalar(out=cst2[:, D_HBIN:D_HBIN + 72], in0=cst[:, C_BIN:C_BIN + 72], scalar1=0.5,
                                              scalar2=None, op0=ALU.mult), ['cst'], ['cst2a'])
        T.op('dve', lambda e: e.tensor_scalar(out=cst2[:, D_HBR:D_HBR + 16], in0=cst[:, C_BR:C_BR + 16], scalar1=0.5,
                                              scalar2=None, op0=ALU.mult), ['cst'], ['cst2b'])
        T.op('dve', lambda e: e.tensor_scalar(out=cst2[:, D_FBU:D_FBU + 8], in0=cst[:, C_BIN + 40:C_BIN + 48],
                                              scalar1=col(C_FLAG), scalar2=None, op0=ALU.mult), ['cst'], ['cst2c'])
        T.op('act', lambda e: e.activation(out=cst2[:, D_TMP:D_TMP + 8], in_=cst[:, C_LAM:C_LAM + 8], func=AF.Exp, scale=-1.0),
             ['cst'], ['cst2t'])
        T.op('act', lambda e: e.activation(out=cst2[:, D_TMP + 8:D_TMP + 16], in_=cst2[:, D_TMP:D_TMP + 8], func=AF.Ln,
                                           bias=col(C_ONE), scale=1.0), ['cst2t', 'cst'], ['cst2u'])
        T.op('dve', lambda e: e.tensor_scalar(out=cst2[:, D_HSC:D_HSC + 8], in0=cst2[:, D_TMP + 8:D_TMP + 16], scalar1=-4.0,
                                              scalar2=None, op0=ALU.mult), ['cst2u'], ['cst2d'])
        T.op('dve', lambda e: e.tensor_scalar(out=cst2[:, D_SC1:D_SC1 + 8], in0=cst2[:, D_TMP + 8:D_TMP + 16], scalar1=-8.0,
                                              scalar2=None, op0=ALU.mult), ['cst2u'], ['cst2e'])
        CST = ['cst', 'cst2a', 'cst2b', 'cst2c', 'cst2d', 'cst2e', 'CB']

        cv = Carver()
        tabs = [cv.f32(2048) for _ in range(4)]
        accX = cv.f32(2048)
        accY = cv.f32(2048)
        QTP = [cv.bf16(2048) for _ in range(2)]
        KT = cv.bf16(4096)
        VK = cv.bf16(4096)
        kn = [cv.bf16(512) for _ in range(2)]
        t1 = [cv.f32(512) for _ in range(2)]
        t2 = cv.f32(512)
        NPS = 6
        PT = [[cv.bf16(256) for _ in range(NPS)] for _ in range(2)]
        sgz = cv.f32(512)
        sgt = cv.f32(512)
        SG = cv.bf16(2048)
        fin = KT.bitcast(F32)
        VTN = cv.bf16(4096)

        for i in range(4):
            T.dma('sp', (lambda i: lambda e: e.dma_start(out=tabs[i], in_=tab_d[i]))(i), [], ['tab%d' % i], 'tab%d' % i)
        TAB = ['tab0', 'tab1', 'tab2', 'tab3']
        T.op('dve', lambda e: e.memset(QTP[0][64:128, :], 0.0), [], ['QTz0'])
        T.op('dve', lambda e: e.memset(QTP[1][0:64, :], 0.0), [], ['QTz1'])

        rcount = [0]

        def rope_tile(pst, psres, n, bcol, ctab, stab, dsts, dstres, d):
            i = rcount[0] % 2
            rcount[0] += 1
            knb, t1b = kn[i], t1[i]
            T.op('act', lambda e: e.activation(out=knb[:, 0:n], in_=pst[:, 0:n], func=AF.Identity, bias=col(bcol), scale=1.0),
                 [psres] + CST, ['kn%d' % i])
            T.op('pool', lambda e: e.tensor_tensor(out=t1b[:, 0:n], in0=knb[:, 0:n], in1=ctab, op=ALU.mult),
                 ['kn%d' % i] + TAB, ['t1%d' % i])

            def post():
                T.op('pe', lambda e: e.matmul(pB[:, 0:n], lhsT=permm, rhs=knb[:, 0:n], start=True, stop=True),
                     ['kn%d' % i, 'CB'], ['pB'])
                T.op('dve', lambda e: e.tensor_tensor(out=t2[:, 0:n], in0=pB[:, 0:n], in1=stab, op=ALU.mult),
                     ['pB'] + TAB, ['t2'])
                for (d_, lo, hi) in dsts:
                    T.op('dve', lambda e, d_=d_, lo=lo, hi=hi: e.tensor_tensor(
                        out=d_, in0=srcview(t1b[lo:hi, 0:n], d), in1=srcview(t2[lo:hi, 0:n], d), op=ALU.add),
                        ['t1%d' % i, 't2'], [dstres])
            defer(post)

        xmain = lambda kc: XTv[:, kc, 2048:4096]
        xhalo = lambda kc: XTv[:, kc, 0:2048]

        for pair in range(4):
            c = 36 + pair
            w, wres = load_w(c)
            for j in range(4):
                pst, psres = next_pA()
                inproj(w, wres, pst, psres, lambda kc, j=j: xmain(kc)[:, 512 * j:512 * j + 512], 512, xcols=(2048 + 512 * j, 2560 + 512 * j))
                T.op('act', lambda e, pst=pst, c=c: e.activation(out=sgz[:], in_=pst[:, 0:512], func=AF.Identity,
                                                                 bias=col(C_BIN + c), scale=1.0), [psres] + CST, ['sgz'])
                T.op('act', lambda e, pst=pst, c=c: e.activation(out=sgt[:], in_=pst[:, 0:512], func=AF.Tanh,
                                                                 bias=col2(D_HBIN + c), scale=0.5), [psres] + CST, ['sgt'])
                T.op('dve', lambda e, j=j: e.scalar_tensor_tensor(out=SG[:, 512 * j:512 * j + 512], in0=sgt[:], scalar=1.0,
                                                                  in1=sgz[:], op0=ALU.add, op1=ALU.mult),
                     ['sgz', 'sgt'], ['SG'])

            for g, d in enumerate(DILS):
                nb = 16 // d
                nhalo = {1: 128, 4: 512, 16: 2048}[d]
                c = 12 + 4 * g + pair
                w, wres = load_w(c)
                KTm = KT[:, 2048:4096]
                for j in range(4):
                    pst, psres = next_pA()
                    inproj(w, wres, pst, psres, lambda kc, j=j: xmain(kc)[:, 512 * j:512 * j + 512], 512, xcols=(2048 + 512 * j, 2560 + 512 * j))
                    rope_tile(pst, psres, 512, C_BIN + c, tabs[0][:, 512 * j:512 * j + 512], tabs[1][:, 512 * j:512 * j + 512],
                              [(dstview(KTm, d, j), 0, 128)], 'KT', d)
                if d == 16:
                    for j in range(4):
                        pst, psres = next_pA()
                        inproj(w, wres, pst, psres, lambda kc, j=j: xhalo(kc)[:, 512 * j:512 * j + 512], 512, xcols=(512 * j, 512 * j + 512))
                        rope_tile(pst, psres, 512, C_BIN + c, tabs[2][:, 512 * j:512 * j + 512], tabs[3][:, 512 * j:512 * j + 512],
                                  [(dstview(KT[:, 0:2048], d, j), 0, 128)], 'KT', d)
                else:
                    h0 = 2048 - nhalo
                    pst, psres = next_pA()
                    inproj(w, wres, pst, psres, lambda kc, h0=h0: xhalo(kc)[:, h0:2048], nhalo)
                    hd = KT[:, 0:nhalo] if d == 1 else KT[:, 0:512].rearrange("p (r m) -> p r m", r=4)
                    rope_tile(pst, psres, nhalo, C_BIN + c, tabs[2][:, h0:2048], tabs[3][:, h0:2048], [(hd, 0, 128)], 'KT', d)
                c = 4 * g + pair
                w, wres = load_w(c)
                for j in range(4):
                    pst, psres = next_pA()
                    inproj(w, wres, pst, psres, lambda kc, j=j: xmain(kc)[:, 512 * j:512 * j + 512], 512, xcols=(2048 + 512 * j, 2560 + 512 * j))
                    rope_tile(pst, psres, 512, C_BIN + c, tabs[0][:, 512 * j:512 * j + 512], tabs[1][:, 512 * j:512 * j + 512],
                              [(dstview(QTP[0][0:64, :], d, j), 0, 64), (dstview(QTP[1][64:128, :], d, j), 64, 128)], 'QT', d)
                c = 24 + 4 * g + pair
                w, wres = load_w(c)

                def vproj(movfn, n, col0, c=c, w=w, wres=wres):
                    pst, psres = next_pA()
                    inproj(w, wres, pst, psres, movfn, n)
                    T.op('act', lambda e: e.activation(out=VTN[:, col0:col0 + n], in_=pst[:, 0:n], func=AF.Identity,
                                                       bias=col(C_BIN + c), scale=1.0), [psres] + CST, ['VTN'])

                def vtrans(srcs, tile0):
                    nq = len(srcs)
                    vi = vtcount[0] % 2
                    vtcount[0] += 1
                    bank, bres = (pT, 'pT') if vi == 0 else (pBb, 'pB')

                    def tr(e):
                        ins = None
                        for q, sv in enumerate(srcs):
                            ins = e.transpose(out=bank[:, 128 * q:128 * q + 128], in_=sv, identity=ident)
                        return ins
                    T.op('pe', tr, ['VTN', 'CB'], [bres])
                    if vi == 0:
                        T.op('dve', lambda e: e.tensor_copy(out=VK[:, 128 * tile0:128 * tile0 + 128 * nq], in_=bank[:, 0:128 * nq]),
                             [bres], ['VK%d' % vi])
                    else:
                        T.op('act', lambda e: e.activation(out=VK[:, 128 * tile0:128 * tile0 + 128 * nq], in_=bank[:, 0:128 * nq],
                                                           func=AF.Identity), [bres], ['VK%d' % vi])

                for j in range(4):
                    vproj(lambda kc, j=j: xmain(kc)[:, 512 * j:512 * j + 512], 512, 2048 + 512 * j)
                if d == 16:
                    for j in range(4):
                        vproj(lambda kc, j=j: xhalo(kc)[:, 512 * j:512 * j + 512], 512, 512 * j)
                else:
                    h0 = 2048 - nhalo
                    vproj(lambda kc, h0=h0: xhalo(kc)[:, h0:2048], nhalo, h0)
                flush()
                VM = VTN[:, 2048:4096]
                VH = VTN[:, 0:2048]
                gsub = lambda ap, r: (ap if d == 1 else ap.rearrange("p (m r) -> p r m", r=d)[:, r, :])
                for q4 in range(4):
                    srcs = []
                    for gb in range(4 * q4, 4 * q4 + 4):
                        s_, b_ = gb // nb, gb % nb
                        srcs.append(gsub(VM, s_)[:, 128 * b_:128 * b_ + 128])
                    vtrans(srcs, 16 + 4 * q4)
                if d == 16:
                    for q4 in range(4):
                        vtrans([gsub(VH, r_)[:, 0:128] for r_ in range(4 * q4, 4 * q4 + 4)], 4 * q4)
                elif d == 4:
                    vtrans([gsub(VH, r_)[:, 384:512] for r_ in range(4)], 0)
                else:
                    vtrans([VH[:, 1920:2048]], 0)

                flush()
                tiles = []
                for s in range(d):
                    for kt in range(-1, nb):
                        tiles.append((s, kt))
                tidx = {t: i for i, t in enumerate(tiles)}
                emitted = [0]

                def emit_S(i):
                    s, kt = tiles[i]
                    for hh in range(2):
                        if kt == -1:
                            ncols, q0, kcol, msk = 128, 128 * (s * nb), 128 * s, maskH
                        else:
                            ncols = 128 if kt == nb - 1 else 256
                            q0, kcol, msk = 128 * (s * nb + kt), 2048 + 128 * (s * nb + kt), maskM[:, 0:ncols]
                        si = (2 * i + hh) % 4
                        pst = SBANKS[si][0][:, 0:ncols]
                        psres = SBANKS[si][1]
                        ptb = PT[hh][i % NPS]
                        ptres = 'PT%d_%d' % (hh, i % NPS)
                        def sfn(e, pst=pst, hh=hh, kcol=kcol, q0=q0, ncols=ncols, msk=msk):
                            e.matmul(pst, lhsT=KT[:, kcol:kcol + 128], rhs=QTP[hh][:, q0:q0 + ncols], start=True, stop=False)
                            return e.matmul(pst, lhsT=ident, rhs=msk, start=False, stop=True)
                        T.op('pe', sfn, ['KT', 'QT', 'QTz0', 'QTz1', 'CB'], [psres])
                        T.op('act', lambda e, pst=pst, ptb=ptb, ncols=ncols: e.activation(
                            out=ptb[:, 0:ncols], in_=pst, func=AF.Exp, scale=0.125), [psres], [ptres])

                def emit_XY(s, b):
                    gb = s * nb + b
                    (bx, bxres), (by, byres) = XYSETS[(gb // 4) % 2]
                    c0 = 128 * (gb % 4)
                    ip = tidx[(s, b - 1)]
                    ic = tidx[(s, b)]
                    prev_cols = (0, 128) if b == 0 else (128, 256)
                    vprev = (s if b == 0 else 16 + gb - 1)
                    vcur = 16 + gb
                    reads = ['VK0', 'VK1', 'CB'] + ['PT%d_%d' % (hh, ip % NPS) for hh in range(2)] + \
                            ['PT%d_%d' % (hh, ic % NPS) for hh in range(2)]

                    def fn(e):
                        ins = None
                        for hh in range(2):
                            pp = PT[hh][ip % NPS][:, prev_cols[0]:prev_cols[1]]
                            pc = PT[hh][ic % NPS][:, 0:128]
                            for (dstp, lp, lc) in ((bx, VK[:, 128 * vprev + 64 * hh:128 * vprev + 64 * hh + 64],
                                                    VK[:, 128 * vcur + 64 * hh:128 * vcur + 64 * hh + 64]),
                                                   (by, ones64, ones64)):
                                o = dstp[64 * hh:64 * hh + 64, c0:c0 + 128]
                                e.matmul(o, lhsT=lp, rhs=pp, start=True, stop=False)
                                ins = e.matmul(o, lhsT=lc, rhs=pc, start=False, stop=True)
                        return ins
                    T.op('pe', fn, reads, [bxres, byres])
                    if gb % 4 == 3:
                        q4 = gb // 4
                        for (bank, res, acc, accres) in ((bx, bxres, accX, 'accX'), (by, byres, accY, 'accY')):
                            dst = gview(acc, d, q4)
                            src = like(bank[:, 0:512], dst)
                            if g == 0:
                                T.op('dve', lambda e, dst=dst, src=src: e.tensor_copy(out=dst, in_=src), [res], [accres])
                            else:
                                T.op('dve', lambda e, dst=dst, src=src: e.tensor_tensor(out=dst, in0=dst, in1=src, op=ALU.add),
                                     [res, accres], [accres])

                LOOK = 2
                for s in range(d):
                    for b in range(nb):
                        need = min(tidx[(s, b)] + LOOK, len(tiles) - 1)
                        while emitted[0] <= need:
                            emit_S(emitted[0])
                            emitted[0] += 1
                        emit_XY(s, b)

            T.op('act', lambda e: e.activation(out=fin, in_=accY, func=AF.Ln), ['accY'], ['KT'])
            T.op('act', lambda e: e.activation(out=fin, in_=fin, func=AF.Exp, scale=-1.0), ['KT'], ['KT'])
            T.op('dve', lambda e: e.tensor_tensor(out=fin, in0=fin, in1=accX, op=ALU.mult), ['KT', 'accX'], ['KT'])
            T.op('dve', lambda e, pair=pair: e.scalar_tensor_tensor(out=AGv[:, pair, :], in0=fin, scalar=0.5, in1=SG,
                                                                    op0=ALU.mult, op1=ALU.mult), ['KT', 'SG'] + ['stgA%d' % i_ for i_ in range(4)], ['AG'])

        T.barrier()

        cv = Carver()
        HG = cv.bf16(8 * NT)
        HGv = HG.rearrange("p (k t) -> p k t", k=8)
        UO = 8
        Us = [cv.bf16(2048 + 8) for _ in range(2)]
        As = [cv.f32(2048) for _ in range(2)]
        Ms = [cv.f32(2048) for _ in range(2)]
        ICs = [cv.f32(2048) for _ in range(2)]
        Ct = [cv.bf16(512) for _ in range(3)]
        thr = cv.f32(512)
        this_ = [cv.f32(512) for _ in range(2)]
        thi = this_[0]
        zb = [cv.bf16(512) for _ in range(2)]
        thb = [cv.bf16(512) for _ in range(2)]
        ticount = [0]
        Ht = [cv.f32(512) for _ in range(2)]
        BDR = cv.bf16(1024)
        BDI = cv.bf16(1024)
        DG = cv.bf16(4 * 128)
        hinit = cv.f32(2)
        stage_cast(bd_d[0], BDR, 'BDR')
        stage_cast(bd_d[1], BDI, 'BDI')
        WLA[0] = 1
        PB4 = [(pA[0], 'pA0'), (pA[1], 'pA1'), (pX, 'pX'), (pY, 'pY')]
        p4count = [0]

        def next_p4():
            i = p4count[0] % 4
            p4count[0] += 1
            return PB4[i]

        hcount = [0]
        ccount = [0]
        wB_ = {}

        def s12_steps(ch, hf):
            st_ = hf
            Uu, ures = Us[st_], 'U%d' % st_
            Au, ares = As[st_], 'A%d' % st_
            Mu, mres = Ms[st_], 'M%d' % st_
            ICu, icres = ICs[st_], 'IC%d' % st_
            c = 40 + ch
            if hf == 0:
                wB_['w'] = load_w(c)
                for j in range(4):
                    T.op('dve', lambda e, j=j: e.tensor_scalar(out=DG[:, 128 * j:128 * j + 128], in0=ident,
                                                               scalar1=col(C_CW + ch * 4 + j), scalar2=None, op0=ALU.mult),
                         CST, ['DG'])
                T.op('dve', lambda e: e.memset(Uu[:, 0:UO], 0.0), [], [ures])
            else:
                T.op('dve', lambda e: e.tensor_copy(out=Uu[:, UO - 3:UO], in_=Us[0][:, UO + 2045:UO + 2048]), ['U0'], [ures])
            w, wres = wB_['w']
            off = 2048 * hf
            convs = {}
            gts = {}

            def proj(k):
                if not (0 <= k < 4):
                    return
                (pb1, pres1) = next_p4()
                inproj(w, wres, pb1, pres1, lambda kc: XTv[:, kc, off + 512 * k:off + 512 * k + 512], 512,
                       xcols=(off + 512 * k, off + 512 * k + 512))
                if hf == 0:
                    T.op('dve', lambda e: e.tensor_scalar(out=Uu[:, UO + 512 * k:UO + 512 * k + 512], in0=pb1[:, 0:512],
                                                          scalar1=col(C_FLAG), scalar2=col2(D_FBU + ch), op0=ALU.mult,
                                                          op1=ALU.add), [pres1] + CST, [ures])
                else:
                    T.op('dve', lambda e: e.tensor_scalar(out=Uu[:, UO + 512 * k:UO + 512 * k + 512], in0=pb1[:, 0:512],
                                                          scalar1=col(C_BIN + c), scalar2=None, op0=ALU.add),
                         [pres1] + CST, [ures])

            def conv(j):
                if not (0 <= j < 4):
                    return
                (pb, pres) = next_p4()

                def convfn(e):
                    ins = None
                    for tp in range(4):
                        ins = e.matmul(pb[:, 0:512], lhsT=DG[:, 128 * tp:128 * tp + 128],
                                       rhs=Uu[:, UO + 512 * j - 3 + tp:UO + 512 * j - 3 + tp + 512], start=(tp == 0), stop=(tp == 3))
                    return ins
                T.op('pe', convfn, ['DG', ures], [pres])
                ci = ccount[0] % 3
                ccount[0] += 1
                cb_, cres = Ct[ci], 'Ct%d' % ci
                convs[j] = (cb_, cres)
                T.op('dve', lambda e: e.tensor_scalar(out=cb_[:], in0=pb[:, 0:512], scalar1=col(C_CB + ch), scalar2=None,
                                                      op0=ALU.add), [pres] + CST, [cres])

            def gates(j2):
                if not (0 <= j2 < 4):
                    return
                gb_, gres = convs[j2]
                sl = slice(512 * j2, 512 * j2 + 512)
                ti = ticount[0] % 2
                ticount[0] += 1
                tib, tires = this_[ti], 'thi%d' % ti
                gts[j2] = (tib, tires, gb_, gres)
                T.op('pe', lambda e: e.matmul(pB[:, 0:512], lhsT=BDR[:, 128 * ch:128 * ch + 128], rhs=gb_[:], start=True, stop=True),
                     [gres, 'BDR'], ['pB'])
                T.op('pe', lambda e: e.matmul(pS[0][:, 0:512], lhsT=BDI[:, 128 * ch:128 * ch + 128], rhs=gb_[:], start=True,
                                              stop=True), [gres, 'BDI'], ['bS0'])
                T.op('act', lambda e: e.activation(out=thr[:], in_=pB[:, 0:512], func=AF.Tanh, bias=col2(D_HBR + ch), scale=0.5),
                     ['pB'] + CST, ['thr'])
                T.op('act', lambda e: e.activation(out=tib[:], in_=pS[0][:, 0:512], func=AF.Tanh, bias=col2(D_HBI + ch), scale=0.5),
                     ['bS0'] + CST, [tires])
                T.op('act', lambda e: e.activation(out=Au[:, sl], in_=thr[:], func=AF.Exp, bias=col2(D_HSC + ch),
                                                   scale=col2(D_HSC + ch)), ['thr'] + CST, [ares])
                T.op('act', lambda e: e.activation(out=Mu[:, sl], in_=thr[:], func=AF.Exp, bias=col2(D_SC1 + ch),
                                                   scale=col2(D_SC1 + ch)), ['thr'] + CST, [mres])

            def ic(j2):
                if not (0 <= j2 < 4):
                    return
                tib, tires, gb_, gres = gts[j2]
                sl = slice(512 * j2, 512 * j2 + 512)
                T.op('dve', lambda e: e.scalar_tensor_tensor(out=ICu[:, sl], in0=tib[:], scalar=1.0, in1=gb_[:], op0=ALU.add,
                                                             op1=ALU.mult), [tires, gres], [icres])
            return proj, conv, gates, ic

        def s45_steps(ch, hf):
            st_ = hf
            Au, ares = As[st_], 'A%d' % st_
            Mu, mres = Ms[st_], 'M%d' % st_
            ICu, icres = ICs[st_], 'IC%d' % st_
            T.op('act', lambda e: e.activation(out=Mu[:], in_=Mu[:], func=AF.Sqrt, bias=col(C_ONE), scale=-1.0),
                 [mres] + CST, [mres])
            if hf == 1:
                c2 = 48 + ch
                w2, wres2 = load_w(c2)
            hts = {}

            def main(j):
                if not (0 <= j < 4):
                    return
                sl = slice(512 * j, 512 * j + 512)
                T.op('dve', lambda e: e.scalar_tensor_tensor(out=Mu[:, sl], in0=Mu[:, sl], scalar=0.5, in1=ICu[:, sl],
                                                             op0=ALU.mult, op1=ALU.mult), [mres, icres], [mres])
                hi = hcount[0] % 2
                hcount[0] += 1
                hb_, hres = Ht[hi], 'Ht%d' % hi
                pb_, pres_ = Ht[1 - hi], 'Ht%d' % (1 - hi)
                hts[j] = (hb_, hres)
                if hf == 0 and j == 0:
                    T.op('dve', lambda e: e.tensor_tensor_scan(out=hb_[:], data0=Au[:, sl], data1=Mu[:, sl], initial=0.0,
                                                               op0=ALU.mult, op1=ALU.add), [ares, mres], [hres])
                elif hf == 1 and j == 0:
                    T.op('dve', lambda e: e.tensor_scalar(out=hinit[:, 0:1], in0=pb_[:, 511:512], scalar1=col(C_FLAG), scalar2=None,
                                                          op0=ALU.mult), [pres_] + CST, ['hinit'])
                    T.op('dve', lambda e: e.tensor_tensor_scan(out=hb_[:], data0=Au[:, sl], data1=Mu[:, sl], initial=hinit[:, 0:1],
                                                               op0=ALU.mult, op1=ALU.add), [ares, mres, 'hinit'], [hres])
                else:
                    T.op('dve', lambda e: e.tensor_tensor_scan(out=hb_[:], data0=Au[:, sl], data1=Mu[:, sl], initial=pb_[:, 511:512],
                                                               op0=ALU.mult, op1=ALU.add), [ares, mres, pres_], [hres])
                if hf == 1:
                    (pb, pres) = next_p4()
                    inproj(w2, wres2, pb, pres, lambda kc: XTv[:, kc, 2048 + 512 * j:2048 + 512 * j + 512], 512,
                           xcols=(2048 + 512 * j, 2560 + 512 * j))
                    zi = j % 2
                    T.op('act', lambda e: e.activation(out=zb[zi][:], in_=pb[:, 0:512], func=AF.Identity, bias=col(C_BIN + c2),
                                                       scale=1.0), [pres] + CST, ['zb%d' % zi])
                    T.op('act', lambda e: e.activation(out=thb[zi][:], in_=pb[:, 0:512], func=AF.Tanh, bias=col2(D_HBIN + c2),
                                                       scale=0.5), [pres] + CST, ['thb%d' % zi])

            def tail(j):
                if not (0 <= j < 4) or hf == 0:
                    return
                sl = slice(512 * j, 512 * j + 512)
                zi = j % 2
                hb_, hres = hts[j]
                T.op('dve', lambda e: e.scalar_tensor_tensor(out=zb[zi][:], in0=thb[zi][:], scalar=1.0, in1=zb[zi][:], op0=ALU.add,
                                                             op1=ALU.mult), ['zb%d' % zi, 'thb%d' % zi], ['zb%d' % zi])
                T.op('dve', lambda e: e.scalar_tensor_tensor(out=HGv[:, ch, sl], in0=zb[zi][:], scalar=0.5, in1=hb_[:],
                                                             op0=ALU.mult, op1=ALU.mult), ['zb%d' % zi, hres], ['HG'])
            return main, tail

        units = [(ch, hf) for ch in range(8) for hf in range(2)]
        NFILL = 2
        NU = len(units)
        E12 = {}
        E45 = {}
        nop = lambda k: None
        for g_ in range(4 * NU + 12):
            def filler(e):
                ins = None
                for _q in range(NFILL):
                    ins = e.matmul(pS[1][:, 0:512], lhsT=ident, rhs=XTv[:, 0, 0:512], start=True, stop=True)
                return ins
            T.op('pe', filler, [], ['bS1'])
            for v in range(NU):
                if v in E45:
                    E45[v][1](g_ - (4 * v + 8))
            gt = g_ - 3
            if 0 <= gt < 4 * NU:
                E12[gt // 4][2](gt % 4)
            for v in range(NU):
                if g_ == 4 * v + 6:
                    E45[v] = s45_steps(*units[v])
                if v in E45:
                    E45[v][0](g_ - (4 * v + 7))
            if 0 <= gt < 4 * NU:
                E12[gt // 4][3](gt % 4)
            ct = g_ - 1
            if 0 <= ct < 4 * NU:
                E12[ct // 4][1](ct % 4)
            if g_ < 4 * NU:
                if g_ % 4 == 0:
                    E12[g_ // 4] = s12_steps(*units[g_ // 4])
                E12[g_ // 4][0](g_ % 4)

        T.barrier()

        cv = Carver()
        HG = cv.bf16(8 * NT)
        HGv = HG.rearrange("p (k t) -> p k t", k=8)
        MG = cv.bf16(8 * NT)
        MGv = MG.rearrange("p (k t) -> p k t", k=8)
        c1_mark = cv.off
        WAP = cv.bf16(4 * 1024)
        WLP = cv.bf16(8 * 1024)
        thA = cv.f32(512)
        thB = cv.f32(512)
        m1 = cv.f32(512)
        m2 = cv.f32(512)
        stgC = [(cv.f32(1024), 'stgC%d' % i) for i in range(3)] + [(wst[i][:], 'wst%d' % i) for i in range(2)]
        engsC = ['act', 'dve', 'pool']
        for kc in range(4):
            stage_cast(wap_d[kc], WAP[:, 1024 * kc:1024 * kc + 1024], 'WAP%d' % kc, eng=engsC[kc % 2], stg=stgC)
        for kc in range(8):
            stage_cast(wlp_d[kc], WLP[:, 1024 * kc:1024 * kc + 1024], 'WLP%d' % kc, eng=engsC[kc % 3], stg=stgC)

        WLA[0] = 1
        for dc in range(8):
            cA, cB_ = 56 + dc, 64 + dc
            wA, wAres = load_w(cA)
            wB, wBres = load_w(cB_)
            for j in range(4):
                sl = slice(512 * j, 512 * j + 512)
                pstA, psresA = next_pA()
                inproj(wA, wAres, pstA, psresA, lambda kc, j=j: xmain(kc)[:, 512 * j:512 * j + 512], 512)
                T.op('act', lambda e, pstA=pstA, cA=cA: e.activation(out=thA[:], in_=pstA[:, 0:512], func=AF.Tanh,
                                                                     bias=col2(D_HBIN + cA), scale=0.5), [psresA] + CST, ['thA'])
                pstB, psresB = next_pA()
                inproj(wB, wBres, pstB, psresB, lambda kc, j=j: xmain(kc)[:, 512 * j:512 * j + 512], 512)
                T.op('act', lambda e, pstB=pstB, cB_=cB_: e.activation(out=thB[:], in_=pstB[:, 0:512], func=AF.Tanh,
                                                                       bias=col2(D_HBIN + cB_), scale=0.5), [psresB] + CST, ['thB'])

                def yafn(e, dc=dc, sl=sl):
                    ins = None
                    for kc in range(4):
                        ins = e.matmul(pX[:, 0:512], lhsT=WAP[:, 1024 * kc + 128 * dc:1024 * kc + 128 * dc + 128],
                                       rhs=AGv[:, kc, sl], start=(kc == 0), stop=(kc == 3))
                    return ins
                T.op('pe', yafn, ['WAP%d' % i_ for i_ in range(4)] + ['AG'], ['pX'])

                def ybfn(e, dc=dc, sl=sl):
                    ins = None
                    for kc in range(8):
                        ins = e.matmul(pY[:, 0:512], lhsT=WLP[:, 1024 * kc + 128 * dc:1024 * kc + 128 * dc + 128],
                                       rhs=HGv[:, kc, sl], start=(kc == 0), stop=(kc == 7))
                    return ins
                T.op('pe', ybfn, ['WLP%d' % i_ for i_ in range(8)] + ['HG'], ['pY'])
                T.op('dve', lambda e: e.scalar_tensor_tensor(out=m1[:], in0=thA[:], scalar=1.0, in1=pX[:, 0:512], op0=ALU.add,
                                                             op1=ALU.mult), ['thA', 'pX'], ['m1'])
                T.op('dve', lambda e: e.scalar_tensor_tensor(out=m2[:], in0=thB[:], scalar=1.0, in1=pY[:, 0:512], op0=ALU.add,
                                                             op1=ALU.mult), ['thB', 'pY'], ['m2'])
                T.op('dve', lambda e, dc=dc, sl=sl: e.tensor_tensor(out=MGv[:, dc, sl], in0=m1[:], in1=m2[:], op=ALU.add),
                     ['m1', 'm2'], ['MG'])

        T.barrier()

        cv.off = c1_mark
        WO = cv.bf16(8 * 1024)
        stats = [cv.f32(16) for _ in range(2)]
        mv = [cv.f32(8) for _ in range(2)]
        for kc in range(8):
            stage_cast(wo_d[kc], WO[:, 1024 * kc:1024 * kc + 1024], 'WO')
        xh = lambda kc: XTv[:, kc, 0:2048].bitcast(F32)
        yo = [xh(3), xh(4)]
        bcb = [xh(5), xh(6), xh(7)]
        hgf = HG.bitcast(F32)
        ybufs = [hgf[:, 0:1024], hgf[:, 1024:2048]]
        for i in range(3):
            T.dma('sp', (lambda i: lambda e: e.dma_start(out=bcb[i], in_=bc_d[i]))(i), [], ['bc%d' % i], 'bc%d' % i)

        xr = [xh(0), xh(1), xh(2)]

        def dma_x(tt):
            xi = tt % 3
            T.dma('sp', lambda e: e.dma_start(out=xr[xi], in_=xres_d[tt]), [], ['xr%d' % xi], 'xr%d' % xi)

        def prep_x(tt):
            xi = tt % 3
            T.op('act', lambda e: e.activation(out=xr[xi], in_=xr[xi], func=AF.Identity, scale=ALPHA), ['xr%d' % xi], ['xr%d' % xi])
            T.op('pool', lambda e: e.tensor_tensor(out=xr[xi], in0=xr[xi], in1=bcb[0], op=ALU.add), ['xr%d' % xi, 'bc0'],
                 ['xr%d' % xi])

        def stage_b1(tt):
            i2 = tt % 2
            mv_, mvres = mv[i2], 'mv%d' % i2
            ybuf, yres = ybufs[i2], 'ybuf%d' % i2
            T.op('dve', lambda e: e.scalar_tensor_tensor(out=mv_[:, 4:5], in0=mv_[:, 0:1], scalar=-1.0, in1=mv_[:, 3:4],
                                                         op0=ALU.mult, op1=ALU.mult), [mvres, mvres + 'c'], [mvres + 'd'])
            T.op('act', lambda e: e.activation(out=yo[i2], in_=ybuf, func=AF.Identity, bias=mv_[:, 4:5], scale=mv_[:, 3:4]),
                 [yres, mvres + 'c', mvres + 'd'], ['yo%d' % i2])

        def stage_b2(tt):
            i2 = tt % 2
            T.op('dve', lambda e: e.tensor_tensor(out=yo[i2], in0=yo[i2], in1=bcb[1], op=ALU.mult), ['yo%d' % i2, 'bc1'],
                 ['yo%d' % i2])
            T.op('pool', lambda e: e.tensor_tensor(out=yo[i2], in0=yo[i2], in1=bcb[2], op=ALU.add), ['yo%d' % i2, 'bc2'],
                 ['yo%d' % i2])
            T.dma('sp', lambda e: e.dma_start(out=y_d[tt], in_=yo[i2]), ['yo%d' % i2], [], 'y%d' % i2)

        def stage_a(tt):
            def filler2(e):
                ins = None
                for _q in range(4):
                    ins = e.matmul(pS[1][:, 0:512], lhsT=ident, rhs=XTv[:, 0, 2048:2560], start=True, stop=True)
                return ins
            T.op('pe', filler2, [], ['bS1'])
            i2, xi = tt % 2, tt % 3
            ybuf, yres = ybufs[i2], 'ybuf%d' % i2
            st_, stres = stats[i2], 'stats%d' % i2
            mv_, mvres = mv[i2], 'mv%d' % i2
            for hf in range(2):
                pst, psres = next_pA()

                def ofn(e, pst=pst, hf=hf):
                    ins = None
                    for kc in range(8):
                        ins = e.matmul(pst[:, 0:512], lhsT=MGv[:, kc, 128 * tt:128 * tt + 128],
                                       rhs=WO[:, 1024 * kc + 512 * hf:1024 * kc + 512 * hf + 512], start=(kc == 0), stop=(kc == 7))
                    return ins
                T.op('pe', ofn, ['MG', 'WO'], [psres])
                T.op('dve', lambda e, pst=pst, hf=hf: e.scalar_tensor_tensor(
                    out=ybuf[:, 512 * hf:512 * hf + 512], in0=pst[:, 0:512], scalar=0.5, in1=xr[xi][:, 512 * hf:512 * hf + 512],
                    op0=ALU.mult, op1=ALU.add), [psres, 'xr%d' % xi], [yres])
                T.op('dve', lambda e, hf=hf: e.bn_stats(out=st_[:, 6 * hf:6 * hf + 6], in_=ybuf[:, 512 * hf:512 * hf + 512]),
                     [yres], [stres])
            T.op('dve', lambda e: e.bn_aggr(out=mv_[:, 0:2], in_=st_[:, 0:12]), [stres], [mvres])
            T.op('act', lambda e: e.activation(out=mv_[:, 2:3], in_=mv_[:, 1:2], func=AF.Ln, bias=col(C_EPS), scale=1.0),
                 [mvres] + CST, [mvres + 'b'])
            T.op('act', lambda e: e.activation(out=mv_[:, 3:4], in_=mv_[:, 2:3], func=AF.Exp, scale=-0.5),
                 [mvres + 'b'], [mvres + 'c'])

        dma_x(0)
        dma_x(1)
        dma_x(2)
        prep_x(0)
        prep_x(1)
        for tt in range(17):
            if tt >= 1:
                stage_b1(tt - 1)
            if tt < 16:
                stage_a(tt)
            if tt + 3 < 16:
                dma_x(tt + 3)
            if tt >= 1:
                stage_b2(tt - 1)
            if tt + 2 < 16:
                prep_x(tt + 2)

        T.wait_all('sp')
        T.emit()
    return nc


def _rope_tables(pos0):
    half = 32
    inv_freq = (10000.0 ** (-np.arange(half, dtype=np.float32) / half)).astype(np.float32)
    pos = np.arange(pos0, pos0 + 2048, dtype=np.float32)
    ang = pos[None, :] * inv_freq[:, None]
    cos = np.cos(ang).astype(np.float32)
    sin = np.sin(ang).astype(np.float32)
    p = np.arange(128)
    i = p % 32
    sign = np.where((p % 64) < 32, -1.0, 1.0).astype(np.float32)
    return cos[i], sin[i] * sign[:, None]


def _consts_bf(flag):
    cb = np.zeros((128, 1024), np.float32)
    cb[:, B_ID:B_ID + 128] = np.eye(128, dtype=np.float32)
    pm = np.zeros((128, 128), np.float32)
    for m in range(128):
        partner = m + 32 if (m % 64) < 32 else m - 32
        pm[partner, m] = 1.0
    cb[:, B_PM:B_PM + 128] = pm
    cb[:, B_ONES:B_ONES + 64] = 1.0
    k = np.arange(128)[:, None]
    q = np.arange(128)[None, :]
    NEG = -1000.0
    cur = np.where(k <= q, 0.0, NEG).astype(np.float32)
    prev = np.where(k >= q, 0.0, NEG).astype(np.float32)
    cb[:, B_M:B_M + 128] = cur
    cb[:, B_M + 128:B_M + 256] = prev
    cb[:, B_MH:B_MH + 128] = prev if flag > 0.5 else NEG
    return cb


_NC_CACHE = {}


def kernel(x, w_in, b_in, conv_w, conv_b, lru_wr, lru_br, lru_wi, lru_bi, lru_lambda,
           w_attn_proj, w_lru_proj, w_out, b_out, ln_gain, ln_bias):
    f = lambda a: np.ascontiguousarray(np.asarray(a, dtype=np.float32))
    x = f(x)
    W = f(w_in)[0]
    win_l = f(W.reshape(8, 128, 72, 128).transpose(2, 1, 0, 3).reshape(72, 128, 1024))
    colmaj = lambda v, n: f(np.asarray(v, np.float32).reshape(n, 128).T)
    bd = np.zeros((2, 8, 128, 128), np.float32)
    for t, wsrc in enumerate((f(lru_wr)[0], f(lru_wi)[0])):
        for ch in range(8):
            bd[t, ch, 0:64, 0:64] = wsrc[2 * ch]
            bd[t, ch, 64:128, 64:128] = wsrc[2 * ch + 1]
    bd_l = f(bd.transpose(0, 2, 1, 3).reshape(2, 128, 1024))
    wap_l = f(f(w_attn_proj)[0].reshape(4, 128, 1024))
    wlp_l = f(f(w_lru_proj)[0].reshape(8, 128, 1024))
    wo_l = f(f(w_out)[0].reshape(8, 128, 1024))
    bc = f(np.stack([np.broadcast_to(f(b_out)[0], (128, 1024)), np.broadcast_to(f(ln_gain)[0], (128, 1024)),
                     np.broadcast_to(f(ln_bias)[0], (128, 1024))]))
    cw = f(conv_w)[0]
    in_maps = []
    for c in range(8):
        b, half = c // 2, c % 2
        flag = float(half)
        xm = x[b, half * 2048:(half + 1) * 2048]
        xh = x[b, 0:2048] if half == 1 else np.zeros((2048, 1024), np.float32)
        xT = np.concatenate([xh.T, xm.T], axis=1)
        cst = np.zeros((128, NCST), np.float32)
        cst[:, C_BIN:C_BIN + 72] = colmaj(f(b_in)[0], 72)
        for ch in range(8):
            for j in range(4):
                cst[:, C_CW + ch * 4 + j] = cw[j, ch * 128:(ch + 1) * 128]
        cst[:, C_CB:C_CB + 8] = colmaj(f(conv_b)[0], 8)
        cst[:, C_BR:C_BR + 8] = colmaj(f(lru_br)[0], 8)
        cst[:, C_BI:C_BI + 8] = colmaj(f(lru_bi)[0], 8)
        cst[:, C_LAM:C_LAM + 8] = colmaj(f(lru_lambda)[0], 8)
        cst[:, C_FLAG] = flag
        cst[:, C_ONE] = 1.0
        cst[:, C_MHALF] = -0.5
        cst[:, C_EPS] = LN_EPS
        cm, sm = _rope_tables(half * 2048)
        chh, shh = _rope_tables(0)
        in_maps.append({
            "xT": f(xT.reshape(8, 128, 4096)),
            "xres": f(xm.reshape(16, 128, 1024)),
            "win": win_l,
            "cst": cst,
            "cb": _consts_bf(flag),
            "tab": f(np.stack([cm, sm, chh, shh])),
            "bd": bd_l,
            "wap": wap_l,
            "wlp": wlp_l,
            "wo": wo_l,
            "bc": bc,
        })
    if "nc" not in _NC_CACHE:
        _NC_CACHE["nc"] = build()
    res = run_bass_kernel_spmd(_NC_CACHE["nc"], in_maps, core_ids=list(range(8)))
    out = np.zeros((4, 4096, 1024), np.float32)
    for c in range(8):
        b, half = c // 2, c % 2
        out[b, half * 2048:(half + 1) * 2048] = np.asarray(res.results[c]["y"]).reshape(2048, 1024)
    return out
```

```python
import contextlib
import numpy as np
import concourse.bass as bass
import concourse.mybir as mybir
from concourse.bass_utils import run_bass_kernel_spmd

F32 = mybir.dt.float32
BF16 = mybir.dt.bfloat16
AF = mybir.ActivationFunctionType
ALU = mybir.AluOpType

NT = 2048
ALPHA = 2.0 ** 0.25
LN_EPS = 1e-5
DILS = (1, 4, 16)

C_BIN, C_CW, C_CB, C_BR, C_BI, C_LAM, C_FLAG, C_ONE, C_MHALF, C_EPS, NCST = 0, 72, 104, 112, 120, 128, 136, 137, 138, 140, 144
D_HBIN, D_HBR, D_HBI, D_HSC, D_SC1, D_FBU, D_TMP, NCST2 = 0, 72, 80, 88, 96, 104, 112, 128
B_ID, B_PM, B_ONES, B_M, B_MH = 0, 128, 256, 320, 576


class Tracker:
    def __init__(self, nc, stack):
        self.nc = nc
        self.stack = stack
        self.engs = {'pe': nc.tensor, 'act': nc.scalar, 'dve': nc.vector, 'pool': nc.gpsimd, 'sp': nc.sync}
        self.streams = {e: [] for e in self.engs}
        self.sems = {}
        self.semval = {}
        self.waited = {e: {} for e in self.engs}
        self.last_w = {}
        self.last_r = {}

    def sem(self, name):
        if name not in self.sems:
            self.sems[name] = self.stack.enter_context(self.nc.semaphore(name))
            self.semval[name] = 0
        return name

    def _need(self, eng, deps):
        for (s, v) in deps:
            if self.waited[eng].get(s, 0) < v:
                self.waited[eng][s] = v
                self.streams[eng].append(('wait', s, v))

    def op(self, eng, fn, reads=(), writes=(), sem=None, inc=1):
        deps = []
        for r in reads:
            if r in self.last_w:
                deps.append(self.last_w[r])
        for w in writes:
            if w in self.last_w:
                deps.append(self.last_w[w])
            for s, v in self.last_r.get(w, {}).items():
                deps.append((s, v))
        own = 'S_' + eng
        if eng == 'pe':
            deps = [d for d in deps if d[0] != own]
        self._need(eng, deps)
        s = self.sem(sem if sem is not None else own)
        self.semval[s] += inc
        v = self.semval[s]
        self.streams[eng].append(('op', fn, s, inc))
        for w in writes:
            self.last_w[w] = (s, v)
            self.last_r[w] = {}
        for r in reads:
            d = self.last_r.setdefault(r, {})
            d[s] = max(d.get(s, 0), v)

    def dma(self, eng, fn, reads, writes, slot):
        self.op(eng, fn, reads, writes, sem='D_' + slot, inc=16)

    def wait_all(self, eng):
        self._need(eng, [(s, v) for s, v in self.semval.items() if v > 0])

    def barrier(self):
        for e in self.engs:
            self.wait_all(e)

    def emit(self):
        nc = self.nc
        with nc.Block() as block:
            def run(engname, eng):
                for item in self.streams[engname]:
                    if item[0] == 'wait':
                        eng.wait_ge(self.sems[item[1]], item[2])
                    else:
                        _, fn, s, inc = item
                        ins = fn(eng)
                        ins.then_inc(self.sems[s], inc)

            @block.tensor
            def _(e):
                run('pe', e)

            @block.scalar
            def _(e):
                run('act', e)

            @block.vector
            def _(e):
                run('dve', e)

            @block.gpsimd
            def _(e):
                run('pool', e)

            @block.sync
            def _(e):
                run('sp', e)


def gview(ap, d, j):
    if d == 1:
        return ap[:, 512 * j:512 * j + 512]
    if d == 4:
        return ap.rearrange("p (m r) -> p r m", r=4)[:, j, :]
    return ap.rearrange("p (m r) -> p r m", r=16)[:, 4 * j:4 * j + 4, :]


def hview(ap, d):
    if d == 1:
        return ap[:, 1920:2048]
    return ap.rearrange("p (m r) -> p r m", r=4)[:, :, 384:512]


def dstview(buf, d, j):
    if d == 1:
        return buf[:, 512 * j:512 * j + 512]
    if d == 4:
        return buf.rearrange("p (r m) -> p r m", r=4)[:, :, 128 * j:128 * j + 128]
    return buf.rearrange("p (r m) -> p r m", r=16)[:, :, 32 * j:32 * j + 32]


def srcview(flat, d):
    if d == 1:
        return flat
    return flat.rearrange("p (m r) -> p r m", r=d)


def like(flat, ref):
    sh = ref.shape
    if len(sh) == 2:
        return flat
    return flat.rearrange("p (a b) -> p a b", a=sh[1])


def build():
    nc = bass.Bass("TRN2", target_bir_lowering=False)
    dt = lambda name, shape, kind="ExternalInput": nc.dram_tensor(name, shape, F32, kind=kind).ap()
    sbn = lambda n: "s_" + n
    xT_d = dt("xT", [8, 128, 4096])
    xres_d = dt("xres", [16, 128, 1024])
    win_d = dt("win", [72, 128, 1024])
    cst_d = dt("cst", [128, NCST])
    cb_d = dt("cb", [128, 1024])
    tab_d = dt("tab", [4, 128, 2048])
    bd_d = dt("bd", [2, 128, 1024])
    wap_d = dt("wap", [4, 128, 1024])
    wlp_d = dt("wlp", [8, 128, 1024])
    wo_d = dt("wo", [8, 128, 1024])
    bc_d = dt("bc", [3, 128, 1024])
    y_d = dt("y", [16, 128, 1024], kind="ExternalOutput")

    with contextlib.ExitStack() as st:
        T = Tracker(nc, st)
        sb = lambda name, shape, dtp: st.enter_context(nc.sbuf_tensor("s_" + name, shape, dtp))
        ps = lambda name, shape, dtp: st.enter_context(nc.psum_tensor(name, shape, dtp))

        XT = sb("XT", [128, 8 * 4096], BF16)
        XTv = XT[:].rearrange("p (k t) -> p k t", k=8)
        AG = sb("AG", [128, 4 * NT], BF16)
        AGv = AG[:].rearrange("p (k t) -> p k t", k=4)
        cst = sb("cst", [128, NCST], F32)
        cst2 = sb("cst2", [128, NCST2], F32)
        CB = sb("CB", [128, 704], BF16)
        wst = [sb("wst%d" % i, [128, 1024], F32) for i in range(2)]
        wbf = [sb("wbf%d" % i, [128, 1024], BF16) for i in range(3)]
        ARENA_F = 28288
        arena = sb("arena", [128, ARENA_F], F32)

        pA = [ps("pA%d" % i, [128, 512], F32) for i in range(2)]
        pB = ps("pB", [128, 512], F32)
        pS = [ps("pS%d" % i, [128, 512], F32) for i in range(2)]
        pX = ps("pX", [128, 512], F32)
        pY = ps("pY", [128, 512], F32)
        pT = ps("pT", [128, 1024], BF16)
        pT32 = pT[:].bitcast(F32)
        pBb = pB[:].bitcast(BF16)
        vtcount = [0]
        SBANKS = [(pS[0], 'bS0'), (pS[1], 'bS1'), (pA[0], 'pA0'), (pA[1], 'pA1')]
        XYSETS = [((pX, 'pX'), (pY, 'pY')), ((pB, 'pB'), (pT32, 'pT'))]

        class Carver:
            def __init__(self):
                self.off = 0

            def f32(self, n):
                a = arena[:, self.off:self.off + n]
                self.off += n
                assert self.off <= ARENA_F, self.off
                return a

            def bf16(self, n):
                assert n % 2 == 0
                return self.f32(n // 2).bitcast(BF16)

        col = lambda c: cst[:, c:c + 1]
        col2 = lambda c: cst2[:, c:c + 1]
        ident = CB[:, B_ID:B_ID + 128]
        permm = CB[:, B_PM:B_PM + 128]
        ones64 = CB[:, B_ONES:B_ONES + 64]
        maskM = CB[:, B_M:B_M + 256]
        maskH = CB[:, B_MH:B_MH + 128]

        wcount = [0]
        CASTE = 'act'

        def do_cast(eng, dst_ap, src_ap):
            if eng == 'act':
                return lambda e: e.activation(out=dst_ap, in_=src_ap, func=AF.Identity)
            return lambda e: e.tensor_copy(out=dst_ap, in_=src_ap)

        def stage_cast(dram_ap, dst_ap, dst_res, eng=None, stg=None):
            eng = eng or CASTE
            i = wcount[0]
            wcount[0] += 1
            if stg is None:
                s_ = i % 2
                sap, sres = wst[s_][:], 'wst%d' % s_
            else:
                sap, sres = stg[i % len(stg)]
            T.dma('sp', lambda e: e.dma_start(out=sap, in_=dram_ap), [], [sres], sres)
            T.op(eng, do_cast(eng, dst_ap, sap), [sres], [dst_res])

        WORDER = []
        for pair_ in range(4):
            WORDER.append(36 + pair_)
            for g_ in range(3):
                WORDER += [12 + 4 * g_ + pair_, 4 * g_ + pair_, 24 + 4 * g_ + pair_]
        WORDER.append(40)
        for ch_ in range(1, 8):
            WORDER += [40 + ch_, 48 + ch_ - 1]
        WORDER.append(55)
        for dc_ in range(8):
            WORDER += [56 + dc_, 64 + dc_]
        wissued = [0]
        wbcount = [0]
        WLA = [2]

        def w_prefetch(upto):
            while wissued[0] < min(upto, len(WORDER)):
                kk = wissued[0]
                wissued[0] += 1
                stage_cast(win_d[WORDER[kk]], wbf[kk % 3][:], 'wbf%d' % (kk % 3))

        def load_w(c):
            k = wbcount[0]
            wbcount[0] += 1
            assert WORDER[k] == c, (k, c, WORDER[k])
            while wissued[0] <= min(k + WLA[0], len(WORDER) - 1):
                kk = wissued[0]
                wissued[0] += 1
                stage_cast(win_d[WORDER[kk]], wbf[kk % 3][:], 'wbf%d' % (kk % 3))
            return wbf[k % 3], 'wbf%d' % (k % 3)

        PEND = []

        def defer(f):
            PEND.append(f)

        def flush():
            run_ = PEND[:]
            del PEND[:]
            for f_ in run_:
                f_()

        pacount = [0]

        def next_pA():
            i = pacount[0] % 2
            pacount[0] += 1
            return pA[i], 'pA%d' % i

        XTALL = ['XT%d_%d' % (kc_, j_) for kc_ in range(8) for j_ in range(4)]

        def inproj(w, wres, pst, psres, movfn, n, extra=(), xcols=None):
            if xcols is None:
                xres_ = XTALL
            else:
                xres_ = ['XT%d_%d' % (kc_, j_) for kc_ in range(8) for j_ in range(xcols[0] // 1024, (xcols[1] - 1) // 1024 + 1)]
            def fn(e):
                ins = None
                for kc in range(8):
                    ins = e.matmul(pst[:, 0:n], lhsT=w[:, kc * 128:(kc + 1) * 128], rhs=movfn(kc),
                                   start=(kc == 0), stop=(kc == 7))
                return ins
            T.op('pe', fn, [wres] + xres_ + list(extra), [psres])
            flush()

        T.dma('sp', lambda e: e.dma_start(out=cst[:], in_=cst_d[:, :]), [], ['cst'], 'cst')
        T.dma('sp', lambda e: e.dma_start(out=wst[0][:, 0:704], in_=cb_d[:, 0:704]), [], ['wst0'], 'wst0')
        T.op('act', lambda e: e.activation(out=CB[:], in_=wst[0][:, 0:704], func=AF.Identity), ['wst0'], ['CB'])
        AGf = AG[:].bitcast(F32)
        stg0 = [(AGf[:, 1024 * i:1024 * i + 1024], 'stgA%d' % i) for i in range(4)] + [(wst[i][:], 'wst%d' % i) for i in range(2)]
        engs3 = ['act', 'dve', 'pool']
        n_ = 0
        for j in (2, 3, 0, 1):
            if j == 0:
                w_prefetch(3)
            for kc in range(8):
                stage_cast(xT_d[kc, :, 1024 * j:1024 * j + 1024], XTv[:, kc, 1024 * j:1024 * j + 1024], 'XT%d_%d' % (kc, j),
                           eng=engs3[n_ % 3], stg=stg0)
                n_ += 1
        T.op('dve', lambda e: e.tensor_scalar(out=cst2[:, D_HBIN:D_HBIN + 72], in0=cst[:, C_BIN:C_BIN + 72], scalar1=0.5,
                                              scalar2=None, op0=ALU.mult), ['cst'], ['cst2a'])
        T.op('dve', lambda e: e.tensor_scalar(out=cst2[:, D_HBR:D_HBR + 16], in0=cst[:, C_BR:C_BR + 16], scalar1=0.5,
                                              scalar2=None, op0=ALU.mult), ['cst'], ['cst2b'])
        T.op('dve', lambda e: e.tensor_scalar(out=cst2[:, D_FBU:D_FBU + 8], in0=cst[:, C_BIN + 40:C_BIN + 48],
                                              scalar1=col(C_FLAG), scalar2=None, op0=ALU.mult), ['cst'], ['cst2c'])
        T.op('act', lambda e: e.activation(out=cst2[:, D_TMP:D_TMP + 8], in_=cst[:, C_LAM:C_LAM + 8], func=AF.Exp, scale=-1.0),
             ['cst'], ['cst2t'])
        T.op('act', lambda e: e.activation(out=cst2[:, D_TMP + 8:D_TMP + 16], in_=cst2[:, D_TMP:D_TMP + 8], func=AF.Ln,
                                           bias=col(C_ONE), scale=1.0), ['cst2t', 'cst'], ['cst2u'])
        T.op('dve', lambda e: e.tensor_scalar(out=cst2[:, D_HSC:D_HSC + 8], in0=cst2[:, D_TMP + 8:D_TMP + 16], scalar1=-4.0,
                                              scalar2=None, op0=ALU.mult), ['cst2u'], ['cst2d'])
        T.op('dve', lambda e: e.tensor_scalar(out=cst2[:, D_SC1:D_SC1 + 8], in0=cst2[:, D_TMP + 8:D_TMP + 16], scalar1=-8.0,
                                              scalar2=None, op0=ALU.mult), ['cst2u'], ['cst2e'])
        CST = ['cst', 'cst2a', 'cst2b', 'cst2c', 'cst2d', 'cst2e', 'CB']

        cv = Carver()
        tabs = [cv.f32(2048) for _ in range(4)]
        accX = cv.f32(2048)
        accY = cv.f32(2048)
        QTP = [cv.bf16(2048) for _ in range(2)]
        KT = cv.bf16(4096)
        VK = cv.bf16(4096)
        kn = [cv.bf16(512) for _ in range(2)]
        t1 = [cv.f32(512) for _ in range(2)]
        t2 = cv.f32(512)
        NPS = 6
        PT = [[cv.bf16(256) for _ in range(NPS)] for _ in range(2)]
        sgz = cv.f32(512)
        sgt = cv.f32(512)
        SG = cv.bf16(2048)
        fin = KT.bitcast(F32)
        VTN = cv.bf16(4096)

        for i in range(4):
            T.dma('sp', (lambda i: lambda e: e.dma_start(out=tabs[i], in_=tab_d[i]))(i), [], ['tab%d' % i], 'tab%d' % i)
        TAB = ['tab0', 'tab1', 'tab2', 'tab3']
        T.op('dve', lambda e: e.memset(QTP[0][64:128, :], 0.0), [], ['QTz0'])
        T.op('dve', lambda e: e.memset(QTP[1][0:64, :], 0.0), [], ['QTz1'])

        rcount = [0]

        def rope_tile(pst, psres, n, bcol, ctab, stab, dsts, dstres, d):
            i = rcount[0] % 2
            rcount[0] += 1
            knb, t1b = kn[i], t1[i]
            T.op('act', lambda e: e.activation(out=knb[:, 0:n], in_=pst[:, 0:n], func=AF.Identity, bias=col(bcol), scale=1.0),
                 [psres] + CST, ['kn%d' % i])
            T.op('pool', lambda e: e.tensor_tensor(out=t1b[:, 0:n], in0=knb[:, 0:n], in1=ctab, op=ALU.mult),
                 ['kn%d' % i] + TAB, ['t1%d' % i])

            def post():
                T.op('pe', lambda e: e.matmul(pB[:, 0:n], lhsT=permm, rhs=knb[:, 0:n], start=True, stop=True),
                     ['kn%d' % i, 'CB'], ['pB'])
                T.op('dve', lambda e: e.tensor_tensor(out=t2[:, 0:n], in0=pB[:, 0:n], in1=stab, op=ALU.mult),
                     ['pB'] + TAB, ['t2'])
                for (d_, lo, hi) in dsts:
                    T.op('dve', lambda e, d_=d_, lo=lo, hi=hi: e.tensor_tensor(
                        out=d_, in0=srcview(t1b[lo:hi, 0:n], d), in1=srcview(t2[lo:hi, 0:n], d), op=ALU.add),
                        ['t1%d' % i, 't2'], [dstres])
            defer(post)

        xmain = lambda kc: XTv[:, kc, 2048:4096]
        xhalo = lambda kc: XTv[:, kc, 0:2048]

        for pair in range(4):
            c = 36 + pair
            w, wres = load_w(c)
            for j in range(4):
                pst, psres = next_pA()
                inproj(w, wres, pst, psres, lambda kc, j=j: xmain(kc)[:, 512 * j:512 * j + 512], 512, xcols=(2048 + 512 * j, 2560 + 512 * j))
                T.op('act', lambda e, pst=pst, c=c: e.activation(out=sgz[:], in_=pst[:, 0:512], func=AF.Identity,
                                                                 bias=col(C_BIN + c), scale=1.0), [psres] + CST, ['sgz'])
                T.op('act', lambda e, pst=pst, c=c: e.activation(out=sgt[:], in_=pst[:, 0:512], func=AF.Tanh,
                                                                 bias=col2(D_HBIN + c), scale=0.5), [psres] + CST, ['sgt'])
                T.op('dve', lambda e, j=j: e.scalar_tensor_tensor(out=SG[:, 512 * j:512 * j + 512], in0=sgt[:], scalar=1.0,
                                                                  in1=sgz[:], op0=ALU.add, op1=ALU.mult),
                     ['sgz', 'sgt'], ['SG'])

            for g, d in enumerate(DILS):
                nb = 16 // d
                nhalo = {1: 128, 4: 512, 16: 2048}[d]
                c = 12 + 4 * g + pair
                w, wres = load_w(c)
                KTm = KT[:, 2048:4096]
                for j in range(4):
                    pst, psres = next_pA()
                    inproj(w, wres, pst, psres, lambda kc, j=j: xmain(kc)[:, 512 * j:512 * j + 512], 512, xcols=(2048 + 512 * j, 2560 + 512 * j))
                    rope_tile(pst, psres, 512, C_BIN + c, tabs[0][:, 512 * j:512 * j + 512], tabs[1][:, 512 * j:512 * j + 512],
                              [(dstview(KTm, d, j), 0, 128)], 'KT', d)
                if d == 16:
                    for j in range(4):
                        pst, psres = next_pA()
                        inproj(w, wres, pst, psres, lambda kc, j=j: xhalo(kc)[:, 512 * j:512 * j + 512], 512, xcols=(512 * j, 512 * j + 512))
                        rope_tile(pst, psres, 512, C_BIN + c, tabs[2][:, 512 * j:512 * j + 512], tabs[3][:, 512 * j:512 * j + 512],
                                  [(dstview(KT[:, 0:2048], d, j), 0, 128)], 'KT', d)
                else:
                    h0 = 2048 - nhalo
                    pst, psres = next_pA()
                    inproj(w, wres, pst, psres, lambda kc, h0=h0: xhalo(kc)[:, h0:2048], nhalo)
                    hd = KT[:, 0:nhalo] if d == 1 else KT[:, 0:512].rearrange("p (r m) -> p r m", r=4)
                    rope_tile(pst, psres, nhalo, C_BIN + c, tabs[2][:, h0:2048], tabs[3][:, h0:2048], [(hd, 0, 128)], 'KT', d)
                c = 4 * g + pair
                w, wres = load_w(c)
                for j in range(4):
                    pst, psres = next_pA()
                    inproj(w, wres, pst, psres, lambda kc, j=j: xmain(kc)[:, 512 * j:512 * j + 512], 512, xcols=(2048 + 512 * j, 2560 + 512 * j))
                    rope_tile(pst, psres, 512, C_BIN + c, tabs[0][:, 512 * j:512 * j + 512], tabs[1][:, 512 * j:512 * j + 512],
                              [(dstview(QTP[0][0:64, :], d, j), 0, 64), (dstview(QTP[1][64:128, :], d, j), 64, 128)], 'QT', d)
                c = 24 + 4 * g + pair
                w, wres = load_w(c)

                def vproj(movfn, n, col0, c=c, w=w, wres=wres):
                    pst, psres = next_pA()
                    inproj(w, wres, pst, psres, movfn, n)
                    T.op('act', lambda e: e.activation(out=VTN[:, col0:col0 + n], in_=pst[:, 0:n], func=AF.Identity,
                                                       bias=col(C_BIN + c), scale=1.0), [psres] + CST, ['VTN'])

                def vtrans(srcs, tile0):
                    nq = len(srcs)
                    vi = vtcount[0] % 2
                    vtcount[0] += 1
                    bank, bres = (pT, 'pT') if vi == 0 else (pBb, 'pB')

                    def tr(e):
                        ins = None
                        for q, sv in enumerate(srcs):
                            ins = e.transpose(out=bank[:, 128 * q:128 * q + 128], in_=sv, identity=ident)
                        return ins
                    T.op('pe', tr, ['VTN', 'CB'], [bres])
                    if vi == 0:
                        T.op('dve', lambda e: e.tensor_copy(out=VK[:, 128 * tile0:128 * tile0 + 128 * nq], in_=bank[:, 0:128 * nq]),
                             [bres], ['VK%d' % vi])
                    else:
                        T.op('act', lambda e: e.activation(out=VK[:, 128 * tile0:128 * tile0 + 128 * nq], in_=bank[:, 0:128 * nq],
                                                           func=AF.Identity), [bres], ['VK%d' % vi])

                for j in range(4):
                    vproj(lambda kc, j=j: xmain(kc)[:, 512 * j:512 * j + 512], 512, 2048 + 512 * j)
                if d == 16:
                    for j in range(4):
                        vproj(lambda kc, j=j: xhalo(kc)[:, 512 * j:512 * j + 512], 512, 512 * j)
                else:
                    h0 = 2048 - nhalo
                    vproj(lambda kc, h0=h0: xhalo(kc)[:, h0:2048], nhalo, h0)
                flush()
                VM = VTN[:, 2048:4096]
                VH = VTN[:, 0:2048]
                gsub = lambda ap, r: (ap if d == 1 else ap.rearrange("p (m r) -> p r m", r=d)[:, r, :])
                for q4 in range(4):
                    srcs = []
                    for gb in range(4 * q4, 4 * q4 + 4):
                        s_, b_ = gb // nb, gb % nb
                        srcs.append(gsub(VM, s_)[:, 128 * b_:128 * b_ + 128])
                    vtrans(srcs, 16 + 4 * q4)
                if d == 16:
                    for q4 in range(4):
                        vtrans([gsub(VH, r_)[:, 0:128] for r_ in range(4 * q4, 4 * q4 + 4)], 4 * q4)
                elif d == 4:
                    vtrans([gsub(VH, r_)[:, 384:512] for r_ in range(4)], 0)
                else:
                    vtrans([VH[:, 1920:2048]], 0)

                flush()
                tiles = []
                for s in range(d):
                    for kt in range(-1, nb):
                        tiles.append((s, kt))
                tidx = {t: i for i, t in enumerate(tiles)}
                emitted = [0]

                def emit_S(i):
                    s, kt = tiles[i]
                    for hh in range(2):
                        if kt == -1:
                            ncols, q0, kcol, msk = 128, 128 * (s * nb), 128 * s, maskH
                        else:
                            ncols = 128 if kt == nb - 1 else 256
                            q0, kcol, msk = 128 * (s * nb + kt), 2048 + 128 * (s * nb + kt), maskM[:, 0:ncols]
                        si = (2 * i + hh) % 4
                        pst = SBANKS[si][0][:, 0:ncols]
                        psres = SBANKS[si][1]
                        ptb = PT[hh][i % NPS]
                        ptres = 'PT%d_%d' % (hh, i % NPS)
                        def sfn(e, pst=pst, hh=hh, kcol=kcol, q0=q0, ncols=ncols, msk=msk):
                            e.matmul(pst, lhsT=KT[:, kcol:kcol + 128], rhs=QTP[hh][:, q0:q0 + ncols], start=True, stop=False)
                            return e.matmul(pst, lhsT=ident, rhs=msk, start=False, stop=True)
                        T.op('pe', sfn, ['KT', 'QT', 'QTz0', 'QTz1', 'CB'], [psres])
                        T.op('act', lambda e, pst=pst, ptb=ptb, ncols=ncols: e.activation(
                            out=ptb[:, 0:ncols], in_=pst, func=AF.Exp, scale=0.125), [psres], [ptres])

                def emit_XY(s, b):
                    gb = s * nb + b
                    (bx, bxres), (by, byres) = XYSETS[(gb // 4) % 2]
                    c0 = 128 * (gb % 4)
                    ip = tidx[(s, b - 1)]
                    ic = tidx[(s, b)]
                    prev_cols = (0, 128) if b == 0 else (128, 256)
                    vprev = (s if b == 0 else 16 + gb - 1)
                    vcur = 16 + gb
                    reads = ['VK0', 'VK1', 'CB'] + ['PT%d_%d' % (hh, ip % NPS) for hh in range(2)] + \
                            ['PT%d_%d' % (hh, ic % NPS) for hh in range(2)]

                    def fn(e):
                        ins = None
                        for hh in range(2):
                            pp = PT[hh][ip % NPS][:, prev_cols[0]:prev_cols[1]]
                            pc = PT[hh][ic % NPS][:, 0:128]
                            for (dstp, lp, lc) in ((bx, VK[:, 128 * vprev + 64 * hh:128 * vprev + 64 * hh + 64],
                                                    VK[:, 128 * vcur + 64 * hh:128 * vcur + 64 * hh + 64]),
                                                   (by, ones64, ones64)):
                                o = dstp[64 * hh:64 * hh + 64, c0:c0 + 128]
                                e.matmul(o, lhsT=lp, rhs=pp, start=True, stop=False)
                                ins = e.matmul(o, lhsT=lc, rhs=pc, start=False, stop=True)
                        return ins
                    T.op('pe', fn, reads, [bxres, byres])
                    if gb % 4 == 3:
                        q4 = gb // 4
                        for (bank, res, acc, accres) in ((bx, bxres, accX, 'accX'), (by, byres, accY, 'accY')):
                            dst = gview(acc, d, q4)
                            src = like(bank[:, 0:512], dst)
                            if g == 0:
                                T.op('dve', lambda e, dst=dst, src=src: e.tensor_copy(out=dst, in_=src), [res], [accres])
                            else:
                                T.op('dve', lambda e, dst=dst, src=src: e.tensor_tensor(out=dst, in0=dst, in1=src, op=ALU.add),
                                     [res, accres], [accres])

                LOOK = 2
                for s in range(d):
                    for b in range(nb):
                        need = min(tidx[(s, b)] + LOOK, len(tiles) - 1)
                        while emitted[0] <= need:
                            emit_S(emitted[0])
                            emitted[0] += 1
                        emit_XY(s, b)

            T.op('act', lambda e: e.activation(out=fin, in_=accY, func=AF.Ln), ['accY'], ['KT'])
            T.op('act', lambda e: e.activation(out=fin, in_=fin, func=AF.Exp, scale=-1.0), ['KT'], ['KT'])
            T.op('dve', lambda e: e.tensor_tensor(out=fin, in0=fin, in1=accX, op=ALU.mult), ['KT', 'accX'], ['KT'])
            T.op('dve', lambda e, pair=pair: e.scalar_tensor_tensor(out=AGv[:, pair, :], in0=fin, scalar=0.5, in1=SG,
                                                                    op0=ALU.mult, op1=ALU.mult), ['KT', 'SG'] + ['stgA%d' % i_ for i_ in range(4)], ['AG'])

        T.barrier()

        cv = Carver()
        HG = cv.bf16(8 * NT)
        HGv = HG.rearrange("p (k t) -> p k t", k=8)
        UO = 8
        Us = [cv.bf16(2048 + 8) for _ in range(2)]
        As = [cv.f32(2048) for _ in range(2)]
        Ms = [cv.f32(2048) for _ in range(2)]
        ICs = [cv.f32(2048) for _ in range(2)]
        Ct = [cv.bf16(512) for _ in range(3)]
        thr = cv.f32(512)
        this_ = [cv.f32(512) for _ in range(2)]
        thi = this_[0]
        zb = [cv.bf16(512) for _ in range(2)]
        thb = [cv.bf16(512) for _ in range(2)]
        ticount = [0]
        Ht = [cv.f32(512) for _ in range(2)]
        BDR = cv.bf16(1024)
        BDI = cv.bf16(1024)
        DG = cv.bf16(4 * 128)
        hinit = cv.f32(2)
        stage_cast(bd_d[0], BDR, 'BDR')
        stage_cast(bd_d[1], BDI, 'BDI')
        WLA[0] = 1
        PB4 = [(pA[0], 'pA0'), (pA[1], 'pA1'), (pX, 'pX'), (pY, 'pY')]
        p4count = [0]

        def next_p4():
            i = p4count[0] % 4
            p4count[0] += 1
            return PB4[i]

        hcount = [0]
        ccount = [0]
        wB_ = {}

        def s12_steps(ch, hf):
            st_ = hf
            Uu, ures = Us[st_], 'U%d' % st_
            Au, ares = As[st_], 'A%d' % st_
            Mu, mres = Ms[st_], 'M%d' % st_
            ICu, icres = ICs[st_], 'IC%d' % st_
            c = 40 + ch
            if hf == 0:
                wB_['w'] = load_w(c)
                for j in range(4):
                    T.op('dve', lambda e, j=j: e.tensor_scalar(out=DG[:, 128 * j:128 * j + 128], in0=ident,
                                                               scalar1=col(C_CW + ch * 4 + j), scalar2=None, op0=ALU.mult),
                         CST, ['DG'])
                T.op('dve', lambda e: e.memset(Uu[:, 0:UO], 0.0), [], [ures])
            else:
                T.op('dve', lambda e: e.tensor_copy(out=Uu[:, UO - 3:UO], in_=Us[0][:, UO + 2045:UO + 2048]), ['U0'], [ures])
            w, wres = wB_['w']
            off = 2048 * hf
            convs = {}
            gts = {}

            def proj(k):
                if not (0 <= k < 4):
                    return
                (pb1, pres1) = next_p4()
                inproj(w, wres, pb1, pres1, lambda kc: XTv[:, kc, off + 512 * k:off + 512 * k + 512], 512,
                       xcols=(off + 512 * k, off + 512 * k + 512))
                if hf == 0:
                    T.op('dve', lambda e: e.tensor_scalar(out=Uu[:, UO + 512 * k:UO + 512 * k + 512], in0=pb1[:, 0:512],
                                                          scalar1=col(C_FLAG), scalar2=col2(D_FBU + ch), op0=ALU.mult,
                                                          op1=ALU.add), [pres1] + CST, [ures])
                else:
                    T.op('dve', lambda e: e.tensor_scalar(out=Uu[:, UO + 512 * k:UO + 512 * k + 512], in0=pb1[:, 0:512],
                                                          scalar1=col(C_BIN + c), scalar2=None, op0=ALU.add),
                         [pres1] + CST, [ures])

            def conv(j):
                if not (0 <= j < 4):
                    return
                (pb, pres) = next_p4()

                def convfn(e):
                    ins = None
                    for tp in range(4):
                        ins = e.matmul(pb[:, 0:512], lhsT=DG[:, 128 * tp:128 * tp + 128],
                                       rhs=Uu[:, UO + 512 * j - 3 + tp:UO + 512 * j - 3 + tp + 512], start=(tp == 0), stop=(tp == 3))
                    return ins
                T.op('pe', convfn, ['DG', ures], [pres])
                ci = ccount[0] % 3
                ccount[0] += 1
                cb_, cres = Ct[ci], 'Ct%d' % ci
                convs[j] = (cb_, cres)
                T.op('dve', lambda e: e.tensor_scalar(out=cb_[:], in0=pb[:, 0:512], scalar1=col(C_CB + ch), scalar2=None,
                                                      op0=ALU.add), [pres] + CST, [cres])

            def gates(j2):
                if not (0 <= j2 < 4):
                    return
                gb_, gres = convs[j2]
                sl = slice(512 * j2, 512 * j2 + 512)
                ti = ticount[0] % 2
                ticount[0] += 1
                tib, tires = this_[ti], 'thi%d' % ti
                gts[j2] = (tib, tires, gb_, gres)
                T.op('pe', lambda e: e.matmul(pB[:, 0:512], lhsT=BDR[:, 128 * ch:128 * ch + 128], rhs=gb_[:], start=True, stop=True),
                     [gres, 'BDR'], ['pB'])
                T.op('pe', lambda e: e.matmul(pS[0][:, 0:512], lhsT=BDI[:, 128 * ch:128 * ch + 128], rhs=gb_[:], start=True,
                                              stop=True), [gres, 'BDI'], ['bS0'])
                T.op('act', lambda e: e.activation(out=thr[:], in_=pB[:, 0:512], func=AF.Tanh, bias=col2(D_HBR + ch), scale=0.5),
                     ['pB'] + CST, ['thr'])
                T.op('act', lambda e: e.activation(out=tib[:], in_=pS[0][:, 0:512], func=AF.Tanh, bias=col2(D_HBI + ch), scale=0.5),
                     ['bS0'] + CST, [tires])
                T.op('act', lambda e: e.activation(out=Au[:, sl], in_=thr[:], func=AF.Exp, bias=col2(D_HSC + ch),
                                                   scale=col2(D_HSC + ch)), ['thr'] + CST, [ares])
                T.op('act', lambda e: e.activation(out=Mu[:, sl], in_=thr[:], func=AF.Exp, bias=col2(D_SC1 + ch),
                                                   scale=col2(D_SC1 + ch)), ['thr'] + CST, [mres])

            def ic(j2):
                if not (0 <= j2 < 4):
                    return
                tib, tires, gb_, gres = gts[j2]
                sl = slice(512 * j2, 512 * j2 + 512)
                T.op('dve', lambda e: e.scalar_tensor_tensor(out=ICu[:, sl], in0=tib[:], scalar=1.0, in1=gb_[:], op0=ALU.add,
                                                             op1=ALU.mult), [tires, gres], [icres])
            return proj, conv, gates, ic

        def s45_steps(ch, hf):
            st_ = hf
            Au, ares = As[st_], 'A%d' % st_
            Mu, mres = Ms[st_], 'M%d' % st_
            ICu, icres = ICs[st_], 'IC%d' % st_
            T.op('act', lambda e: e.activation(out=Mu[:], in_=Mu[:], func=AF.Sqrt, bias=col(C_ONE), scale=-1.0),
                 [mres] + CST, [mres])
            if hf == 1:
                c2 = 48 + ch
                w2, wres2 = load_w(c2)
            hts = {}

            def main(j):
                if not (0 <= j < 4):
                    return
                sl = slice(512 * j, 512 * j + 512)
                T.op('dve', lambda e: e.scalar_tensor_tensor(out=Mu[:, sl], in0=Mu[:, sl], scalar=0.5, in1=ICu[:, sl],
                                                             op0=ALU.mult, op1=ALU.mult), [mres, icres], [mres])
                hi = hcount[0] % 2
                hcount[0] += 1
                hb_, hres = Ht[hi], 'Ht%d' % hi
                pb_, pres_ = Ht[1 - hi], 'Ht%d' % (1 - hi)
                hts[j] = (hb_, hres)
                if hf == 0 and j == 0:
                    T.op('dve', lambda e: e.tensor_tensor_scan(out=hb_[:], data0=Au[:, sl], data1=Mu[:, sl], initial=0.0,
                                                               op0=ALU.mult, op1=ALU.add), [ares, mres], [hres])
                elif hf == 1 and j == 0:
                    T.op('dve', lambda e: e.tensor_scalar(out=hinit[:, 0:1], in0=pb_[:, 511:512], scalar1=col(C_FLAG), scalar2=None,
                                                          op0=ALU.mult), [pres_] + CST, ['hinit'])
                    T.op('dve', lambda e: e.tensor_tensor_scan(out=hb_[:], data0=Au[:, sl], data1=Mu[:, sl], initial=hinit[:, 0:1],
                                                               op0=ALU.mult, op1=ALU.add), [ares, mres, 'hinit'], [hres])
                else:
                    T.op('dve', lambda e: e.tensor_tensor_scan(out=hb_[:], data0=Au[:, sl], data1=Mu[:, sl], initial=pb_[:, 511:512],
                                                               op0=ALU.mult, op1=ALU.add), [ares, mres, pres_], [hres])
                if hf == 1:
                    (pb, pres) = next_p4()
                    inproj(w2, wres2, pb, pres, lambda kc: XTv[:, kc, 2048 + 512 * j:2048 + 512 * j + 512], 512,
                           xcols=(2048 + 512 * j, 2560 + 512 * j))
                    zi = j % 2
                    T.op('act', lambda e: e.activation(out=zb[zi][:], in_=pb[:, 0:512], func=AF.Identity, bias=col(C_BIN + c2),
                                                       scale=1.0), [pres] + CST, ['zb%d' % zi])
                    T.op('act', lambda e: e.activation(out=thb[zi][:], in_=pb[:, 0:512], func=AF.Tanh, bias=col2(D_HBIN + c2),
                                                       scale=0.5), [pres] + CST, ['thb%d' % zi])

            def tail(j):
                if not (0 <= j < 4) or hf == 0:
                    return
                sl = slice(512 * j, 512 * j + 512)
                zi = j % 2
                hb_, hres = hts[j]
                T.op('dve', lambda e: e.scalar_tensor_tensor(out=zb[zi][:], in0=thb[zi][:], scalar=1.0, in1=zb[zi][:], op0=ALU.add,
                                                             op1=ALU.mult), ['zb%d' % zi, 'thb%d' % zi], ['zb%d' % zi])
                T.op('dve', lambda e: e.scalar_tensor_tensor(out=HGv[:, ch, sl], in0=zb[zi][:], scalar=0.5, in1=hb_[:],
                                                             op0=ALU.mult, op1=ALU.mult), ['zb%d' % zi, hres], ['HG'])
            return main, tail

        units = [(ch, hf) for ch in range(8) for hf in range(2)]
        NFILL = 2
        NU = len(units)
        E12 = {}
        E45 = {}
        nop = lambda k: None
        for g_ in range(4 * NU + 12):
            def filler(e):
                ins = None
                for _q in range(NFILL):
                    ins = e.matmul(pS[1][:, 0:512], lhsT=ident, rhs=XTv[:, 0, 0:512], start=True, stop=True)
                return ins
            T.op('pe', filler, [], ['bS1'])
            for v in range(NU):
                if v in E45:
                    E45[v][1](g_ - (4 * v + 8))
            gt = g_ - 3
            if 0 <= gt < 4 * NU:
                E12[gt // 4][2](gt % 4)
            for v in range(NU):
                if g_ == 4 * v + 6:
                    E45[v] = s45_steps(*units[v])
                if v in E45:
                    E45[v][0](g_ - (4 * v + 7))
            if 0 <= gt < 4 * NU:
                E12[gt // 4][3](gt % 4)
            ct = g_ - 1
            if 0 <= ct < 4 * NU:
                E12[ct // 4][1](ct % 4)
            if g_ < 4 * NU:
                if g_ % 4 == 0:
                    E12[g_ // 4] = s12_steps(*units[g_ // 4])
                E12[g_ // 4][0](g_ % 4)

        T.barrier()

        cv = Carver()
        HG = cv.bf16(8 * NT)
        HGv = HG.rearrange("p (k t) -> p k t", k=8)
        MG = cv.bf16(8 * NT)
        MGv = MG.rearrange("p (k t) -> p k t", k=8)
        c1_mark = cv.off
        WAP = cv.bf16(4 * 1024)
        WLP = cv.bf16(8 * 1024)
        thA = cv.f32(512)
        thB = cv.f32(512)
        m1 = cv.f32(512)
        m2 = cv.f32(512)
        stgC = [(cv.f32(1024), 'stgC%d' % i) for i in range(3)] + [(wst[i][:], 'wst%d' % i) for i in range(2)]
        engsC = ['act', 'dve', 'pool']
        for kc in range(4):
            stage_cast(wap_d[kc], WAP[:, 1024 * kc:1024 * kc + 1024], 'WAP%d' % kc, eng=engsC[kc % 2], stg=stgC)
        for kc in range(8):
            stage_cast(wlp_d[kc], WLP[:, 1024 * kc:1024 * kc + 1024], 'WLP%d' % kc, eng=engsC[kc % 3], stg=stgC)

        WLA[0] = 1
        for dc in range(8):
            cA, cB_ = 56 + dc, 64 + dc
            wA, wAres = load_w(cA)
            wB, wBres = load_w(cB_)
            for j in range(4):
                sl = slice(512 * j, 512 * j + 512)
                pstA, psresA = next_pA()
                inproj(wA, wAres, pstA, psresA, lambda kc, j=j: xmain(kc)[:, 512 * j:512 * j + 512], 512)
                T.op('act', lambda e, pstA=pstA, cA=cA: e.activation(out=thA[:], in_=pstA[:, 0:512], func=AF.Tanh,
                                                                     bias=col2(D_HBIN + cA), scale=0.5), [psresA] + CST, ['thA'])
                pstB, psresB = next_pA()
                inproj(wB, wBres, pstB, psresB, lambda kc, j=j: xmain(kc)[:, 512 * j:512 * j + 512], 512)
                T.op('act', lambda e, pstB=pstB, cB_=cB_: e.activation(out=thB[:], in_=pstB[:, 0:512], func=AF.Tanh,
                                                                       bias=col2(D_HBIN + cB_), scale=0.5), [psresB] + CST, ['thB'])

                def yafn(e, dc=dc, sl=sl):
                    ins = None
                    for kc in range(4):
                        ins = e.matmul(pX[:, 0:512], lhsT=WAP[:, 1024 * kc + 128 * dc:1024 * kc + 128 * dc + 128],
                                       rhs=AGv[:, kc, sl], start=(kc == 0), stop=(kc == 3))
                    return ins
                T.op('pe', yafn, ['WAP%d' % i_ for i_ in range(4)] + ['AG'], ['pX'])

                def ybfn(e, dc=dc, sl=sl):
                    ins = None
                    for kc in range(8):
                        ins = e.matmul(pY[:, 0:512], lhsT=WLP[:, 1024 * kc + 128 * dc:1024 * kc + 128 * dc + 128],
                                       rhs=HGv[:, kc, sl], start=(kc == 0), stop=(kc == 7))
                    return ins
                T.op('pe', ybfn, ['WLP%d' % i_ for i_ in range(8)] + ['HG'], ['pY'])
                T.op('dve', lambda e: e.scalar_tensor_tensor(out=m1[:], in0=thA[:], scalar=1.0, in1=pX[:, 0:512], op0=ALU.add,
                                                             op1=ALU.mult), ['thA', 'pX'], ['m1'])
                T.op('dve', lambda e: e.scalar_tensor_tensor(out=m2[:], in0=thB[:], scalar=1.0, in1=pY[:, 0:512], op0=ALU.add,
                                                             op1=ALU.mult), ['thB', 'pY'], ['m2'])
                T.op('dve', lambda e, dc=dc, sl=sl: e.tensor_tensor(out=MGv[:, dc, sl], in0=m1[:], in1=m2[:], op=ALU.add),
                     ['m1', 'm2'], ['MG'])

        T.barrier()

        cv.off = c1_mark
        WO = cv.bf16(8 * 1024)
        stats = [cv.f32(16) for _ in range(2)]
        mv = [cv.f32(8) for _ in range(2)]
        for kc in range(8):
            stage_cast(wo_d[kc], WO[:, 1024 * kc:1024 * kc + 1024], 'WO')
        xh = lambda kc: XTv[:, kc, 0:2048].bitcast(F32)
        yo = [xh(3), xh(4)]
        bcb = [xh(5), xh(6), xh(7)]
        hgf = HG.bitcast(F32)
        ybufs = [hgf[:, 0:1024], hgf[:, 1024:2048]]
        for i in range(3):
            T.dma('sp', (lambda i: lambda e: e.dma_start(out=bcb[i], in_=bc_d[i]))(i), [], ['bc%d' % i], 'bc%d' % i)

        xr = [xh(0), xh(1), xh(2)]

        def dma_x(tt):
            xi = tt % 3
            T.dma('sp', lambda e: e.dma_start(out=xr[xi], in_=xres_d[tt]), [], ['xr%d' % xi], 'xr%d' % xi)

        def prep_x(tt):
            xi = tt % 3
            T.op('act', lambda e: e.activation(out=xr[xi], in_=xr[xi], func=AF.Identity, scale=ALPHA), ['xr%d' % xi], ['xr%d' % xi])
            T.op('pool', lambda e: e.tensor_tensor(out=xr[xi], in0=xr[xi], in1=bcb[0], op=ALU.add), ['xr%d' % xi, 'bc0'],
                 ['xr%d' % xi])

        def stage_b1(tt):
            i2 = tt % 2
            mv_, mvres = mv[i2], 'mv%d' % i2
            ybuf, yres = ybufs[i2], 'ybuf%d' % i2
            T.op('dve', lambda e: e.scalar_tensor_tensor(out=mv_[:, 4:5], in0=mv_[:, 0:1], scalar=-1.0, in1=mv_[:, 3:4],
                                                         op0=ALU.mult, op1=ALU.mult), [mvres, mvres + 'c'], [mvres + 'd'])
            T.op('act', lambda e: e.activation(out=yo[i2], in_=ybuf, func=AF.Identity, bias=mv_[:, 4:5], scale=mv_[:, 3:4]),
                 [yres + '_0', yres + '_1', mvres + 'c', mvres + 'd'], ['yo%d' % i2])

        def stage_b2(tt):
            i2 = tt % 2
            T.op('dve', lambda e: e.tensor_tensor(out=yo[i2], in0=yo[i2], in1=bcb[1], op=ALU.mult), ['yo%d' % i2, 'bc1'],
                 ['yo%d' % i2])
            T.op('pool', lambda e: e.tensor_tensor(out=yo[i2], in0=yo[i2], in1=bcb[2], op=ALU.add), ['yo%d' % i2, 'bc2'],
                 ['yo%d' % i2])
            T.dma('sp', lambda e: e.dma_start(out=y_d[tt], in_=yo[i2]), ['yo%d' % i2], [], 'y%d' % i2)

        def stage_a(tt):
            i2, xi = tt % 2, tt % 3
            ybuf, yres = ybufs[i2], 'ybuf%d' % i2
            st_, stres = stats[i2], 'stats%d' % i2
            mv_, mvres = mv[i2], 'mv%d' % i2
            for hf in range(2):
                pst, psres = next_pA()

                def ofn(e, pst=pst, hf=hf):
                    ins = None
                    for kc in range(8):
                        ins = e.matmul(pst[:, 0:512], lhsT=MGv[:, kc, 128 * tt:128 * tt + 128],
                                       rhs=WO[:, 1024 * kc + 512 * hf:1024 * kc + 512 * hf + 512], start=(kc == 0), stop=(kc == 7))
                    return ins
                T.op('pe', ofn, ['MG', 'WO'], [psres])
                T.op('dve', lambda e, pst=pst, hf=hf: e.scalar_tensor_tensor(
                    out=ybuf[:, 512 * hf:512 * hf + 512], in0=pst[:, 0:512], scalar=0.5, in1=xr[xi][:, 512 * hf:512 * hf + 512],
                    op0=ALU.mult, op1=ALU.add), [psres, 'xr%d' % xi], [yres + '_%d' % hf])
                T.op('dve', lambda e, hf=hf: e.bn_stats(out=st_[:, 6 * hf:6 * hf + 6], in_=ybuf[:, 512 * hf:512 * hf + 512]),
                     [yres + '_%d' % hf], [stres + '_%d' % hf])
            T.op('dve', lambda e: e.bn_aggr(out=mv_[:, 0:2], in_=st_[:, 0:12]), [stres + '_0', stres + '_1'], [mvres])
            T.op('act', lambda e: e.activation(out=mv_[:, 2:3], in_=mv_[:, 1:2], func=AF.Ln, bias=col(C_EPS), scale=1.0),
                 [mvres] + CST, [mvres + 'b'])
            T.op('act', lambda e: e.activation(out=mv_[:, 3:4], in_=mv_[:, 2:3], func=AF.Exp, scale=-0.5),
                 [mvres + 'b'], [mvres + 'c'])

        dma_x(0)
        dma_x(1)
        dma_x(2)
        prep_x(0)
        prep_x(1)
        for tt in range(17):
            if tt >= 1:
                stage_b1(tt - 1)
            if tt < 16:
                stage_a(tt)
            if tt + 3 < 16:
                dma_x(tt + 3)
            if tt >= 1:
                stage_b2(tt - 1)
            if tt + 2 < 16:
                prep_x(tt + 2)

        T.wait_all('sp')
        T.emit()
    return nc


def _rope_tables(pos0):
    half = 32
    inv_freq = (10000.0 ** (-np.arange(half, dtype=np.float32) / half)).astype(np.float32)
    pos = np.arange(pos0, pos0 + 2048, dtype=np.float32)
    ang = pos[None, :] * inv_freq[:, None]
    cos = np.cos(ang).astype(np.float32)
    sin = np.sin(ang).astype(np.float32)
    p = np.arange(128)
    i = p % 32
    sign = np.where((p % 64) < 32, -1.0, 1.0).astype(np.float32)
    return cos[i], sin[i] * sign[:, None]


def _consts_bf(flag):
    cb = np.zeros((128, 1024), np.float32)
    cb[:, B_ID:B_ID + 128] = np.eye(128, dtype=np.float32)
    pm = np.zeros((128, 128), np.float32)
    for m in range(128):
        partner = m + 32 if (m % 64) < 32 else m - 32
        pm[partner, m] = 1.0
    cb[:, B_PM:B_PM + 128] = pm
    cb[:, B_ONES:B_ONES + 64] = 1.0
    k = np.arange(128)[:, None]
    q = np.arange(128)[None, :]
    NEG = -1000.0
    cur = np.where(k <= q, 0.0, NEG).astype(np.float32)
    prev = np.where(k >= q, 0.0, NEG).astype(np.float32)
    cb[:, B_M:B_M + 128] = cur
    cb[:, B_M + 128:B_M + 256] = prev
    cb[:, B_MH:B_MH + 128] = prev if flag > 0.5 else NEG
    return cb


_NC_CACHE = {}


def kernel(x, w_in, b_in, conv_w, conv_b, lru_wr, lru_br, lru_wi, lru_bi, lru_lambda,
           w_attn_proj, w_lru_proj, w_out, b_out, ln_gain, ln_bias):
    f = lambda a: np.ascontiguousarray(np.asarray(a, dtype=np.float32))
    x = f(x)
    W = f(w_in)[0]
    win_l = f(W.reshape(8, 128, 72, 128).transpose(2, 1, 0, 3).reshape(72, 128, 1024))
    colmaj = lambda v, n: f(np.asarray(v, np.float32).reshape(n, 128).T)
    bd = np.zeros((2, 8, 128, 128), np.float32)
    for t, wsrc in enumerate((f(lru_wr)[0], f(lru_wi)[0])):
        for ch in range(8):
            bd[t, ch, 0:64, 0:64] = wsrc[2 * ch]
            bd[t, ch, 64:128, 64:128] = wsrc[2 * ch + 1]
    bd_l = f(bd.transpose(0, 2, 1, 3).reshape(2, 128, 1024))
    wap_l = f(f(w_attn_proj)[0].reshape(4, 128, 1024))
    wlp_l = f(f(w_lru_proj)[0].reshape(8, 128, 1024))
    wo_l = f(f(w_out)[0].reshape(8, 128, 1024))
    bc = f(np.stack([np.broadcast_to(f(b_out)[0], (128, 1024)), np.broadcast_to(f(ln_gain)[0], (128, 1024)),
                     np.broadcast_to(f(ln_bias)[0], (128, 1024))]))
    cw = f(conv_w)[0]
    in_maps = []
    for c in range(8):
        b, half = c // 2, c % 2
        flag = float(half)
        xm = x[b, half * 2048:(half + 1) * 2048]
        xh = x[b, 0:2048] if half == 1 else np.zeros((2048, 1024), np.float32)
        xT = np.concatenate([xh.T, xm.T], axis=1)
        cst = np.zeros((128, NCST), np.float32)
        cst[:, C_BIN:C_BIN + 72] = colmaj(f(b_in)[0], 72)
        for ch in range(8):
            for j in range(4):
                cst[:, C_CW + ch * 4 + j] = cw[j, ch * 128:(ch + 1) * 128]
        cst[:, C_CB:C_CB + 8] = colmaj(f(conv_b)[0], 8)
        cst[:, C_BR:C_BR + 8] = colmaj(f(lru_br)[0], 8)
        cst[:, C_BI:C_BI + 8] = colmaj(f(lru_bi)[0], 8)
        cst[:, C_LAM:C_LAM + 8] = colmaj(f(lru_lambda)[0], 8)
        cst[:, C_FLAG] = flag
        cst[:, C_ONE] = 1.0
        cst[:, C_MHALF] = -0.5
        cst[:, C_EPS] = LN_EPS
        cm, sm = _rope_tables(half * 2048)
        chh, shh = _rope_tables(0)
        in_maps.append({
            "xT": f(xT.reshape(8, 128, 4096)),
            "xres": f(xm.reshape(16, 128, 1024)),
            "win": win_l,
            "cst": cst,
            "cb": _consts_bf(flag),
            "tab": f(np.stack([cm, sm, chh, shh])),
            "bd": bd_l,
            "wap": wap_l,
            "wlp": wlp_l,
            "wo": wo_l,
            "bc": bc,
        })
    if "nc" not in _NC_CACHE:
        _NC_CACHE["nc"] = build()
    res = run_bass_kernel_spmd(_NC_CACHE["nc"], in_maps, core_ids=list(range(8)))
    out = np.zeros((4, 4096, 1024), np.float32)
    for c in range(8):
        b, half = c // 2, c % 2
        out[b, half * 2048:(half + 1) * 2048] = np.asarray(res.results[c]["y"]).reshape(2048, 1024)
    return out
```

```python
import contextlib
import numpy as np
import concourse.bass as bass
import concourse.mybir as mybir
from concourse.bass_utils import run_bass_kernel_spmd

F32 = mybir.dt.float32
BF16 = mybir.dt.bfloat16
AF = mybir.ActivationFunctionType
ALU = mybir.AluOpType

NT = 2048
ALPHA = 2.0 ** 0.25
LN_EPS = 1e-5
DILS = (1, 4, 16)

C_BIN, C_CW, C_CB, C_BR, C_BI, C_LAM, C_FLAG, C_ONE, C_MHALF, C_EPS, NCST = 0, 72, 104, 112, 120, 128, 136, 137, 138, 140, 144
D_HBIN, D_HBR, D_HBI, D_HSC, D_SC1, D_FBU, D_TMP, NCST2 = 0, 72, 80, 88, 96, 104, 112, 128
B_ID, B_PM, B_ONES, B_M, B_MH = 0, 128, 256, 320, 576


class Tracker:
    def __init__(self, nc, stack):
        self.nc = nc
        self.stack = stack
        self.engs = {'pe': nc.tensor, 'act': nc.scalar, 'dve': nc.vector, 'pool': nc.gpsimd, 'sp': nc.sync}
        self.streams = {e: [] for e in self.engs}
        self.sems = {}
        self.semval = {}
        self.waited = {e: {} for e in self.engs}
        self.last_w = {}
        self.last_r = {}

    def sem(self, name):
        if name not in self.sems:
            self.sems[name] = self.stack.enter_context(self.nc.semaphore(name))
            self.semval[name] = 0
        return name

    def _need(self, eng, deps):
        for (s, v) in deps:
            if self.waited[eng].get(s, 0) < v:
                self.waited[eng][s] = v
                self.streams[eng].append(('wait', s, v))

    def op(self, eng, fn, reads=(), writes=(), sem=None, inc=1):
        deps = []
        for r in reads:
            if r in self.last_w:
                deps.append(self.last_w[r])
        own = 'S_' + eng
        for w in writes:
            if w in self.last_w and self.last_w[w][0] != own:
                deps.append(self.last_w[w])
            for s, v in self.last_r.get(w, {}).items():
                deps.append((s, v))
        if eng == 'pe':
            deps = [d for d in deps if d[0] != own]
        self._need(eng, deps)
        s = self.sem(sem if sem is not None else own)
        self.semval[s] += inc
        v = self.semval[s]
        self.streams[eng].append(('op', fn, s, inc))
        for w in writes:
            self.last_w[w] = (s, v)
            self.last_r[w] = {}
        for r in reads:
            d = self.last_r.setdefault(r, {})
            d[s] = max(d.get(s, 0), v)

    def dma(self, eng, fn, reads, writes, slot):
        self.op(eng, fn, reads, writes, sem='D_' + slot, inc=16)

    def wait_all(self, eng):
        self._need(eng, [(s, v) for s, v in self.semval.items() if v > 0])

    def barrier(self):
        for e in self.engs:
            self.wait_all(e)

    def emit(self):
        nc = self.nc
        with nc.Block() as block:
            def run(engname, eng):
                for item in self.streams[engname]:
                    if item[0] == 'wait':
                        eng.wait_ge(self.sems[item[1]], item[2])
                    else:
                        _, fn, s, inc = item
                        ins = fn(eng)
                        ins.then_inc(self.sems[s], inc)

            @block.tensor
            def _(e):
                run('pe', e)

            @block.scalar
            def _(e):
                run('act', e)

            @block.vector
            def _(e):
                run('dve', e)

            @block.gpsimd
            def _(e):
                run('pool', e)

            @block.sync
            def _(e):
                run('sp', e)


def gview(ap, d, j):
    if d == 1:
        return ap[:, 512 * j:512 * j + 512]
    if d == 4:
        return ap.rearrange("p (m r) -> p r m", r=4)[:, j, :]
    return ap.rearrange("p (m r) -> p r m", r=16)[:, 4 * j:4 * j + 4, :]


def hview(ap, d):
    if d == 1:
        return ap[:, 1920:2048]
    return ap.rearrange("p (m r) -> p r m", r=4)[:, :, 384:512]


def dstview(buf, d, j):
    if d == 1:
        return buf[:, 512 * j:512 * j + 512]
    if d == 4:
        return buf.rearrange("p (r m) -> p r m", r=4)[:, :, 128 * j:128 * j + 128]
    return buf.rearrange("p (r m) -> p r m", r=16)[:, :, 32 * j:32 * j + 32]


def srcview(flat, d):
    if d == 1:
        return flat
    return flat.rearrange("p (m r) -> p r m", r=d)


def like(flat, ref):
    sh = ref.shape
    if len(sh) == 2:
        return flat
    return flat.rearrange("p (a b) -> p a b", a=sh[1])


def build():
    nc = bass.Bass("TRN2", target_bir_lowering=False)
    dt = lambda name, shape, kind="ExternalInput": nc.dram_tensor(name, shape, F32, kind=kind).ap()
    sbn = lambda n: "s_" + n
    xT_d = dt("xT", [8, 128, 4096])
    xres_d = dt("xres", [16, 128, 1024])
    win_d = dt("win", [72, 128, 1024])
    cst_d = dt("cst", [128, NCST])
    cb_d = dt("cb", [128, 1024])
    tab_d = dt("tab", [4, 128, 2048])
    bd_d = dt("bd", [2, 128, 1024])
    wap_d = dt("wap", [4, 128, 1024])
    wlp_d = dt("wlp", [8, 128, 1024])
    wo_d = dt("wo", [8, 128, 1024])
    bc_d = dt("bc", [3, 128, 1024])
    y_d = dt("y", [16, 128, 1024], kind="ExternalOutput")

    with contextlib.ExitStack() as st:
        T = Tracker(nc, st)
        sb = lambda name, shape, dtp: st.enter_context(nc.sbuf_tensor("s_" + name, shape, dtp))
        ps = lambda name, shape, dtp: st.enter_context(nc.psum_tensor(name, shape, dtp))

        XT = sb("XT", [128, 8 * 4096], BF16)
        XTv = XT[:].rearrange("p (k t) -> p k t", k=8)
        AG = sb("AG", [128, 4 * NT], BF16)
        AGv = AG[:].rearrange("p (k t) -> p k t", k=4)
        cst = sb("cst", [128, NCST], F32)
        cst2 = sb("cst2", [128, NCST2], F32)
        CB = sb("CB", [128, 704], BF16)
        wst = [sb("wst%d" % i, [128, 1024], F32) for i in range(2)]
        wbf = [sb("wbf%d" % i, [128, 1024], BF16) for i in range(3)]
        ARENA_F = 28288
        arena = sb("arena", [128, ARENA_F], F32)

        pA = [ps("pA%d" % i, [128, 512], F32) for i in range(2)]
        pB = ps("pB", [128, 512], F32)
        pS = [ps("pS%d" % i, [128, 512], F32) for i in range(2)]
        pX = ps("pX", [128, 512], F32)
        pY = ps("pY", [128, 512], F32)
        pT = ps("pT", [128, 1024], BF16)
        pT32 = pT[:].bitcast(F32)
        pBb = pB[:].bitcast(BF16)
        vtcount = [0]
        SBANKS = [(pS[0], 'bS0'), (pS[1], 'bS1'), (pA[0], 'pA0'), (pA[1], 'pA1')]
        XYSETS = [((pX, 'pX'), (pY, 'pY')), ((pB, 'pB'), (pT32, 'pT'))]

        class Carver:
            def __init__(self):
                self.off = 0

            def f32(self, n):
                a = arena[:, self.off:self.off + n]
                self.off += n
                assert self.off <= ARENA_F, self.off
                return a

            def bf16(self, n):
                assert n % 2 == 0
                return self.f32(n // 2).bitcast(BF16)

        col = lambda c: cst[:, c:c + 1]
        col2 = lambda c: cst2[:, c:c + 1]
        ident = CB[:, B_ID:B_ID + 128]
        permm = CB[:, B_PM:B_PM + 128]
        ones64 = CB[:, B_ONES:B_ONES + 64]
        maskM = CB[:, B_M:B_M + 256]
        maskH = CB[:, B_MH:B_MH + 128]

        wcount = [0]
        CASTE = 'act'

        def do_cast(eng, dst_ap, src_ap):
            if eng == 'act':
                return lambda e: e.activation(out=dst_ap, in_=src_ap, func=AF.Identity)
            return lambda e: e.tensor_copy(out=dst_ap, in_=src_ap)

        def stage_cast(dram_ap, dst_ap, dst_res, eng=None, stg=None):
            eng = eng or CASTE
            i = wcount[0]
            wcount[0] += 1
            if stg is None:
                s_ = i % 2
                sap, sres = wst[s_][:], 'wst%d' % s_
            else:
                sap, sres = stg[i % len(stg)]
            T.dma('sp', lambda e: e.dma_start(out=sap, in_=dram_ap), [], [sres], sres)
            T.op(eng, do_cast(eng, dst_ap, sap), [sres], [dst_res])

        WORDER = []
        for pair_ in range(4):
            WORDER.append(36 + pair_)
            for g_ in range(3):
                WORDER += [12 + 4 * g_ + pair_, 4 * g_ + pair_, 24 + 4 * g_ + pair_]
        WORDER.append(40)
        for ch_ in range(1, 8):
            WORDER += [40 + ch_, 48 + ch_ - 1]
        WORDER.append(55)
        for dc_ in range(8):
            WORDER += [56 + dc_, 64 + dc_]
        wissued = [0]
        wbcount = [0]
        WLA = [2]

        def w_prefetch(upto):
            while wissued[0] < min(upto, len(WORDER)):
                kk = wissued[0]
                wissued[0] += 1
                stage_cast(win_d[WORDER[kk]], wbf[kk % 3][:], 'wbf%d' % (kk % 3))

        def load_w(c):
            k = wbcount[0]
            wbcount[0] += 1
            assert WORDER[k] == c, (k, c, WORDER[k])
            while wissued[0] <= min(k + WLA[0], len(WORDER) - 1):
                kk = wissued[0]
                wissued[0] += 1
                stage_cast(win_d[WORDER[kk]], wbf[kk % 3][:], 'wbf%d' % (kk % 3))
            return wbf[k % 3], 'wbf%d' % (k % 3)

        PEND = []

        def defer(f):
            PEND.append(f)

        def flush():
            run_ = PEND[:]
            del PEND[:]
            for f_ in run_:
                f_()

        pacount = [0]

        def next_pA():
            i = pacount[0] % 2
            pacount[0] += 1
            return pA[i], 'pA%d' % i

        XTALL = ['XT%d_%d' % (kc_, j_) for kc_ in range(8) for j_ in range(4)]

        def inproj(w, wres, pst, psres, movfn, n, extra=(), xcols=None):
            if xcols is None:
                xres_ = XTALL
            else:
                xres_ = ['XT%d_%d' % (kc_, j_) for kc_ in range(8) for j_ in range(xcols[0] // 1024, (xcols[1] - 1) // 1024 + 1)]
            def fn(e):
                ins = None
                for kc in range(8):
                    ins = e.matmul(pst[:, 0:n], lhsT=w[:, kc * 128:(kc + 1) * 128], rhs=movfn(kc),
                                   start=(kc == 0), stop=(kc == 7))
                return ins
            T.op('pe', fn, [wres] + xres_ + list(extra), [psres])
            flush()

        T.dma('sp', lambda e: e.dma_start(out=cst[:], in_=cst_d[:, :]), [], ['cst'], 'cst')
        T.dma('sp', lambda e: e.dma_start(out=wst[0][:, 0:704], in_=cb_d[:, 0:704]), [], ['wst0'], 'wst0')
        T.op('act', lambda e: e.activation(out=CB[:], in_=wst[0][:, 0:704], func=AF.Identity), ['wst0'], ['CB'])
        AGf = AG[:].bitcast(F32)
        stg0 = [(AGf[:, 1024 * i:1024 * i + 1024], 'stgA%d' % i) for i in range(4)] + [(wst[i][:], 'wst%d' % i) for i in range(2)]
        engs3 = ['act', 'dve', 'pool']
        n_ = 0
        for j in (2, 3, 0, 1):
            if j == 0:
                w_prefetch(3)
            for kc in range(8):
                stage_cast(xT_d[kc, :, 1024 * j:1024 * j + 1024], XTv[:, kc, 1024 * j:1024 * j + 1024], 'XT%d_%d' % (kc, j),
                           eng=engs3[n_ % 3], stg=stg0)
                n_ += 1
        T.op('dve', lambda e: e.tensor_scalar(out=cst2[:, D_HBIN:D_HBIN + 72], in0=cst[:, C_BIN:C_BIN + 72], scalar1=0.5,
                                              scalar2=None, op0=ALU.mult), ['cst'], ['cst2a'])
        T.op('dve', lambda e: e.tensor_scalar(out=cst2[:, D_HBR:D_HBR + 16], in0=cst[:, C_BR:C_BR + 16], scalar1=0.5,
                                              scalar2=None, op0=ALU.mult), ['cst'], ['cst2b'])
        T.op('dve', lambda e: e.tensor_scalar(out=cst2[:, D_FBU:D_FBU + 8], in0=cst[:, C_BIN + 40:C_BIN + 48],
                                              scalar1=col(C_FLAG), scalar2=None, op0=ALU.mult), ['cst'], ['cst2c'])
        T.op('act', lambda e: e.activation(out=cst2[:, D_TMP:D_TMP + 8], in_=cst[:, C_LAM:C_LAM + 8], func=AF.Exp, scale=-1.0),
             ['cst'], ['cst2t'])
        T.op('act', lambda e: e.activation(out=cst2[:, D_TMP + 8:D_TMP + 16], in_=cst2[:, D_TMP:D_TMP + 8], func=AF.Ln,
                                           bias=col(C_ONE), scale=1.0), ['cst2t', 'cst'], ['cst2u'])
        T.op('dve', lambda e: e.tensor_scalar(out=cst2[:, D_HSC:D_HSC + 8], in0=cst2[:, D_TMP + 8:D_TMP + 16], scalar1=-4.0,
                                              scalar2=None, op0=ALU.mult), ['cst2u'], ['cst2d'])
        T.op('dve', lambda e: e.tensor_scalar(out=cst2[:, D_SC1:D_SC1 + 8], in0=cst2[:, D_TMP + 8:D_TMP + 16], scalar1=-8.0,
                                              scalar2=None, op0=ALU.mult), ['cst2u'], ['cst2e'])
        CST = ['cst', 'cst2a', 'cst2b', 'cst2c', 'cst2d', 'cst2e', 'CB']

        cv = Carver()
        tabs = [cv.f32(2048) for _ in range(4)]
        accX = cv.f32(2048)
        accY = cv.f32(2048)
        QTP = [cv.bf16(2048) for _ in range(2)]
        KT = cv.bf16(4096)
        VK = cv.bf16(4096)
        kn = [cv.bf16(512) for _ in range(2)]
        t1 = [cv.f32(512) for _ in range(2)]
        t2 = cv.f32(512)
        NPS = 6
        PT = [[cv.bf16(256) for _ in range(NPS)] for _ in range(2)]
        sgz = cv.f32(512)
        sgt = cv.f32(512)
        SG = cv.bf16(2048)
        fin = KT.bitcast(F32)
        VTN = cv.bf16(4096)

        for i in range(4):
            T.dma('sp', (lambda i: lambda e: e.dma_start(out=tabs[i], in_=tab_d[i]))(i), [], ['tab%d' % i], 'tab%d' % i)
        TAB = ['tab0', 'tab1', 'tab2', 'tab3']
        T.op('dve', lambda e: e.memset(QTP[0][64:128, :], 0.0), [], ['QTz0'])
        T.op('dve', lambda e: e.memset(QTP[1][0:64, :], 0.0), [], ['QTz1'])

        rcount = [0]

        def rope_tile(pst, psres, n, bcol, ctab, stab, dsts, dstres, d):
            i = rcount[0] % 2
            rcount[0] += 1
            knb, t1b = kn[i], t1[i]
            T.op('act', lambda e: e.activation(out=knb[:, 0:n], in_=pst[:, 0:n], func=AF.Identity, bias=col(bcol), scale=1.0),
                 [psres] + CST, ['kn%d' % i])
            T.op('pool', lambda e: e.tensor_tensor(out=t1b[:, 0:n], in0=knb[:, 0:n], in1=ctab, op=ALU.mult),
                 ['kn%d' % i] + TAB, ['t1%d' % i])

            def post():
                T.op('pe', lambda e: e.matmul(pB[:, 0:n], lhsT=permm, rhs=knb[:, 0:n], start=True, stop=True),
                     ['kn%d' % i, 'CB'], ['pB'])
                T.op('dve', lambda e: e.tensor_tensor(out=t2[:, 0:n], in0=pB[:, 0:n], in1=stab, op=ALU.mult),
                     ['pB'] + TAB, ['t2'])
                for (d_, lo, hi) in dsts:
                    T.op('dve', lambda e, d_=d_, lo=lo, hi=hi: e.tensor_tensor(
                        out=d_, in0=srcview(t1b[lo:hi, 0:n], d), in1=srcview(t2[lo:hi, 0:n], d), op=ALU.add),
                        ['t1%d' % i, 't2'], [dstres])
            defer(post)

        xmain = lambda kc: XTv[:, kc, 2048:4096]
        xhalo = lambda kc: XTv[:, kc, 0:2048]

        for pair in range(4):
            c = 36 + pair
            w, wres = load_w(c)
            for j in range(4):
                pst, psres = next_pA()
                inproj(w, wres, pst, psres, lambda kc, j=j: xmain(kc)[:, 512 * j:512 * j + 512], 512, xcols=(2048 + 512 * j, 2560 + 512 * j))
                T.op('act', lambda e, pst=pst, c=c: e.activation(out=sgz[:], in_=pst[:, 0:512], func=AF.Identity,
                                                                 bias=col(C_BIN + c), scale=1.0), [psres] + CST, ['sgz'])
                T.op('act', lambda e, pst=pst, c=c: e.activation(out=sgt[:], in_=pst[:, 0:512], func=AF.Tanh,
                                                                 bias=col2(D_HBIN + c), scale=0.5), [psres] + CST, ['sgt'])
                T.op('dve', lambda e, j=j: e.scalar_tensor_tensor(out=SG[:, 512 * j:512 * j + 512], in0=sgt[:], scalar=1.0,
                                                                  in1=sgz[:], op0=ALU.add, op1=ALU.mult),
                     ['sgz', 'sgt'], ['SG'])

            for g, d in enumerate(DILS):
                nb = 16 // d
                nhalo = {1: 128, 4: 512, 16: 2048}[d]
                c = 12 + 4 * g + pair
                w, wres = load_w(c)
                KTm = KT[:, 2048:4096]
                for j in range(4):
                    pst, psres = next_pA()
                    inproj(w, wres, pst, psres, lambda kc, j=j: xmain(kc)[:, 512 * j:512 * j + 512], 512, xcols=(2048 + 512 * j, 2560 + 512 * j))
                    rope_tile(pst, psres, 512, C_BIN + c, tabs[0][:, 512 * j:512 * j + 512], tabs[1][:, 512 * j:512 * j + 512],
                              [(dstview(KTm, d, j), 0, 128)], 'KT', d)
                if d == 16:
                    for j in range(4):
                        pst, psres = next_pA()
                        inproj(w, wres, pst, psres, lambda kc, j=j: xhalo(kc)[:, 512 * j:512 * j + 512], 512, xcols=(512 * j, 512 * j + 512))
                        rope_tile(pst, psres, 512, C_BIN + c, tabs[2][:, 512 * j:512 * j + 512], tabs[3][:, 512 * j:512 * j + 512],
                                  [(dstview(KT[:, 0:2048], d, j), 0, 128)], 'KT', d)
                else:
                    h0 = 2048 - nhalo
                    pst, psres = next_pA()
                    inproj(w, wres, pst, psres, lambda kc, h0=h0: xhalo(kc)[:, h0:2048], nhalo)
                    hd = KT[:, 0:nhalo] if d == 1 else KT[:, 0:512].rearrange("p (r m) -> p r m", r=4)
                    rope_tile(pst, psres, nhalo, C_BIN + c, tabs[2][:, h0:2048], tabs[3][:, h0:2048], [(hd, 0, 128)], 'KT', d)
                c = 4 * g + pair
                w, wres = load_w(c)
                for j in range(4):
                    pst, psres = next_pA()
                    inproj(w, wres, pst, psres, lambda kc, j=j: xmain(kc)[:, 512 * j:512 * j + 512], 512, xcols=(2048 + 512 * j, 2560 + 512 * j))
                    rope_tile(pst, psres, 512, C_BIN + c, tabs[0][:, 512 * j:512 * j + 512], tabs[1][:, 512 * j:512 * j + 512],
                              [(dstview(QTP[0][0:64, :], d, j), 0, 64), (dstview(QTP[1][64:128, :], d, j), 64, 128)], 'QT', d)
                c = 24 + 4 * g + pair
                w, wres = load_w(c)

                def vproj(movfn, n, col0, c=c, w=w, wres=wres):
                    pst, psres = next_pA()
                    inproj(w, wres, pst, psres, movfn, n)
                    T.op('act', lambda e: e.activation(out=VTN[:, col0:col0 + n], in_=pst[:, 0:n], func=AF.Identity,
                                                       bias=col(C_BIN + c), scale=1.0), [psres] + CST, ['VTN'])

                def vtrans(srcs, tile0):
                    nq = len(srcs)
                    vi = vtcount[0] % 2
                    vtcount[0] += 1
                    bank, bres = (pT, 'pT') if vi == 0 else (pBb, 'pB')

                    def tr(e):
                        ins = None
                        for q, sv in enumerate(srcs):
                            ins = e.transpose(out=bank[:, 128 * q:128 * q + 128], in_=sv, identity=ident)
                        return ins
                    T.op('pe', tr, ['VTN', 'CB'], [bres])
                    if vi == 0:
                        T.op('dve', lambda e: e.tensor_copy(out=VK[:, 128 * tile0:128 * tile0 + 128 * nq], in_=bank[:, 0:128 * nq]),
                             [bres], ['VK%d' % vi])
                    else:
                        T.op('act', lambda e: e.activation(out=VK[:, 128 * tile0:128 * tile0 + 128 * nq], in_=bank[:, 0:128 * nq],
                                                           func=AF.Identity), [bres], ['VK%d' % vi])

                for j in range(4):
                    vproj(lambda kc, j=j: xmain(kc)[:, 512 * j:512 * j + 512], 512, 2048 + 512 * j)
                if d == 16:
                    for j in range(4):
                        vproj(lambda kc, j=j: xhalo(kc)[:, 512 * j:512 * j + 512], 512, 512 * j)
                else:
                    h0 = 2048 - nhalo
                    vproj(lambda kc, h0=h0: xhalo(kc)[:, h0:2048], nhalo, h0)
                flush()
                VM = VTN[:, 2048:4096]
                VH = VTN[:, 0:2048]
                gsub = lambda ap, r: (ap if d == 1 else ap.rearrange("p (m r) -> p r m", r=d)[:, r, :])
                for q4 in range(4):
                    srcs = []
                    for gb in range(4 * q4, 4 * q4 + 4):
                        s_, b_ = gb // nb, gb % nb
                        srcs.append(gsub(VM, s_)[:, 128 * b_:128 * b_ + 128])
                    vtrans(srcs, 16 + 4 * q4)
                if d == 16:
                    for q4 in range(4):
                        vtrans([gsub(VH, r_)[:, 0:128] for r_ in range(4 * q4, 4 * q4 + 4)], 4 * q4)
                elif d == 4:
                    vtrans([gsub(VH, r_)[:, 384:512] for r_ in range(4)], 0)
                else:
                    vtrans([VH[:, 1920:2048]], 0)

                flush()
                tiles = []
                for s in range(d):
                    for kt in range(-1, nb):
                        tiles.append((s, kt))
                tidx = {t: i for i, t in enumerate(tiles)}
                emitted = [0]

                def emit_S(i):
                    s, kt = tiles[i]
                    for hh in range(2):
                        if kt == -1:
                            ncols, q0, kcol, msk = 128, 128 * (s * nb), 128 * s, maskH
                        else:
                            ncols = 128 if kt == nb - 1 else 256
                            q0, kcol, msk = 128 * (s * nb + kt), 2048 + 128 * (s * nb + kt), maskM[:, 0:ncols]
                        si = (2 * i + hh) % 4
                        pst = SBANKS[si][0][:, 0:ncols]
                        psres = SBANKS[si][1]
                        ptb = PT[hh][i % NPS]
                        ptres = 'PT%d_%d' % (hh, i % NPS)
                        def sfn(e, pst=pst, hh=hh, kcol=kcol, q0=q0, ncols=ncols, msk=msk):
                            e.matmul(pst, lhsT=KT[:, kcol:kcol + 128], rhs=QTP[hh][:, q0:q0 + ncols], start=True, stop=False)
                            return e.matmul(pst, lhsT=ident, rhs=msk, start=False, stop=True)
                        T.op('pe', sfn, ['KT', 'QT', 'QTz0', 'QTz1', 'CB'], [psres])
                        T.op('act', lambda e, pst=pst, ptb=ptb, ncols=ncols: e.activation(
                            out=ptb[:, 0:ncols], in_=pst, func=AF.Exp, scale=0.125), [psres], [ptres])

                def emit_XY(s, b):
                    gb = s * nb + b
                    (bx, bxres), (by, byres) = XYSETS[(gb // 4) % 2]
                    c0 = 128 * (gb % 4)
                    ip = tidx[(s, b - 1)]
                    ic = tidx[(s, b)]
                    prev_cols = (0, 128) if b == 0 else (128, 256)
                    vprev = (s if b == 0 else 16 + gb - 1)
                    vcur = 16 + gb
                    reads = ['VK0', 'VK1', 'CB'] + ['PT%d_%d' % (hh, ip % NPS) for hh in range(2)] + \
                            ['PT%d_%d' % (hh, ic % NPS) for hh in range(2)]

                    def fn(e):
                        ins = None
                        for hh in range(2):
                            pp = PT[hh][ip % NPS][:, prev_cols[0]:prev_cols[1]]
                            pc = PT[hh][ic % NPS][:, 0:128]
                            for (dstp, lp, lc) in ((bx, VK[:, 128 * vprev + 64 * hh:128 * vprev + 64 * hh + 64],
                                                    VK[:, 128 * vcur + 64 * hh:128 * vcur + 64 * hh + 64]),
                                                   (by, ones64, ones64)):
                                o = dstp[64 * hh:64 * hh + 64, c0:c0 + 128]
                                e.matmul(o, lhsT=lp, rhs=pp, start=True, stop=False)
                                ins = e.matmul(o, lhsT=lc, rhs=pc, start=False, stop=True)
                        return ins
                    T.op('pe', fn, reads, [bxres, byres])
                    if gb % 4 == 3:
                        q4 = gb // 4
                        for (bank, res, acc, accres) in ((bx, bxres, accX, 'accX'), (by, byres, accY, 'accY')):
                            dst = gview(acc, d, q4)
                            src = like(bank[:, 0:512], dst)
                            if g == 0:
                                T.op('dve', lambda e, dst=dst, src=src: e.tensor_copy(out=dst, in_=src), [res], [accres])
                            else:
                                T.op('dve', lambda e, dst=dst, src=src: e.tensor_tensor(out=dst, in0=dst, in1=src, op=ALU.add),
                                     [res, accres], [accres])

                LOOK = 2
                for s in range(d):
                    for b in range(nb):
                        need = min(tidx[(s, b)] + LOOK, len(tiles) - 1)
                        while emitted[0] <= need:
                            emit_S(emitted[0])
                            emitted[0] += 1
                        emit_XY(s, b)

            T.op('act', lambda e: e.activation(out=fin, in_=accY, func=AF.Ln), ['accY'], ['KT'])
            T.op('act', lambda e: e.activation(out=fin, in_=fin, func=AF.Exp, scale=-1.0), ['KT'], ['KT'])
            T.op('dve', lambda e: e.tensor_tensor(out=fin, in0=fin, in1=accX, op=ALU.mult), ['KT', 'accX'], ['KT'])
            T.op('dve', lambda e, pair=pair: e.scalar_tensor_tensor(out=AGv[:, pair, :], in0=fin, scalar=0.5, in1=SG,
                                                                    op0=ALU.mult, op1=ALU.mult), ['KT', 'SG'] + ['stgA%d' % i_ for i_ in range(4)], ['AG'])

        T.barrier()

        cv = Carver()
        HG = cv.bf16(8 * NT)
        HGv = HG.rearrange("p (k t) -> p k t", k=8)
        UO = 8
        Us = [cv.bf16(2048 + 8) for _ in range(2)]
        As = [cv.f32(2048) for _ in range(2)]
        Ms = [cv.f32(2048) for _ in range(2)]
        ICs = [cv.f32(2048) for _ in range(2)]
        Ct = [cv.bf16(512) for _ in range(3)]
        thr = cv.f32(512)
        this_ = [cv.f32(512) for _ in range(2)]
        thi = this_[0]
        zb = [cv.bf16(512) for _ in range(2)]
        thb = [cv.bf16(512) for _ in range(2)]
        ticount = [0]
        Ht = [cv.f32(512) for _ in range(2)]
        BDR = cv.bf16(1024)
        BDI = cv.bf16(1024)
        DG = cv.bf16(4 * 128)
        hinit = cv.f32(2)
        stage_cast(bd_d[0], BDR, 'BDR')
        stage_cast(bd_d[1], BDI, 'BDI')
        WLA[0] = 1
        PB4 = [(pA[0], 'pA0'), (pA[1], 'pA1'), (pX, 'pX'), (pY, 'pY')]
        p4count = [0]

        def next_p4():
            i = p4count[0] % 4
            p4count[0] += 1
            return PB4[i]

        hcount = [0]
        ccount = [0]
        wB_ = {}

        def s12_steps(ch, hf):
            st_ = hf
            Uu, ures = Us[st_], 'U%d' % st_
            Au, ares = As[st_], 'A%d' % st_
            Mu, mres = Ms[st_], 'M%d' % st_
            ICu, icres = ICs[st_], 'IC%d' % st_
            c = 40 + ch
            if hf == 0:
                wB_['w'] = load_w(c)
                for j in range(4):
                    T.op('dve', lambda e, j=j: e.tensor_scalar(out=DG[:, 128 * j:128 * j + 128], in0=ident,
                                                               scalar1=col(C_CW + ch * 4 + j), scalar2=None, op0=ALU.mult),
                         CST, ['DG'])
                T.op('dve', lambda e: e.memset(Uu[:, 0:UO], 0.0), [], [ures])
            else:
                T.op('dve', lambda e: e.tensor_copy(out=Uu[:, UO - 3:UO], in_=Us[0][:, UO + 2045:UO + 2048]), ['U0'], [ures])
            w, wres = wB_['w']
            off = 2048 * hf
            convs = {}
            gts = {}

            def proj(k):
                if not (0 <= k < 4):
                    return
                (pb1, pres1) = next_p4()
                inproj(w, wres, pb1, pres1, lambda kc: XTv[:, kc, off + 512 * k:off + 512 * k + 512], 512,
                       xcols=(off + 512 * k, off + 512 * k + 512))
                if hf == 0:
                    T.op('dve', lambda e: e.tensor_scalar(out=Uu[:, UO + 512 * k:UO + 512 * k + 512], in0=pb1[:, 0:512],
                                                          scalar1=col(C_FLAG), scalar2=col2(D_FBU + ch), op0=ALU.mult,
                                                          op1=ALU.add), [pres1] + CST, [ures])
                else:
                    T.op('dve', lambda e: e.tensor_scalar(out=Uu[:, UO + 512 * k:UO + 512 * k + 512], in0=pb1[:, 0:512],
                                                          scalar1=col(C_BIN + c), scalar2=None, op0=ALU.add),
                         [pres1] + CST, [ures])

            def conv(j):
                if not (0 <= j < 4):
                    return
                (pb, pres) = next_p4()

                def convfn(e):
                    ins = None
                    for tp in range(4):
                        ins = e.matmul(pb[:, 0:512], lhsT=DG[:, 128 * tp:128 * tp + 128],
                                       rhs=Uu[:, UO + 512 * j - 3 + tp:UO + 512 * j - 3 + tp + 512], start=(tp == 0), stop=(tp == 3))
                    return ins
                T.op('pe', convfn, ['DG', ures], [pres])
                ci = ccount[0] % 3
                ccount[0] += 1
                cb_, cres = Ct[ci], 'Ct%d' % ci
                convs[j] = (cb_, cres)
                T.op('dve', lambda e: e.tensor_scalar(out=cb_[:], in0=pb[:, 0:512], scalar1=col(C_CB + ch), scalar2=None,
                                                      op0=ALU.add), [pres] + CST, [cres])

            def gates(j2):
                if not (0 <= j2 < 4):
                    return
                gb_, gres = convs[j2]
                sl = slice(512 * j2, 512 * j2 + 512)
                ti = ticount[0] % 2
                ticount[0] += 1
                tib, tires = this_[ti], 'thi%d' % ti
                gts[j2] = (tib, tires, gb_, gres)
                T.op('pe', lambda e: e.matmul(pB[:, 0:512], lhsT=BDR[:, 128 * ch:128 * ch + 128], rhs=gb_[:], start=True, stop=True),
                     [gres, 'BDR'], ['pB'])
                T.op('pe', lambda e: e.matmul(pS[0][:, 0:512], lhsT=BDI[:, 128 * ch:128 * ch + 128], rhs=gb_[:], start=True,
                                              stop=True), [gres, 'BDI'], ['bS0'])
                T.op('act', lambda e: e.activation(out=thr[:], in_=pB[:, 0:512], func=AF.Tanh, bias=col2(D_HBR + ch), scale=0.5),
                     ['pB'] + CST, ['thr'])
                T.op('act', lambda e: e.activation(out=tib[:], in_=pS[0][:, 0:512], func=AF.Tanh, bias=col2(D_HBI + ch), scale=0.5),
                     ['bS0'] + CST, [tires])
                T.op('act', lambda e: e.activation(out=Au[:, sl], in_=thr[:], func=AF.Exp, bias=col2(D_HSC + ch),
                                                   scale=col2(D_HSC + ch)), ['thr'] + CST, [ares])
                T.op('act', lambda e: e.activation(out=Mu[:, sl], in_=thr[:], func=AF.Exp, bias=col2(D_SC1 + ch),
                                                   scale=col2(D_SC1 + ch)), ['thr'] + CST, [mres])

            def ic(j2):
                if not (0 <= j2 < 4):
                    return
                tib, tires, gb_, gres = gts[j2]
                sl = slice(512 * j2, 512 * j2 + 512)
                T.op('dve', lambda e: e.scalar_tensor_tensor(out=ICu[:, sl], in0=tib[:], scalar=1.0, in1=gb_[:], op0=ALU.add,
                                                             op1=ALU.mult), [tires, gres], [icres])
            return proj, conv, gates, ic

        def s45_steps(ch, hf):
            st_ = hf
            Au, ares = As[st_], 'A%d' % st_
            Mu, mres = Ms[st_], 'M%d' % st_
            ICu, icres = ICs[st_], 'IC%d' % st_
            T.op('act', lambda e: e.activation(out=Mu[:], in_=Mu[:], func=AF.Sqrt, bias=col(C_ONE), scale=-1.0),
                 [mres] + CST, [mres])
            if hf == 1:
                c2 = 48 + ch
                w2, wres2 = load_w(c2)
            hts = {}

            def main(j):
                if not (0 <= j < 4):
                    return
                sl = slice(512 * j, 512 * j + 512)
                T.op('dve', lambda e: e.scalar_tensor_tensor(out=Mu[:, sl], in0=Mu[:, sl], scalar=0.5, in1=ICu[:, sl],
                                                             op0=ALU.mult, op1=ALU.mult), [mres, icres], [mres])
                hi = hcount[0] % 2
                hcount[0] += 1
                hb_, hres = Ht[hi], 'Ht%d' % hi
                pb_, pres_ = Ht[1 - hi], 'Ht%d' % (1 - hi)
                hts[j] = (hb_, hres)
                if hf == 0 and j == 0:
                    T.op('dve', lambda e: e.tensor_tensor_scan(out=hb_[:], data0=Au[:, sl], data1=Mu[:, sl], initial=0.0,
                                                               op0=ALU.mult, op1=ALU.add), [ares, mres], [hres])
                elif hf == 1 and j == 0:
                    T.op('dve', lambda e: e.tensor_scalar(out=hinit[:, 0:1], in0=pb_[:, 511:512], scalar1=col(C_FLAG), scalar2=None,
                                                          op0=ALU.mult), [pres_] + CST, ['hinit'])
                    T.op('dve', lambda e: e.tensor_tensor_scan(out=hb_[:], data0=Au[:, sl], data1=Mu[:, sl], initial=hinit[:, 0:1],
                                                               op0=ALU.mult, op1=ALU.add), [ares, mres, 'hinit'], [hres])
                else:
                    T.op('dve', lambda e: e.tensor_tensor_scan(out=hb_[:], data0=Au[:, sl], data1=Mu[:, sl], initial=pb_[:, 511:512],
                                                               op0=ALU.mult, op1=ALU.add), [ares, mres, pres_], [hres])
                if hf == 1:
                    (pb, pres) = next_p4()
                    inproj(w2, wres2, pb, pres, lambda kc: XTv[:, kc, 2048 + 512 * j:2048 + 512 * j + 512], 512,
                           xcols=(2048 + 512 * j, 2560 + 512 * j))
                    zi = j % 2
                    T.op('act', lambda e: e.activation(out=zb[zi][:], in_=pb[:, 0:512], func=AF.Identity, bias=col(C_BIN + c2),
                                                       scale=1.0), [pres] + CST, ['zb%d' % zi])
                    T.op('act', lambda e: e.activation(out=thb[zi][:], in_=pb[:, 0:512], func=AF.Tanh, bias=col2(D_HBIN + c2),
                                                       scale=0.5), [pres] + CST, ['thb%d' % zi])

            def tail(j):
                if not (0 <= j < 4) or hf == 0:
                    return
                sl = slice(512 * j, 512 * j + 512)
                zi = j % 2
                hb_, hres = hts[j]
                T.op('dve', lambda e: e.scalar_tensor_tensor(out=zb[zi][:], in0=thb[zi][:], scalar=1.0, in1=zb[zi][:], op0=ALU.add,
                                                             op1=ALU.mult), ['zb%d' % zi, 'thb%d' % zi], ['zb%d' % zi])
                T.op('dve', lambda e: e.scalar_tensor_tensor(out=HGv[:, ch, sl], in0=zb[zi][:], scalar=0.5, in1=hb_[:],
                                                             op0=ALU.mult, op1=ALU.mult), ['zb%d' % zi, hres], ['HG'])
            return main, tail

        units = [(ch, hf) for ch in range(8) for hf in range(2)]
        NFILL = 2
        NU = len(units)
        E12 = {}
        E45 = {}
        nop = lambda k: None
        for g_ in range(4 * NU + 12):
            def filler(e):
                ins = None
                for _q in range(NFILL):
                    ins = e.matmul(pS[1][:, 0:512], lhsT=ident, rhs=XTv[:, 0, 0:512], start=True, stop=True)
                return ins
            T.op('pe', filler, [], ['bS1'])
            for v in range(NU):
                if v in E45:
                    E45[v][1](g_ - (4 * v + 8))
            gt = g_ - 3
            if 0 <= gt < 4 * NU:
                E12[gt // 4][2](gt % 4)
            for v in range(NU):
                if g_ == 4 * v + 6:
                    E45[v] = s45_steps(*units[v])
                if v in E45:
                    E45[v][0](g_ - (4 * v + 7))
            if 0 <= gt < 4 * NU:
                E12[gt // 4][3](gt % 4)
            ct = g_ - 1
            if 0 <= ct < 4 * NU:
                E12[ct // 4][1](ct % 4)
            if g_ < 4 * NU:
                if g_ % 4 == 0:
                    E12[g_ // 4] = s12_steps(*units[g_ // 4])
                E12[g_ // 4][0](g_ % 4)

        T.barrier()

        cv = Carver()
        HG = cv.bf16(8 * NT)
        HGv = HG.rearrange("p (k t) -> p k t", k=8)
        MG = cv.bf16(8 * NT)
        MGv = MG.rearrange("p (k t) -> p k t", k=8)
        c1_mark = cv.off
        WAP = cv.bf16(4 * 1024)
        WLP = cv.bf16(8 * 1024)
        thA = cv.f32(512)
        thB = cv.f32(512)
        m1 = cv.f32(512)
        m2 = cv.f32(512)
        stgC = [(cv.f32(1024), 'stgC%d' % i) for i in range(3)] + [(wst[i][:], 'wst%d' % i) for i in range(2)]
        engsC = ['act', 'dve', 'pool']
        for kc in range(4):
            stage_cast(wap_d[kc], WAP[:, 1024 * kc:1024 * kc + 1024], 'WAP%d' % kc, eng=engsC[kc % 2], stg=stgC)
        for kc in range(8):
            stage_cast(wlp_d[kc], WLP[:, 1024 * kc:1024 * kc + 1024], 'WLP%d' % kc, eng=engsC[kc % 3], stg=stgC)

        WLA[0] = 1
        for dc in range(8):
            cA, cB_ = 56 + dc, 64 + dc
            wA, wAres = load_w(cA)
            wB, wBres = load_w(cB_)
            for j in range(4):
                sl = slice(512 * j, 512 * j + 512)
                pstA, psresA = next_pA()
                inproj(wA, wAres, pstA, psresA, lambda kc, j=j: xmain(kc)[:, 512 * j:512 * j + 512], 512)
                T.op('act', lambda e, pstA=pstA, cA=cA: e.activation(out=thA[:], in_=pstA[:, 0:512], func=AF.Tanh,
                                                                     bias=col2(D_HBIN + cA), scale=0.5), [psresA] + CST, ['thA'])
                pstB, psresB = next_pA()
                inproj(wB, wBres, pstB, psresB, lambda kc, j=j: xmain(kc)[:, 512 * j:512 * j + 512], 512)
                T.op('act', lambda e, pstB=pstB, cB_=cB_: e.activation(out=thB[:], in_=pstB[:, 0:512], func=AF.Tanh,
                                                                       bias=col2(D_HBIN + cB_), scale=0.5), [psresB] + CST, ['thB'])

                def yafn(e, dc=dc, sl=sl):
                    ins = None
                    for kc in range(4):
                        ins = e.matmul(pX[:, 0:512], lhsT=WAP[:, 1024 * kc + 128 * dc:1024 * kc + 128 * dc + 128],
                                       rhs=AGv[:, kc, sl], start=(kc == 0), stop=(kc == 3))
                    return ins
                T.op('pe', yafn, ['WAP%d' % i_ for i_ in range(4)] + ['AG'], ['pX'])

                def ybfn(e, dc=dc, sl=sl):
                    ins = None
                    for kc in range(8):
                        ins = e.matmul(pY[:, 0:512], lhsT=WLP[:, 1024 * kc + 128 * dc:1024 * kc + 128 * dc + 128],
                                       rhs=HGv[:, kc, sl], start=(kc == 0), stop=(kc == 7))
                    return ins
                T.op('pe', ybfn, ['WLP%d' % i_ for i_ in range(8)] + ['HG'], ['pY'])
                T.op('dve', lambda e: e.scalar_tensor_tensor(out=m1[:], in0=thA[:], scalar=1.0, in1=pX[:, 0:512], op0=ALU.add,
                                                             op1=ALU.mult), ['thA', 'pX'], ['m1'])
                T.op('dve', lambda e: e.scalar_tensor_tensor(out=m2[:], in0=thB[:], scalar=1.0, in1=pY[:, 0:512], op0=ALU.add,
                                                             op1=ALU.mult), ['thB', 'pY'], ['m2'])
                T.op('dve', lambda e, dc=dc, sl=sl: e.tensor_tensor(out=MGv[:, dc, sl], in0=m1[:], in1=m2[:], op=ALU.add),
                     ['m1', 'm2'], ['MG'])

        T.barrier()

        cv.off = c1_mark
        WO = cv.bf16(8 * 1024)
        stats = [cv.f32(16) for _ in range(2)]
        mv = [cv.f32(8) for _ in range(2)]
        for kc in range(8):
            stage_cast(wo_d[kc], WO[:, 1024 * kc:1024 * kc + 1024], 'WO')
        xh = lambda kc: XTv[:, kc, 0:2048].bitcast(F32)
        yo = [xh(3), xh(4)]
        bcb = [xh(5), xh(6), xh(7)]
        hgf = HG.bitcast(F32)
        ybufs = [hgf[:, 0:1024], hgf[:, 1024:2048]]
        for i in range(3):
            T.dma('sp', (lambda i: lambda e: e.dma_start(out=bcb[i], in_=bc_d[i]))(i), [], ['bc%d' % i], 'bc%d' % i)

        xr = [xh(0), xh(1), xh(2)]

        def dma_x(tt):
            xi = tt % 3
            T.dma('sp', lambda e: e.dma_start(out=xr[xi], in_=xres_d[tt]), [], ['xr%d' % xi], 'xr%d' % xi)

        def prep_x(tt):
            xi = tt % 3
            T.op('act', lambda e: e.activation(out=xr[xi], in_=xr[xi], func=AF.Identity, scale=ALPHA), ['xr%d' % xi], ['xr%d' % xi])
            T.op('pool', lambda e: e.tensor_tensor(out=xr[xi], in0=xr[xi], in1=bcb[0], op=ALU.add), ['xr%d' % xi, 'bc0'],
                 ['xr%d' % xi])

        def stage_b1(tt):
            i2 = tt % 2
            mv_, mvres = mv[i2], 'mv%d' % i2
            ybuf, yres = ybufs[i2], 'ybuf%d' % i2
            T.op('dve', lambda e: e.scalar_tensor_tensor(out=mv_[:, 4:5], in0=mv_[:, 0:1], scalar=-1.0, in1=mv_[:, 3:4],
                                                         op0=ALU.mult, op1=ALU.mult), [mvres, mvres + 'c'], [mvres + 'd'])
            T.op('act', lambda e: e.activation(out=yo[i2], in_=ybuf, func=AF.Identity, bias=mv_[:, 4:5], scale=mv_[:, 3:4]),
                 [yres + '_0', yres + '_1', mvres + 'c', mvres + 'd'], ['yo%d' % i2])

        def stage_b2(tt):
            i2 = tt % 2
            T.op('dve', lambda e: e.tensor_tensor(out=yo[i2], in0=yo[i2], in1=bcb[1], op=ALU.mult), ['yo%d' % i2, 'bc1'],
                 ['yo%d' % i2])
            T.op('pool', lambda e: e.tensor_tensor(out=yo[i2], in0=yo[i2], in1=bcb[2], op=ALU.add), ['yo%d' % i2, 'bc2'],
                 ['yo%d' % i2])
            T.dma('sp', lambda e: e.dma_start(out=y_d[tt], in_=yo[i2]), ['yo%d' % i2], [], 'y%d' % i2)

        def stage_a(tt):
            i2, xi = tt % 2, tt % 3
            ybuf, yres = ybufs[i2], 'ybuf%d' % i2
            st_, stres = stats[i2], 'stats%d' % i2
            mv_, mvres = mv[i2], 'mv%d' % i2
            for hf in range(2):
                pst, psres = next_pA()

                def ofn(e, pst=pst, hf=hf):
                    ins = None
                    for kc in range(8):
                        ins = e.matmul(pst[:, 0:512], lhsT=MGv[:, kc, 128 * tt:128 * tt + 128],
                                       rhs=WO[:, 1024 * kc + 512 * hf:1024 * kc + 512 * hf + 512], start=(kc == 0), stop=(kc == 7))
                    return ins
                T.op('pe', ofn, ['MG', 'WO'], [psres])
                T.op('dve', lambda e, pst=pst, hf=hf: e.scalar_tensor_tensor(
                    out=ybuf[:, 512 * hf:512 * hf + 512], in0=pst[:, 0:512], scalar=0.5, in1=xr[xi][:, 512 * hf:512 * hf + 512],
                    op0=ALU.mult, op1=ALU.add), [psres, 'xr%d' % xi], [yres + '_%d' % hf])
                T.op('dve', lambda e, hf=hf: e.bn_stats(out=st_[:, 6 * hf:6 * hf + 6], in_=ybuf[:, 512 * hf:512 * hf + 512]),
                     [yres + '_%d' % hf], [stres + '_%d' % hf])
            T.op('dve', lambda e: e.bn_aggr(out=mv_[:, 0:2], in_=st_[:, 0:12]), [stres + '_0', stres + '_1'], [mvres])
            T.op('act', lambda e: e.activation(out=mv_[:, 2:3], in_=mv_[:, 1:2], func=AF.Ln, bias=col(C_EPS), scale=1.0),
                 [mvres] + CST, [mvres + 'b'])
            T.op('act', lambda e: e.activation(out=mv_[:, 3:4], in_=mv_[:, 2:3], func=AF.Exp, scale=-0.5),
                 [mvres + 'b'], [mvres + 'c'])

        dma_x(0)
        dma_x(1)
        dma_x(2)
        prep_x(0)
        prep_x(1)
        for tt in range(17):
            if tt >= 1:
                stage_b1(tt - 1)
            if tt < 16:
                stage_a(tt)
            if tt + 3 < 16:
                dma_x(tt + 3)
            if tt >= 1:
                stage_b2(tt - 1)
            if tt + 2 < 16:
                prep_x(tt + 2)

        T.wait_all('sp')
        T.emit()
    return nc


def _rope_tables(pos0):
    half = 32
    inv_freq = (10000.0 ** (-np.arange(half, dtype=np.float32) / half)).astype(np.float32)
    pos = np.arange(pos0, pos0 + 2048, dtype=np.float32)
    ang = pos[None, :] * inv_freq[:, None]
    cos = np.cos(ang).astype(np.float32)
    sin = np.sin(ang).astype(np.float32)
    p = np.arange(128)
    i = p % 32
    sign = np.where((p % 64) < 32, -1.0, 1.0).astype(np.float32)
    return cos[i], sin[i] * sign[:, None]


def _consts_bf(flag):
    cb = np.zeros((128, 1024), np.float32)
    cb[:, B_ID:B_ID + 128] = np.eye(128, dtype=np.float32)
    pm = np.zeros((128, 128), np.float32)
    for m in range(128):
        partner = m + 32 if (m % 64) < 32 else m - 32
        pm[partner, m] = 1.0
    cb[:, B_PM:B_PM + 128] = pm
    cb[:, B_ONES:B_ONES + 64] = 1.0
    k = np.arange(128)[:, None]
    q = np.arange(128)[None, :]
    NEG = -1000.0
    cur = np.where(k <= q, 0.0, NEG).astype(np.float32)
    prev = np.where(k >= q, 0.0, NEG).astype(np.float32)
    cb[:, B_M:B_M + 128] = cur
    cb[:, B_M + 128:B_M + 256] = prev
    cb[:, B_MH:B_MH + 128] = prev if flag > 0.5 else NEG
    return cb


_NC_CACHE = {}


def kernel(x, w_in, b_in, conv_w, conv_b, lru_wr, lru_br, lru_wi, lru_bi, lru_lambda,
           w_attn_proj, w_lru_proj, w_out, b_out, ln_gain, ln_bias):
    f = lambda a: np.ascontiguousarray(np.asarray(a, dtype=np.float32))
    x = f(x)
    W = f(w_in)[0]
    win_l = f(W.reshape(8, 128, 72, 128).transpose(2, 1, 0, 3).reshape(72, 128, 1024))
    colmaj = lambda v, n: f(np.asarray(v, np.float32).reshape(n, 128).T)
    bd = np.zeros((2, 8, 128, 128), np.float32)
    for t, wsrc in enumerate((f(lru_wr)[0], f(lru_wi)[0])):
        for ch in range(8):
            bd[t, ch, 0:64, 0:64] = wsrc[2 * ch]
            bd[t, ch, 64:128, 64:128] = wsrc[2 * ch + 1]
    bd_l = f(bd.transpose(0, 2, 1, 3).reshape(2, 128, 1024))
    wap_l = f(f(w_attn_proj)[0].reshape(4, 128, 1024))
    wlp_l = f(f(w_lru_proj)[0].reshape(8, 128, 1024))
    wo_l = f(f(w_out)[0].reshape(8, 128, 1024))
    bc = f(np.stack([np.broadcast_to(f(b_out)[0], (128, 1024)), np.broadcast_to(f(ln_gain)[0], (128, 1024)),
                     np.broadcast_to(f(ln_bias)[0], (128, 1024))]))
    cw = f(conv_w)[0]
    in_maps = []
    for c in range(8):
        b, half = c // 2, c % 2
        flag = float(half)
        xm = x[b, half * 2048:(half + 1) * 2048]
        xh = x[b, 0:2048] if half == 1 else np.zeros((2048, 1024), np.float32)
        xT = np.concatenate([xh.T, xm.T], axis=1)
        cst = np.zeros((128, NCST), np.float32)
        cst[:, C_BIN:C_BIN + 72] = colmaj(f(b_in)[0], 72)
        for ch in range(8):
            for j in range(4):
                cst[:, C_CW + ch * 4 + j] = cw[j, ch * 128:(ch + 1) * 128]
        cst[:, C_CB:C_CB + 8] = colmaj(f(conv_b)[0], 8)
        cst[:, C_BR:C_BR + 8] = colmaj(f(lru_br)[0], 8)
        cst[:, C_BI:C_BI + 8] = colmaj(f(lru_bi)[0], 8)
        cst[:, C_LAM:C_LAM + 8] = colmaj(f(lru_lambda)[0], 8)
        cst[:, C_FLAG] = flag
        cst[:, C_ONE] = 1.0
        cst[:, C_MHALF] = -0.5
        cst[:, C_EPS] = LN_EPS
        cm, sm = _rope_tables(half * 2048)
        chh, shh = _rope_tables(0)
        in_maps.append({
            "xT": f(xT.reshape(8, 128, 4096)),
            "xres": f(xm.reshape(16, 128, 1024)),
            "win": win_l,
            "cst": cst,
            "cb": _consts_bf(flag),
            "tab": f(np.stack([cm, sm, chh, shh])),
            "bd": bd_l,
            "wap": wap_l,
            "wlp": wlp_l,
            "wo": wo_l,
            "bc": bc,
        })
    if "nc" not in _NC_CACHE:
        _NC_CACHE["nc"] = build()
    res = run_bass_kernel_spmd(_NC_CACHE["nc"], in_maps, core_ids=list(range(8)))
    out = np.zeros((4, 4096, 1024), np.float32)
    for c in range(8):
        b, half = c // 2, c % 2
        out[b, half * 2048:(half + 1) * 2048] = np.asarray(res.results[c]["y"]).reshape(2048, 1024)
    return out
```
